# Optimizing a Trainium2 kernel written in Bass

```python
import jax, jax.numpy as jnp
from jax import lax
import numpy as np

D_MODEL = 1024
BATCH = 2
SEQ = 8192
DEPTH = 1
DEC_BATCH = 8
DEC_SEQ = 16
PAST_LEN = 4096

CHUNK = 64
HEAD_DIM = 64
A_HEADS = 6
A_LEFT_CHUNKS = 8
A_REACH = A_LEFT_CHUNKS * CHUNK
REL_CLIP = 128
B_HEADS = 6
B_KV_HEADS = 2
B_GROUP = B_HEADS // B_KV_HEADS
B_WINDOW = 128
B_LEFT_CHUNKS = B_WINDOW // CHUNK
B_REACH = B_LEFT_CHUNKS * CHUNK
M_HEADS = 4
N_MEM = 256
A_W = A_HEADS * HEAD_DIM
B_W = B_HEADS * HEAD_DIM
B_KVW = B_KV_HEADS * HEAD_DIM
M_W = M_HEADS * HEAD_DIM
D_MIX = A_W + B_W + M_W
SPLIT_SIZES = (A_W, A_W, A_W, A_W, B_W, B_KVW, B_KVW, B_W, M_W, M_W)
D_IN = sum(SPLIT_SIZES)
EPS = 1e-6
NEG = -1e30
SCALE = HEAD_DIM ** -0.5

kernel_name = "hybrid_chunk_streaming_encoder_step"


def _rmsnorm(x, g):
    xf = x.astype(jnp.float32)
    xf = xf * lax.rsqrt(jnp.mean(xf * xf, axis=-1, keepdims=True) + EPS)
    return (xf * g.astype(jnp.float32)).astype(x.dtype)


def _heads(t, h):
    return t.reshape(*t.shape[:-1], h, HEAD_DIM)


def _alibi_slopes():
    return 2.0 ** (-8.0 * jnp.arange(1, B_HEADS + 1, dtype=jnp.float32) / B_HEADS)


def _relpos_bias(table, d):
    return table[:, jnp.clip(d, -REL_CLIP, REL_CLIP) + REL_CLIP].astype(jnp.float32)[:, None]


def _alibi_bias(d):
    b = -_alibi_slopes()[:, None, None] * jnp.abs(d).astype(jnp.float32)[None]
    return b.reshape(B_KV_HEADS, B_GROUP, *d.shape)


def _attn_core(q, k, v, bias, valid, sink):
    s = jnp.einsum('...qhgd,...khd->...hgqk', q, k).astype(jnp.float32) * SCALE
    if bias is not None:
        s = s + bias
    if valid is not None:
        s = jnp.where(valid[..., None, None, None, :], s, NEG)
    if sink is None:
        p = jax.nn.softmax(s, axis=-1)
    else:
        sk = sink.astype(jnp.float32).reshape(q.shape[-3], q.shape[-2], 1, 1)
        m = jnp.maximum(jnp.max(s, axis=-1, keepdims=True), sk)
        e = jnp.exp(s - m)
        p = e / (jnp.sum(e, axis=-1, keepdims=True) + jnp.exp(sk - m))
    return jnp.einsum('...hgqk,...khd->...qhgd', p.astype(v.dtype), v)


def _band_attention(q, k, v, left_chunks, bias, sink):
    b, s = q.shape[:2]
    nc = s // CHUNK
    qc = q.reshape(b, nc, CHUNK, *q.shape[2:])
    kc = k.reshape(b, nc, CHUNK, *k.shape[2:])
    vc = v.reshape(b, nc, CHUNK, *v.shape[2:])
    band = jnp.arange(nc)[:, None] + jnp.arange(-left_chunks, 1)[None, :]
    valid = jnp.repeat(band >= 0, CHUNK, axis=1)
    band = jnp.maximum(band, 0)
    n_keys = (left_chunks + 1) * CHUNK
    kb = kc[:, band].reshape(b, nc, n_keys, *k.shape[2:])
    vb = vc[:, band].reshape(b, nc, n_keys, *v.shape[2:])
    o = _attn_core(qc, kb, vb, bias, valid, sink)
    return o.reshape(b, s, -1)


def _band_distance(left_chunks):
    n_keys = (left_chunks + 1) * CHUNK
    return left_chunks * CHUNK + jnp.arange(CHUNK)[:, None] - jnp.arange(n_keys)[None, :]


def _project(x, g_pre, w_in):
    z = _rmsnorm(x, g_pre) @ w_in
    return jnp.split(z, np.cumsum(SPLIT_SIZES)[:-1].tolist(), axis=-1)


def _mem_kv(mem, g_mem, w_mem_kv):
    mk, mv = jnp.split(_rmsnorm(mem, g_mem) @ w_mem_kv, 2, axis=-1)
    return _heads(mk, M_HEADS), _heads(mv, M_HEADS)


def _merge(x, oa, ga, ob, gb, om, gm, w_out, g_post):
    o = jnp.concatenate([oa * jax.nn.silu(ga), ob * jax.nn.silu(gb), om * jax.nn.silu(gm)], axis=-1) @ w_out
    return x + _rmsnorm(o, g_post)


def _gqa_q(t):
    return _heads(t, B_HEADS).reshape(*t.shape[:-1], B_KV_HEADS, B_GROUP, HEAD_DIM)


def setup_inputs(seed: int = 0) -> dict:
    key = jax.random.key(seed)
    ks = jax.random.split(key, 17)
    f32 = jnp.float32
    la = min(A_REACH, PAST_LEN)
    lb = min(B_REACH, PAST_LEN)
    return {
        "x_prompt": jax.random.normal(ks[0], (BATCH, SEQ, D_MODEL), f32),
        "x_sample": jax.random.normal(ks[1], (DEC_BATCH, DEC_SEQ, D_MODEL), f32),
        "cache_a_k": jax.random.normal(ks[2], (DEPTH, DEC_BATCH, la, A_HEADS, HEAD_DIM), f32),
        "cache_a_v": jax.random.normal(ks[3], (DEPTH, DEC_BATCH, la, A_HEADS, HEAD_DIM), f32),
        "cache_b_k": jax.random.normal(ks[4], (DEPTH, DEC_BATCH, lb, B_KV_HEADS, HEAD_DIM), f32),
        "cache_b_v": jax.random.normal(ks[5], (DEPTH, DEC_BATCH, lb, B_KV_HEADS, HEAD_DIM), f32),
        "cache_mem_k": jax.random.normal(ks[6], (DEPTH, DEC_BATCH, N_MEM, M_HEADS, HEAD_DIM), f32),
        "cache_mem_v": jax.random.normal(ks[7], (DEPTH, DEC_BATCH, N_MEM, M_HEADS, HEAD_DIM), f32),
        "mem_prompt": jax.random.normal(ks[8], (BATCH, N_MEM, D_MODEL), f32),
        "g_pre": 1.0 + 0.02 * jax.random.normal(ks[9], (DEPTH, D_MODEL), f32),
        "w_in": jax.random.normal(ks[10], (DEPTH, D_MODEL, D_IN), f32) * D_MODEL ** -0.5,
        "rel_bias_a": 0.1 * jax.random.normal(ks[11], (DEPTH, A_HEADS, 2 * REL_CLIP + 1), f32),
        "sink_b": 0.5 * jax.random.normal(ks[12], (DEPTH, B_HEADS), f32),
        "g_mem": 1.0 + 0.02 * jax.random.normal(ks[13], (DEPTH, D_MODEL), f32),
        "w_mem_kv": jax.random.normal(ks[14], (DEPTH, D_MODEL, 2 * M_W), f32) * D_MODEL ** -0.5,
        "w_out": jax.random.normal(ks[15], (DEPTH, D_MIX, D_MODEL), f32) * D_MIX ** -0.5,
        "g_post": 1.0 + 0.02 * jax.random.normal(ks[16], (DEPTH, D_MODEL), f32),
    }


def reference(x_prompt, x_sample, cache_a_k, cache_a_v, cache_b_k, cache_b_v, cache_mem_k, cache_mem_v,
              mem_prompt, g_pre, w_in, rel_bias_a, sink_b, g_mem, w_mem_kv, w_out, g_post):
    seq = x_prompt.shape[1]
    n_new = x_sample.shape[1]
    la_p = min(A_REACH, seq)
    lb_p = min(B_REACH, seq)
    la_s = cache_a_k.shape[2]
    lb_s = cache_b_k.shape[2]

    d_a_prompt = _band_distance(A_LEFT_CHUNKS)
    d_b_prompt = _band_distance(B_LEFT_CHUNKS)
    q_pos = PAST_LEN + jnp.arange(n_new)
    k_pos_a = jnp.concatenate([PAST_LEN - la_s + jnp.arange(la_s), q_pos])
    k_pos_b = jnp.concatenate([PAST_LEN - lb_s + jnp.arange(lb_s), q_pos])
    d_a_sample = q_pos[:, None] - k_pos_a[None, :]
    d_b_sample = q_pos[:, None] - k_pos_b[None, :]
    bias_b_prompt = _alibi_bias(d_b_prompt)
    bias_b_sample = _alibi_bias(d_b_sample)

    yp, ys = x_prompt, x_sample
    akp, avp, bkp, bvp, mkp, mvp, aks, avs, bks, bvs = ([] for _ in range(10))
    for l in range(DEPTH):
        qa, ka, va, ga, qb, kb, vb, gb, qm, gm = _project(yp, g_pre[l], w_in[l])
        ka, va = _heads(ka, A_HEADS), _heads(va, A_HEADS)
        kb, vb = _heads(kb, B_KV_HEADS), _heads(vb, B_KV_HEADS)
        oa = _band_attention(_heads(qa, A_HEADS)[..., None, :], ka, va, A_LEFT_CHUNKS,
                             _relpos_bias(rel_bias_a[l], d_a_prompt), None)
        ob = _band_attention(_gqa_q(qb), kb, vb, B_LEFT_CHUNKS, bias_b_prompt, sink_b[l])
        mk, mv = _mem_kv(mem_prompt, g_mem[l], w_mem_kv[l])
        om = _attn_core(_heads(qm, M_HEADS)[..., None, :], mk, mv, None, None, None)
        om = om.reshape(*om.shape[:2], M_W)
        yp = _merge(yp, oa, ga, ob, gb, om, gm, w_out[l], g_post[l])
        akp.append(ka[:, seq - la_p:]); avp.append(va[:, seq - la_p:])
        bkp.append(kb[:, seq - lb_p:]); bvp.append(vb[:, seq - lb_p:])
        mkp.append(mk); mvp.append(mv)

        qa, ka, va, ga, qb, kb, vb, gb, qm, gm = _project(ys, g_pre[l], w_in[l])
        ka, va = _heads(ka, A_HEADS), _heads(va, A_HEADS)
        kb, vb = _heads(kb, B_KV_HEADS), _heads(vb, B_KV_HEADS)
        ka_all = jnp.concatenate([cache_a_k[l].astype(ka.dtype), ka], axis=1)
        va_all = jnp.concatenate([cache_a_v[l].astype(va.dtype), va], axis=1)
        kb_all = jnp.concatenate([cache_b_k[l].astype(kb.dtype), kb], axis=1)
        vb_all = jnp.concatenate([cache_b_v[l].astype(vb.dtype), vb], axis=1)
        oa = _attn_core(_heads(qa, A_HEADS)[..., None, :], ka_all, va_all,
                        _relpos_bias(rel_bias_a[l], d_a_sample), None, None)
        oa = oa.reshape(*oa.shape[:2], A_W)
        ob = _attn_core(_gqa_q(qb), kb_all, vb_all, bias_b_sample, None, sink_b[l])
        ob = ob.reshape(*ob.shape[:2], B_W)
        om = _attn_core(_heads(qm, M_HEADS)[..., None, :], cache_mem_k[l], cache_mem_v[l], None, None, None)
        om = om.reshape(*om.shape[:2], M_W)
        ys = _merge(ys, oa, ga, ob, gb, om, gm, w_out[l], g_post[l])
        aks.append(ka); avs.append(va); bks.append(kb); bvs.append(vb)

    state_a_k_prompt = jnp.stack(akp)
    state_a_v_prompt = jnp.stack(avp)
    state_b_k_prompt = jnp.stack(bkp)
    state_b_v_prompt = jnp.stack(bvp)
    state_mem_k_prompt = jnp.stack(mkp)
    state_mem_v_prompt = jnp.stack(mvp)
    state_a_k_sample = jnp.stack(aks)
    state_a_v_sample = jnp.stack(avs)
    state_b_k_sample = jnp.stack(bks)
    state_b_v_sample = jnp.stack(bvs)
    return (yp, ys, state_a_k_prompt, state_a_v_prompt, state_b_k_prompt, state_b_v_prompt,
            state_mem_k_prompt, state_mem_v_prompt, state_a_k_sample, state_a_v_sample,
            state_b_k_sample, state_b_v_sample)
```

```python
import os
import numpy as np
import concourse.bass as bass
import concourse.mybir as mybir
from concourse.bass_utils import run_bass_kernel_spmd
from contextlib import ExitStack

F32 = mybir.dt.float32
BF16 = mybir.dt.bfloat16
ALU = mybir.AluOpType
AF = mybir.ActivationFunctionType

ENGS = ("pe", "act", "dve", "pool", "sp")
NCORES = 8
TOK = 2048
HALO = 512
SCALE = 0.125
EPS = 1e-6
SLOPES = [2.0 ** (-8.0 * (h + 1) / 6.0) for h in range(6)]
SCHED_WINDOW = 0.5
XLAT = 0.3
PECAL = False
SCHED_MODE = 'nodelay'
ND_MARGIN = 1.0
_CFG = '0'
PS_CFG = {
    '0': dict(S=[0, 1], ACC=[2, 3], PROJ=[0, 1, 2, 3], OUT=[2, 3]),
    'I': dict(S=[0, 1], ACC=[2], PROJ=[6, 7], OUT=[3]),
    'J': dict(S=[0], ACC=[1, 2], PROJ=[6, 7], OUT=[3]),
    'Z': dict(S=[0], ACC=[2, 3], PROJ=[2, 3], OUT=[1]),
    'W': dict(S=[0, 1], ACC=[2], PROJ=[6, 7], OUT=[3]),
    'V': dict(S=[0], ACC=[1, 2], PROJ=[6, 7], OUT=[3]),
}[_CFG]


class Ins:
    __slots__ = ("eng", "fn", "dma_key", "deps", "needs_inc", "milestone", "dma_target", "est", "cls", "pos",
                 "tag", "t0", "t1", "boost")

    def __init__(self, eng, fn, dma_key, est, cls):
        self.eng = eng
        self.fn = fn
        self.dma_key = dma_key
        self.deps = {}
        self.needs_inc = False
        self.milestone = None
        self.dma_target = None
        self.est = est
        self.cls = cls
        self.pos = None


class Prog:
    def __init__(self):
        self.order = []
        self.tok_w = {}
        self.tok_r = {}
        self.lists = None

    def add(self, eng, fn, reads=(), writes=(), dma_key=None, est=0.1, cls=None):
        ins = Ins(eng, fn, dma_key, est, cls)
        ins.tag = getattr(self, "tag", None)
        ins.boost = getattr(self, "boost", 0.0)
        for t in reads:
            w = self.tok_w.get(t)
            if w is not None:
                ins.deps[w] = "raw"
            if isinstance(t, tuple) and t[0] == "ps":
                for r in self.tok_r.get(t, ()):
                    if r.eng != eng and r not in ins.deps:
                        ins.deps[r] = "rar"
        for t in writes:
            w = self.tok_w.get(t)
            if w is not None and w not in ins.deps:
                ins.deps[w] = "waw"
            for r in self.tok_r.get(t, ()):
                if r is not ins and r not in ins.deps:
                    ins.deps[r] = "war"
        for t in writes:
            self.tok_w[t] = ins
            self.tok_r[t] = []
        for t in reads:
            if t in writes:
                continue
            self.tok_r.setdefault(t, []).append(ins)
        self.order.append(ins)
        return ins

    @staticmethod
    def _need_wait(cons, prod, kind):
        if prod.dma_key is not None:
            return True
        if prod.eng != cons.eng:
            return True
        if cons.dma_key is not None:
            return True
        if prod.eng == "pe":
            return False
        return kind in ("raw", "waw", "war")

    def schedule(self, reorder=True):
        order = self.order
        if not reorder:
            self.lists = {e: [i for i in order if i.eng == e] for e in ENGS}
            return
        succs = {i: [] for i in order}
        for i in order:
            for d in i.deps:
                succs[d].append(i)
        prio = {}
        for i in reversed(order):
            m = 0.0
            for sc in succs[i]:
                if prio[sc] > m:
                    m = prio[sc]
            prio[i] = i.est + m
        for i in order:
            prio[i] += i.boost
        noise = getattr(self, "noise", 0.0)
        if noise > 0:
            import random
            rnd = random.Random(getattr(self, "seed", 0))
            for i in order:
                prio[i] *= 1.0 + noise * (2 * rnd.random() - 1)
        window = getattr(self, "window", SCHED_WINDOW)
        ndep = {i: len(i.deps) for i in order}
        ready = {e: [] for e in ENGS}
        for i in order:
            if ndep[i] == 0:
                ready[i.eng].append(i)
        free = {e: 0.0 for e in ENGS}
        done = {}
        lists = {e: [] for e in ENGS}
        act_cls = None
        dma_free = 0.0
        n = len(order)
        k = 0
        while k < n:
            cands = []
            for e in ENGS:
                fe = free[e]
                for i in ready[e]:
                    st = fe
                    for d in i.deps:
                        t = done[d] + (XLAT if (d.eng != e or d.dma_key is not None) else 0.0)
                        if t > st:
                            st = t
                    if e == "act" and i.cls is not None and i.cls != act_cls:
                        st += 1.3
                    cands.append((st, -prio[i], len(cands), i))
            if SCHED_MODE == "nodelay":
                chosen = []
                per = {}
                for c in cands:
                    per.setdefault(c[3].eng, []).append(c)
                for e, cl in per.items():
                    cl.sort(key=lambda c: (c[0], c[1]))
                    x = cl[0]
                    changed = True
                    while changed:
                        changed = False
                        dur = 0.07 if x[3].dma_key is not None else x[3].est
                        for y in cl:
                            if y is x:
                                continue
                            if y[1] < x[1] * ND_MARGIN and y[0] < x[0] + dur - 1e-9 and y[0] >= x[0]:
                                x = y
                                changed = True
                                break
                    chosen.append(x)
                mn = min(c[0] for c in chosen)
                win = [c for c in chosen if c[0] <= mn + 1e-9]
                st, _, _, i = min(win, key=lambda c: (c[1], c[0]))
            else:
                mn = min(c[0] for c in cands)
                win = [c for c in cands if c[0] <= mn + window]
                _, _, _, i = min(win, key=lambda c: (c[1], c[0]))
                st = [c[0] for c in win if c[3] is i][0]
            e = i.eng
            ready[e].remove(i)
            lists[e].append(i)
            if i.dma_key is not None:
                free[e] = st + 0.07
                x0 = max(st + 1.0, dma_free)
                dma_free = x0 + i.est
                done[i] = dma_free + 1.0
            else:
                if e == "act" and i.cls is not None:
                    act_cls = i.cls
                done[i] = st + i.est
                free[e] = done[i]
            i.t0 = st
            i.t1 = done[i]
            for sc in succs[i]:
                ndep[sc] -= 1
                if ndep[sc] == 0:
                    ready[sc.eng].append(sc)
            k += 1
        self.lists = lists
        self.makespan = max(done.values())

    def emit(self, sems, dma_sems, block):
        lists = self.lists
        for e in ENGS:
            cnt = {}
            for p_, ins in enumerate(lists[e]):
                ins.pos = p_
                if ins.dma_key is not None:
                    c = cnt.get(ins.dma_key, 0) + 1
                    cnt[ins.dma_key] = c
                    ins.dma_target = 16 * c
        waits = {}
        for e in ENGS:
            for ins in lists[e]:
                latest = {}
                dmaw = {}
                for p, kind in ins.deps.items():
                    if not self._need_wait(ins, p, kind):
                        continue
                    if p.dma_key is not None:
                        if p.dma_target > dmaw.get(p.dma_key, 0):
                            dmaw[p.dma_key] = p.dma_target
                    else:
                        q = latest.get(p.eng)
                        if q is None or p.pos > q.pos:
                            latest[p.eng] = p
                for p in latest.values():
                    p.needs_inc = True
                waits[ins] = (latest, dmaw)
        for e in ENGS:
            c = 0
            for ins in lists[e]:
                if ins.dma_key is None and ins.needs_inc:
                    c += 1
                    ins.milestone = c
        if False:
            for e in ENGS:
                ms = [i.milestone for i in lists[e] if i.milestone]
                print("ENG", e, "n_ins", len(lists[e]), "max_milestone", max(ms) if ms else 0)
            print("est makespan us", getattr(self, "makespan", None))

        def run(engname, eng):
            waited = {}
            for ins in lists[engname]:
                latest, dmaw = waits[ins]
                for pe_, p in latest.items():
                    s_ = ("e", pe_)
                    if waited.get(s_, 0) >= p.milestone:
                        continue
                    waited[s_] = p.milestone
                    eng.wait_ge(sems[pe_], p.milestone)
                for key, v in dmaw.items():
                    s_ = ("d", key)
                    if waited.get(s_, 0) >= v:
                        continue
                    waited[s_] = v
                    eng.wait_ge(dma_sems[key], v)
                h = ins.fn(eng)
                if ins.dma_key is not None:
                    h.then_inc(dma_sems[ins.dma_key], 16)
                elif ins.needs_inc:
                    h.then_inc(sems[engname], 1)

        @block.tensor
        def _(eng):
            run("pe", eng)

        @block.scalar
        def _(eng):
            run("act", eng)

        @block.vector
        def _(eng):
            run("dve", eng)

        @block.gpsimd
        def _(eng):
            run("pool", eng)

        @block.sync
        def _(eng):
            run("sp", eng)


def build_program():
    nc = bass.Bass("TRN2", target_bir_lowering=False)

    def din(name, shape):
        return nc.dram_tensor(name, list(shape), F32, kind="ExternalInput")

    def dout(name, shape):
        return nc.dram_tensor(name, list(shape), F32, kind="ExternalOutput")

    xh = din("xh", [HALO + TOK, 1024])
    hm_d = din("hm", [128, 1])
    xs_d = din("xs", [16, 1024])
    cak = din("cak", [512, 384]); cav = din("cav", [512, 384])
    cbk = din("cbk", [128, 128]); cbv = din("cbv", [128, 128])
    cmk = din("cmk", [256, 256]); cmv = din("cmv", [256, 256])
    mem_d = din("mem", [256, 1024])
    w_in = din("w_in", [1024, 3072])
    w_mem = din("w_mem", [1024, 512])
    w_out = din("w_out", [1024, 1024])
    gpre_d = din("gpre_t", [128, 8])
    gmem_d = din("gmem_t", [128, 8])
    gpost_d = din("gpost_b", [128, 1024])
    biasA_d = din("biasA", [128, 6 * 256])
    cA_d = din("cA", [128, 6])
    biasAs_d = din("biasAs", [128, 2 * 96])
    distB_d = din("distB", [128, 256])
    distBs_d = din("distBs", [128, 2 * 16])
    sink_d = din("sink_r", [128, 6])

    y_d = dout("y", [TOK, 1024])
    ys_d = dout("ys", [16, 1024])
    sak_d = dout("sak", [512, 384]); sav_d = dout("sav", [512, 384])
    sbk_d = dout("sbk", [128, 128]); sbv_d = dout("sbv", [128, 128])
    smk_d = dout("smk", [256, 256]); smv_d = dout("smv", [256, 256])
    aks_d = dout("aks", [16, 384]); avs_d = dout("avs", [16, 384])
    bks_d = dout("bks", [16, 128]); bvs_d = dout("bvs", [16, 128])

    es = ExitStack()
    with es:
        def sb(name, shape, dt):
            return es.enter_context(nc.sbuf_tensor("sb_" + name, list(shape), dt))

        w_in_bf = sb("w_in_bf", [128, 8, 3072], BF16)
        w_out_bf = sb("w_out_bf", [128, 8, 1024], BF16)
        xin = [sb("xin%d" % i, [128, 1024], F32) for i in range(3)]
        yst = [sb("yst%d" % i, [128, 1024], F32) for i in range(2)]
        xn_bf = [sb("xn_bf%d" % i, [128, 1024], BF16) for i in range(2)]
        junk = sb("junk", [128, 1024], BF16)
        NSLOT = 16
        xnT2 = [sb("xnT%d" % i, [128, 8, 512], BF16) for i in range(2)]
        kT_a = sb("kT_a", [128, 3, NSLOT * 128], BF16)
        vaug_a = sb("vaug_a", [128, NSLOT, 3, 192], BF16)
        kT_b = sb("kT_b", [128, NSLOT * 128], BF16)
        vaug_b = sb("vaug_b", [128, NSLOT, 192], BF16)
        NQ = 1
        qT_a2 = [sb("qT_a%d" % i, [128, 3, 512], BF16) for i in range(NQ)] * (2 // NQ)
        qT_b2 = [sb("qT_b%d" % i, [128, 3, 512], BF16) for i in range(NQ)] * (2 // NQ)
        qT_m2 = [sb("qT_m%d" % i, [128, 2, 512], BF16) for i in range(NQ)] * (2 // NQ)
        sgog = [sb("sgog%d" % i, [128, 8, 512], BF16) for i in range(2)]
        tbuf = [sb("tbuf%d" % i, [128, 512], BF16) for i in range(2)]
        NPT = 6
        PT = [sb("PT%d" % i, [128, 2, 512], BF16) for i in range(NPT)]
        nrm = [sb("nrm%d" % i, [128, 512], F32) for i in range(1)]
        nrmT = [sb("nrmT%d" % i, [128, 512], F32) for i in range(1)]
        expBB_a = sb("expBB_a", [128, 6, 256], BF16)
        expBB_b = sb("expBB_b", [128, 6, 256], BF16)
        gpost_b = sb("gpost_b", [128, 1024], F32)
        kT_m = sb("kT_m", [128, 2, 256], BF16)
        vaug_m = sb("vaug_m", [128, 2, 2, 192], BF16)
        ident = sb("ident", [128, 128], BF16)
        iota_t = sb("iota_t", [128, 128], F32)
        pid = sb("pid", [128, 1], F32)
        stat = sb("stat", [128, 48], F32)
        stat2 = sb("stat2", [128, 48], F32)
        rstd = sb("rstd", [128, 48], F32)
        mhalf = sb("mhalf", [128, 1], F32)
        onec = sb("onec", [128, 1], F32)
        gpre_t = sb("gpre_t", [128, 8], F32)
        gmem_t = sb("gmem_t", [128, 8], F32)
        hm = sb("hm", [128, 1], F32)
        esink = sb("esink", [128, 6], F32)
        esink2 = sb("esink2", [128, 3], F32)
        sink_t = sb("sink_t", [128, 6], F32)
        cA = sb("cA", [128, 6], F32)
        esrow = sb("esrow", [1, 6, 16], BF16)
        selrow = sb("selrow", [1, 2, 128], BF16)
        xnTs = sb("xnTs", [128, 8, 16], BF16)
        qs_a = sb("qs_a", [128, 3, 16], BF16)
        qs_b = sb("qs_b", [128, 3, 16], BF16)
        qs_m = sb("qs_m", [128, 2, 16], BF16)
        sgs = sb("sgs", [128, 8, 16], BF16)
        tbs = sb("tbs", [128, 8, 16], BF16)
        ogs = sb("ogs", [128, 8, 16], BF16)
        PTs = sb("PTs", [128, 5, 2, 48], BF16)
        PTsb = sb("PTsb", [128, 2, 2, 48], BF16)
        PTsm = sb("PTsm", [128, 2, 2, 32], BF16)
        expBs_a = sb("expBs_a", [128, 2, 96], BF16)
        expBs_b = sb("expBs_b", [128, 2, 96], BF16)
        distBs = sb("distBs", [128, 32], F32)
        nrms = iota_t
        kcast = sb("kcast", [128, 384], BF16)

        PS = [es.enter_context(nc.psum_tensor("PS%d" % i, [128, 1024], F32)) for i in range(4)]

        def bank(i):
            return PS[i // 2][:, (i % 2) * 512:(i % 2) * 512 + 512]

        def bank_bf(i):
            return PS[i // 2][:].bitcast(BF16)[:, (i % 2) * 1024:(i % 2) * 1024 + 1024]

        sems = {e: es.enter_context(nc.semaphore("s_" + e)) for e in ENGS}
        dkeys = ["xin0", "xin1", "xin2", "yst0", "yst1", "c0", "c1", "c2", "c3", "c4", "c5",
                 "c6", "c7", "c8", "c9", "o0", "o1", "o2", "o3", "o4", "o5", "o6", "o7"]
        dsems = {k: es.enter_context(nc.semaphore("d_" + k)) for k in dkeys}
        block = es.enter_context(nc.Block())
        P = Prog()
        out_tokens = []

        def fsz(ap):
            n = 1
            for d in ap.shape[1:]:
                n *= d
            return n

        def dma(out, in_, key, reads=(), writes=(), eng="sp"):
            nbytes = out.shape[0] * fsz(out) * 4
            return P.add(eng, lambda e, o=out, i=in_: e.dma_start(out=o, in_=i),
                         reads=reads, writes=writes, dma_key=key, est=nbytes / 170e3)

        def mm(out, lhsT, rhs, start, stop, reads, writes, skip=False):
            return P.add("pe", lambda e, o=out, l=lhsT, r=rhs, s=start, t=stop, k=skip:
                         e.matmul(o, lhsT=l, rhs=r, start=s, stop=t, skip_group_check=k), reads=reads, writes=writes,
                         est=(max(64, fsz(rhs)) / 2100.0 + 0.06) if PECAL else (max(64, fsz(rhs)) / 1960.0 + 0.02))

        def tr(out, in_, idn, reads, writes):
            return P.add("pe", lambda e, o=out, i=in_, d=idn: e.transpose(o, i, d),
                         reads=reads, writes=writes, est=0.12)

        def act(out, in_, func, reads, writes, bias=None, scale=None, accum=None):
            kw = {}
            if bias is not None:
                kw["bias"] = bias
            if scale is not None:
                kw["scale"] = scale
            if accum is not None:
                kw["accum_out"] = accum
            cls = "tanh" if func == AF.Tanh else ("ln" if func == AF.Ln else None)
            est = (fsz(in_) + (230 if PECAL else 300)) / 1400.0 + (0.1 if accum is not None else 0.0) + \
                (0.09 if not isinstance(scale, (int, float, type(None))) else 0.0)
            return P.add("act", lambda e, o=out, i=in_, f=func, k=kw: e.activation(o, i, f, **k),
                         reads=reads, writes=writes, est=est, cls=cls)

        def vest(eng, n, two_src=False):
            if eng == "pool":
                return 0.12 + n * 0.002
            return (n + 150) / 960.0

        def ts(eng, out, in0, s1, s2, op0, op1, reads, writes):
            est = vest(eng, fsz(in0))
            if op1 is None:
                return P.add(eng, lambda e, o=out, i=in0, a=s1, p0=op0: e.tensor_scalar(o, i, a, None, p0),
                             reads=reads, writes=writes, est=est)
            return P.add(eng, lambda e, o=out, i=in0, a=s1, b=s2, p0=op0, p1=op1:
                         e.tensor_scalar(o, i, a, b, p0, p1), reads=reads, writes=writes, est=est)

        def tt(eng, out, in0, in1, op, reads, writes):
            est = 0.5 if op == ALU.pow else vest(eng, fsz(in0))
            return P.add(eng, lambda e, o=out, a=in0, b=in1, p=op: e.tensor_tensor(o, a, b, p),
                         reads=reads, writes=writes, est=est)

        def stt(eng, out, in0, scalar, in1, op0, op1, reads, writes):
            return P.add(eng, lambda e, o=out, a=in0, s=scalar, b=in1, p0=op0, p1=op1:
                         e.scalar_tensor_tensor(o, a, s, b, p0, p1), reads=reads, writes=writes,
                         est=vest(eng, fsz(in0)))

        def cp(eng, out, in_, reads, writes):
            if eng == "act":
                return act(out, in_, AF.Copy, reads, writes)
            p0 = in_.base_partition()
            sc = onec[p0:p0 + in_.shape[0], 0:1]
            return P.add(eng, lambda e, o=out, i=in_, c=sc: e.tensor_scalar(o, i, c, None, ALU.mult),
                         reads=list(reads) + ["onec"], writes=writes, est=vest(eng, fsz(in_)) + 0.06)

        def mset(eng, ap, val, writes):
            return P.add(eng, lambda e, a=ap, v=val: e.memset(a, v), writes=writes, est=0.15 + fsz(ap) * 0.0008)

        rr = {"xin": 0, "bank": 0, "yst": 0, "xn": 0, "stat": 0, "tb": 0, "pt": 0, "nrm": 0,
              "sc": 0, "acc": 0, "nrmT": 0, "op": 0, "evac": 0, "cast": 0, "ck": 0, "ok": 0}

        def nxt(name, n):
            v = rr[name]
            rr[name] = (v + 1) % n
            return v

        def next_bank():
            return PS_CFG["PROJ"][nxt("bank", len(PS_CFG["PROJ"]))]

        def next_stat():
            v = rr["stat"]
            rr["stat"] += 1
            assert v < 48
            return v

        P.add("pool", lambda e: e.iota(iota_t[:], [[1, 128]], base=0, channel_multiplier=0,
                                       allow_small_or_imprecise_dtypes=True), writes=["iota"], est=0.3)
        P.add("pool", lambda e: e.iota(pid[:], [[0, 1]], base=0, channel_multiplier=1,
                                       allow_small_or_imprecise_dtypes=True), writes=["pid"], est=0.3)
        ts("dve", ident[:], iota_t[:], pid[:, 0:1], None, ALU.is_equal, None, ["iota", "pid"], ["ident"])
        mset("pool", mhalf[:], -0.5, ["mhalf"])
        mset("pool", onec[:], 1.0, ["onec"])
        mset("pool", selrow[0:1, 0, 0:64], 0.0, ["selrow"])
        mset("pool", selrow[0:1, 0, 64:128], 1.0, ["selrow"])
        mset("pool", selrow[0:1, 1, 0:64], 1.0, ["selrow"])
        mset("pool", selrow[0:1, 1, 64:128], 0.0, ["selrow"])
        mset("pool", vaug_a[:], 1.0, [("va", s) for s in range(NSLOT)])
        mset("pool", vaug_b[:], 1.0, [("vb", s) for s in range(NSLOT)])
        mset("pool", vaug_m[:], 1.0, [("vm", 0), ("vm", 1)])

        dma(gpre_t[:], gpre_d.ap(), "c0", writes=["gpre"])
        dma(gmem_t[:], gmem_d.ap(), "c1", writes=["gmem"])
        dma(hm[:], hm_d.ap(), "c2", writes=["hm"])
        dma(sink_t[:], sink_d.ap(), "c3", writes=["sink_t"])
        dma(cA[:], cA_d.ap(), "c4", writes=["cA"])
        dma(gpost_b[:], gpost_d.ap(), "c5", writes=["gpost"])
        dma(distBs[:], distBs_d.ap(), "c6", writes=["distBs"])
        for s in range(12, 16):
            ts("dve", vaug_a[:, s, :, 64:128], vaug_a[:, s, :, 64:128], hm[:, 0:1], None, ALU.mult, None,
               ["hm", ("va", s)], [("va", s)])
            ts("dve", vaug_b[:, s, 64:128], vaug_b[:, s, 64:128], hm[:, 0:1], None, ALU.mult, None,
               ["hm", ("vb", s)], [("vb", s)])
        act(esink[:], sink_t[:], AF.Exp, ["sink_t"], ["esink"])
        cp("dve", esink2[0:64, 0:3], esink[0:64, 0:3], ["esink"], ["esink2"])
        cp("dve", esink2[64:128, 0:3], esink[64:128, 3:6], ["esink"], ["esink2"])
        for h in range(6):
            cp("dve", esrow[0:1, h, :], esink[0:1, h:h + 1].to_broadcast([1, 16]), ["esink"], ["esrow"])
        ts("dve", cA[:], cA[:], -1.0, None, ALU.mult, None, ["cA"], ["ncA"])

        def xin_slot():
            return nxt("xin", 3)

        for half in range(2):
            s = xin_slot()
            dma(xin[s][:, 0:768], biasA_d.ap()[:, half * 768:(half + 1) * 768], "xin%d" % s,
                writes=[("xin", s)])
            for hh in range(3):
                h = half * 3 + hh
                act(expBB_a[:, h, :], xin[s][:, hh * 256:(hh + 1) * 256], AF.Exp, [("xin", s), "ncA"],
                    ["ebba"], bias=cA[:, h:h + 1])
        s = xin_slot()
        dma(xin[s][:, 0:256], distB_d.ap(), "xin%d" % s, writes=[("xin", s)])
        dma(xin[s][:, 256:448], biasAs_d.ap(), "xin%d" % s, writes=[("xin", s)])
        for h in range(6):
            act(expBB_b[:, h, :], xin[s][:, 0:256], AF.Exp, [("xin", s)], ["ebbb"], scale=-SLOPES[h])
        for blk in range(2):
            for e_ in range(2):
                for c_ in range(3):
                    ha = 2 * c_ + e_
                    hb = c_ + 3 * e_
                    col = blk * 96 + e_ * 48 + c_ * 16
                    act(expBs_a[:, blk, e_ * 48 + c_ * 16:e_ * 48 + c_ * 16 + 16],
                        xin[s][:, 256 + col:256 + col + 16], AF.Exp, [("xin", s), "ncA"], ["ebsa"],
                        bias=cA[:, ha:ha + 1])
                    act(expBs_b[:, blk, e_ * 48 + c_ * 16:e_ * 48 + c_ * 16 + 16],
                        distBs[:, blk * 16:blk * 16 + 16], AF.Exp, ["distBs"], ["ebsb"], scale=-SLOPES[hb])
        mset("pool", expBB_a[64:128, :, 0:64], 0.0, ["ebba"])
        mset("pool", expBB_b[64:128, :, 0:64], 0.0, ["ebbb"])
        mset("pool", expBB_b[0:64, :, 192:256], 0.0, ["ebbb"])

        wpool = [(xin[0], "xin0", ("xin", 0)), (yst[0], "yst0", ("yst", 0)), (xin[1], "xin1", ("xin", 1)),
                 (yst[1], "yst1", ("yst", 1)), (xin[2], "xin2", ("xin", 2))]
        rr["wp"] = 0

        def wslot():
            return wpool[nxt("wp", 5)]

        def load_w_in(cs):
          for hf in range(2):
            for k in range(8):
                wt, wkey, wtok = wslot()
                c_lo = cs * 1024 + hf * 512
                dma(wt[:, 0:512], w_in.ap()[k * 128:(k + 1) * 128, c_lo:c_lo + 512], wkey, writes=[wtok])
                pieces = {
                    0: [(0, 1024, 0, False)],
                    1: [(0, 128, 1024, False), (128, 256, 2048, False), (256, 512, 1280, False),
                        (512, 896, 1536, True), (896, 1024, 1920, False)],
                    2: [(0, 128, 1152, False), (128, 512, 2176, True), (512, 1024, 2560, False)],
                }[cs]
                if cs == 0:
                    pieces = [(0, 512, 0, False), (512, 1024, 512, False)]
                pieces = [p_ for p_ in pieces if hf * 512 <= p_[0] < (hf + 1) * 512]
                use_dve = nxt("cast", 2) == 0
                for (a0, a1, d0, pm) in pieces:
                    o_ap = w_in_bf[:, k, d0:d0 + (a1 - a0)]
                    i_ap = wt[:, a0 - hf * 512:a1 - hf * 512]
                    if pm:
                        o_ap = o_ap.rearrange("p (h a d) -> p h a d", h=3, a=2, d=64)
                        i_ap = i_ap.rearrange("p (a h d) -> p h a d", a=2, h=3, d=64)
                    if use_dve:
                        ts("dve", o_ap, i_ap, gpre_t[:, k:k + 1], None, ALU.mult, None, [wtok, "gpre"], [("win", k, cs, hf)])
                    else:
                        act(o_ap, i_ap, AF.Copy, [wtok, "gpre"], [("win", k, cs, hf)], scale=gpre_t[:, k:k + 1])

        def load_w_mem():
            for k in range(8):
                wt, wkey, wtok = wslot()
                dma(wt[:, 0:512], w_mem.ap()[k * 128:(k + 1) * 128, :], wkey, writes=[wtok])
                ts("dve", sgog[0][:, k, :], wt[:, 0:512], gmem_t[:, k:k + 1], None, ALU.mult, None,
                   [wtok, "gmem"], [("sg", 0, k)])

        def load_w_out():
            for c in range(8):
                s = xin_slot()
                if 3 <= c < 6:
                    p_ = c - 3
                    r0 = 384 + 64 * p_
                    r1 = 384 + 64 * (p_ + 3)
                    dma(xin[s][0:64, :], w_out.ap()[r0:r0 + 64, :], "xin%d" % s, writes=[("xin", s)])
                    dma(xin[s][64:128, :], w_out.ap()[r1:r1 + 64, :], "xin%d" % s, writes=[("xin", s)])
                else:
                    dma(xin[s][:], w_out.ap()[c * 128:(c + 1) * 128, :], "xin%d" % s, writes=[("xin", s)])
                cp("act" if c % 2 else "dve", w_out_bf[:, c, :], xin[s][:], [("xin", s)], [("wout", c)])

        def norm_transpose(src_ap, n, dst, dst_tok, gain_unused=None):
            import os
            NT = 99
            s = xin_slot()
            dma(xin[s][0:n, :], src_ap, "xin%d" % s, writes=[("xin", s)])
            b = nxt("xn", 2)
            st = next_stat()
            if NT < 1: return
            act(xn_bf[b][0:n, :], xin[s][0:n, :], AF.Square, [("xin", s)], [("xn", b), ("stat", st)],
                scale=1.0 / 32.0, accum=stat[0:n, st:st + 1])
            if NT < 2: return
            ts("dve", stat2[0:n, st:st + 1], stat[0:n, st:st + 1], EPS, None, ALU.add, None,
               [("stat", st)], [("stat2", st)])
            tt("pool", rstd[0:n, st:st + 1], stat2[0:n, st:st + 1], mhalf[0:n, :], ALU.pow,
               [("stat2", st), "mhalf"], [("rstd", st)])
            if NT < 3: return
            ts("dve", xn_bf[b][0:n, :], xin[s][0:n, :], rstd[0:n, st:st + 1], None, ALU.mult, None,
               [("xin", s), ("rstd", st)], [("xn", b)])
            if NT < 4: return
            bk = next_bank()
            pv = bank_bf(bk)
            for k in range(8):
                tr(pv[:, k * n:(k + 1) * n], xn_bf[b][0:n, k * 128:(k + 1) * 128], ident[0:n, 0:n],
                   [("xn", b), "ident"], [("ps", bk)])
            if NT < 5: return
            cp("dve", dst, pv[:, 0:8 * n].rearrange("p (k n) -> p k n", k=8), [("ps", bk)], [dst_tok])

        C_QA, C_KA, C_VA, C_GA, C_QB, C_KB, C_VB, C_GB, C_QM, C_GM = 0, 384, 768, 1152, 1536, 1920, 2048, 2176, 2560, 2816

        def win_reads(c0, c1):
            hs = sorted(set((d // 1024, (d % 1024) // 512) for d in range(c0, c1, 64)))
            return [("win", k, cs, hf) for k in range(8) for (cs, hf) in hs]

        def wcols(k, c0, n=128):
            return w_in_bf[:, k, c0:c0 + n]

        def ga_col(c):
            return 2048 if c == 0 else C_GA + 128 * c

        def wcols_pair(k, base, p_):
            return w_in_bf[:, k, base + 128 * p_:base + 128 * p_ + 128]

        def evac_engine():
            return "dve"

        rr["mul"] = 0

        def mul_engine():
            return "pool" if nxt("mul", 3) == 0 else "dve"

        def proj_fm(lhs_fn, wreads, rhs_fn, xtoks, ncols, evac):
            bk = next_bank()
            for k in range(8):
                mm(bank(bk)[:, 0:ncols], lhs_fn(k), rhs_fn(k), k == 0, k == 7, wreads + xtoks, [("ps", bk)])
            evac(bank(bk)[:, 0:ncols], ("ps", bk))

        def gate_evac(dst, dst_tok, n):
            def f(ps, pstok):
                import os
                KG = 99
                if KG < 1: return
                b = nxt("tb", 2)
                act(tbuf[b][:, 0:n], ps, AF.Tanh, [pstok], [("tb", b)], scale=0.5)
                if KG < 2: return
                stt("dve", dst, tbuf[b][:, 0:n], 1.0, ps, ALU.add, ALU.mult, [("tb", b), pstok], [dst_tok])
            return f

        def copy_evac(dst, dst_tok):
            def f(ps, pstok):
                cp(evac_engine(), dst, ps, [pstok], [dst_tok])
            return f

        def slot_of(blk):
            return 12 + blk if blk < 4 else (blk - 4) % 12

        xbuf_of = {}
        rr["xb"] = 0

        def project_stage(st_i, xonly=False, xdone=False):
            halo = st_i == 0
            blk0 = 4 * st_i
            sl0 = slot_of(blk0)
            if not xdone:
                xb = nxt("xb", 2)
                xbuf_of[st_i] = xb
                for tb in range(4):
                    r0 = (blk0 + tb) * 128
                    norm_transpose(xh.ap()[r0:r0 + 128, :], 128, xnT2[xb][:, :, tb * 128:(tb + 1) * 128], ("xnT", xb, tb))
                    yield 1.0
            xb = xbuf_of[st_i]
            xnT = xnT2[xb]
            xt = [("xnT", xb, tb) for tb in range(4)]
            par = st_i % 2
            qT_a, qT_b, qT_m, sg = qT_a2[par], qT_b2[par], qT_m2[par], sgog[par]
            if xonly:
                return
            rhs_x = lambda k: xnT[:, k, :]
            allwin_ = [("win", k, cs, hf) for k in range(8) for cs in range(3) for hf in range(2)]
            CH = 2.3
            if not halo:
                for c in range(3):
                    proj_fm(lambda k, c=c: wcols(k, C_QA + 128 * c), win_reads(C_QA, C_QA + 384), rhs_x, xt, 512,
                            copy_evac(qT_a[:, c, :], ("qa", par % NQ, c)))
                    yield CH
            for c in range(3):
                def kevac(ps, pstok, c=c):
                    cp(evac_engine(), kT_a[:, c, sl0 * 128:sl0 * 128 + 512], ps, [pstok],
                       [("ka", sl0 + i, c) for i in range(4)])
                proj_fm(lambda k, c=c: wcols(k, C_KA + 128 * c), win_reads(C_KA, C_KA + 384), rhs_x, xt, 512, kevac)
                yield CH
            if not halo:
                for p_ in range(3):
                    proj_fm(lambda k, p_=p_: wcols_pair(k, C_QB, p_), win_reads(C_QB, C_QB + 384), rhs_x, xt, 512,
                            copy_evac(qT_b[:, p_, :], ("qb", par % NQ, p_)))
                    yield CH

            def kbevac(ps, pstok):
                cp(evac_engine(), kT_b[:, sl0 * 128:sl0 * 128 + 512], ps, [pstok],
                   [("kb", sl0 + i) for i in range(4)])
            proj_fm(lambda k: wcols(k, C_KB), win_reads(C_KB, C_KB + 128), rhs_x, xt, 512, kbevac)
            yield CH
            if not halo:
                for c in range(2):
                    proj_fm(lambda k, c=c: wcols(k, C_QM + 128 * c), win_reads(C_QM, C_QM + 256), rhs_x, xt, 512,
                            copy_evac(qT_m[:, c, :], ("qm", par % NQ, c)))
                    yield CH
            last = st_i == 4
            for tb in range(4):
                sl = sl0 + tb
                bk = next_bank()
                pstok = ("ps", bk)
                pb = bank(bk)
                for k in range(8):
                    mm(pb[:, 0:512], xnT[:, k, tb * 128:(tb + 1) * 128], w_in_bf[:, k, C_VA:C_VA + 512],
                       k == 0, k == 7, allwin_ + [("xnT", xb, tb)], [pstok])
                pa = pb[:, 0:384].rearrange("p (c e d) -> p c e d", c=3, e=2, d=64)
                va = vaug_a[:, sl]
                if halo:
                    ts("dve", va[:, :, 0:64], pa[:, :, 0, :], hm[:, 0:1], None, ALU.mult, None, [pstok, "hm"], [("va", sl)])
                    ts("dve", va[:, :, 128:192], pa[:, :, 1, :], hm[:, 0:1], None, ALU.mult, None, [pstok, "hm"], [("va", sl)])
                    ts("dve", vaug_b[:, sl, 0:64], pb[:, 384:448], hm[:, 0:1], None, ALU.mult, None, [pstok, "hm"], [("vb", sl)])
                    ts("dve", vaug_b[:, sl, 128:192], pb[:, 448:512], hm[:, 0:1], None, ALU.mult, None, [pstok, "hm"], [("vb", sl)])
                else:
                    cp("dve", va[:, :, 0:64], pa[:, :, 0, :], [pstok], [("va", sl)])
                    cp("dve", va[:, :, 128:192], pa[:, :, 1, :], [pstok], [("va", sl)])
                    cp("dve", vaug_b[:, sl, 0:64], pb[:, 384:448], [pstok], [("vb", sl)])
                    cp("dve", vaug_b[:, sl, 128:192], pb[:, 448:512], [pstok], [("vb", sl)])
                if last:
                    ys_ = nxt("yst", 2)
                    ytok = ("yst", ys_)
                    cp("act", yst[ys_][:, 512:1024], pb[:, 0:512], [pstok], [ytok])
                    bk2 = next_bank()
                    pstok2 = ("ps", bk2)
                    pb2 = bank(bk2)
                    for k in range(8):
                        mm(pb2[:, 0:384], xnT[:, k, tb * 128:(tb + 1) * 128], w_in_bf[:, k, C_KA:C_KA + 384],
                           k == 0, k == 7, win_reads(C_KA, C_KA + 384) + [("xnT", xb, tb)], [pstok2])
                    for k in range(8):
                        mm(pb2[:, 384:512], xnT[:, k, tb * 128:(tb + 1) * 128], w_in_bf[:, k, C_KB:C_KB + 128],
                           k == 0, k == 7, win_reads(C_KB, C_KB + 128) + [("xnT", xb, tb)], [pstok2])
                    cp("dve", yst[ys_][:, 0:512], pb2[:, 0:512], [pstok2], [ytok])
                    key = "yst%d" % ys_
                    dma(sak_d.ap()[tb * 128:(tb + 1) * 128, :], yst[ys_][:, 0:384], key, reads=[ytok])
                    dma(sav_d.ap()[tb * 128:(tb + 1) * 128, :], yst[ys_][:, 512:896], key, reads=[ytok])
                    if tb == 3:
                        dma(sbk_d.ap(), yst[ys_][:, 384:512], key, reads=[ytok])
                        dma(sbv_d.ap(), yst[ys_][:, 896:1024], key, reads=[ytok])
                    out_tokens.append(ytok)
                yield 3.0 if not last else 6.0
            if not halo:
                for c in range(3):
                    proj_fm(lambda k, c=c: wcols(k, ga_col(c)), win_reads(C_GA, C_GA + 384), rhs_x, xt, 512,
                            gate_evac(sg[:, c, :], ("sg", par, c), 512))
                    yield CH
                for p_ in range(3):
                    proj_fm(lambda k, p_=p_: wcols_pair(k, C_GB, p_), win_reads(C_GB, C_GB + 384), rhs_x, xt, 512,
                            gate_evac(sg[:, 3 + p_, :], ("sg", par, 3 + p_), 512))
                    yield CH
                for c in range(2):
                    proj_fm(lambda k, c=c: wcols(k, C_GM + 128 * c), win_reads(C_GM, C_GM + 256), rhs_x, xt, 512,
                            gate_evac(sg[:, 6 + c, :], ("sg", par, 6 + c), 512))
                    yield CH

        def mem_kv():
            xb = nxt("xb", 2)
            memxT = xnT2[xb]
            ogT = sgog[0]
            for mb in range(2):
                norm_transpose(mem_d.ap()[mb * 128:(mb + 1) * 128, :], 128, memxT[:, :, mb * 128:(mb + 1) * 128],
                               ("xnT", xb, mb))
            mt = [("xnT", xb, 0), ("xnT", xb, 1)]
            ogr = [("sg", 0, k) for k in range(8)]
            for c in range(2):
                bk = next_bank()
                for k in range(8):
                    mm(bank(bk)[:, 0:256], ogT[:, k, c * 128:(c + 1) * 128], memxT[:, k, 0:256], k == 0, k == 7,
                       ogr + mt, [("ps", bk)])
                cp("dve", kT_m[:, c, :], bank(bk)[:, 0:256], [("ps", bk)], [("km", c)])
            for mb in range(2):
                bk = next_bank()
                pstok = ("ps", bk)
                for k in range(8):
                    mm(bank(bk)[:, 0:512], memxT[:, k, mb * 128:(mb + 1) * 128], ogT[:, k, :], k == 0, k == 7,
                       ogr + mt, [pstok])
                ys_ = nxt("yst", 2)
                ytok = ("yst", ys_)
                cp("act", yst[ys_][:, 0:512], bank(bk)[:, 0:512], [pstok], [ytok])
                pm = bank(bk)[:, 256:512].rearrange("p (c e d) -> p c e d", c=2, e=2, d=64)
                vm = vaug_m[:, mb]
                cp("act", vm[:, :, 0:64], pm[:, :, 0, :], [pstok], [("vm", mb)])
                cp("act", vm[:, :, 128:192], pm[:, :, 1, :], [pstok], [("vm", mb)])
                key = "yst%d" % ys_
                dma(smk_d.ap()[mb * 128:(mb + 1) * 128, :], yst[ys_][:, 0:256], key, reads=[ytok])
                dma(smv_d.ap()[mb * 128:(mb + 1) * 128, :], yst[ys_][:, 256:512], key, reads=[ytok])
                out_tokens.append(ytok)

        def attention_group(tiles, chunk, n_q, sink_heads=None):
            ab = PS_CFG["ACC"][nxt("acc", len(PS_CFG["ACC"]))]
            accE = bank(2 * ab)
            accO = bank(2 * ab + 1)
            tokE = ("ps", 2 * ab)
            tokO = ("ps", 2 * ab + 1)
            n = len(tiles)
            state = {}

            def qk(i):
                t = tiles[i]
                sc = PS_CFG["S"][nxt("sc", len(PS_CFG["S"]))]
                S = PS[sc]
                stok = [("ps", 2 * sc), ("ps", 2 * sc + 1)]
                nc_ = t["ncols"]
                r = t["rows"]
                mm(S[0:r, 0:nc_], t["kE"], t["qE"], True, True, t["ktoks"] + t["qtoks"], [stok[0]])
                mm(S[0:r, 512:512 + nc_], t["kO"], t["qO"], True, True, t["ktoks"] + t["qtoks"], [stok[1]])
                pi = nxt("pt", NPT)
                pttok = ("pt", pi)
                Sv = S[:].rearrange("p (b n) -> p b n", b=2)
                act(PT[pi][0:r, :, 0:nc_], Sv[0:r, :, 0:nc_], AF.Exp, stok, [pttok], scale=SCALE)
                if t["post"] is not None:
                    t["post"](PT[pi], pttok)
                state[i] = (pi, pttok)

            def pv(i):
                t = tiles[i]
                pi, pttok = state[i]
                nc_ = t["ncols"]
                r = t["rows"]
                c0 = t["c0"]
                first = i == 0
                lastmm = (i == n - 1) and sink_heads is None
                vE_, one_, vO_ = t["vE"][:, 0:64], t["vE"][:, 64:128], t["vO"][:, 64:128]
                pe_, po_ = PT[pi][0:r, 0, 0:nc_], PT[pi][0:r, 1, 0:nc_]
                rd = t["vtoks"] + [pttok]
                mm(accE[0:64, c0:c0 + nc_], vE_, pe_, first, lastmm, rd, [tokE], skip=True)
                mm(accE[64:128, c0:c0 + nc_], vO_, po_, first, lastmm, rd, [tokE], skip=True)
                mm(accO[0:64, c0:c0 + nc_], one_, pe_, first, lastmm, rd, [tokO], skip=True)
                mm(accO[64:128, c0:c0 + nc_], one_, po_, first, lastmm, rd, [tokO], skip=True)

            for i in range(n + 1):
                if i < n:
                    qk(i)
                if i >= 1:
                    pv(i - 1)
                yield 0.7
            if sink_heads is not None:
                hE, hO, srow = sink_heads
                mm(accE[:, 0:n_q], selrow[0:1, 0, :], srow(hE), False, True, ["selrow", "esrow"], [tokE])
                mm(accO[:, 0:n_q], selrow[0:1, 1, :], srow(hO), False, True, ["selrow", "esrow"], [tokO])
            return accE, accO, tokE, tokO

        def normalise(accE, accO, tokE, tokO, sg_ap, sgtok, og_ap, ogtok, n_q, sinks=None):
            ni = nxt("nrm", 1)
            ntok = ("nrm", ni)
            N = nrm[ni]
            if sinks is None:
                act(N[:, 0:n_q], accO[:, 0:n_q], AF.Ln, [tokO], [ntok])
            else:
                act(N[:, 0:n_q], accO[:, 0:n_q], AF.Ln, [tokO, "esink2"], [ntok], bias=esink2[:, sinks[0]:sinks[0] + 1])
            act(N[:, 0:n_q], N[:, 0:n_q], AF.Exp, [ntok], [ntok], scale=-1.0)
            ti = nxt("nrmT", 1)
            ttok = ("nrmT", ti)
            stt("dve", nrmT[ti][:, 0:n_q], accE[:, 0:n_q], 0.5, sg_ap, ALU.mult, ALU.mult, [tokE, sgtok], [ttok])
            tt("dve", og_ap, nrmT[ti][:, 0:n_q], N[:, 0:n_q], ALU.mult, [ttok, ntok], [ogtok])

        def attention_stage(st_i):
            I0 = 4 * st_i
            par = st_i % 2
            qT_a, qT_b, qT_m, sg = qT_a2[par], qT_b2[par], qT_m2[par], sgog[par]
            ogT = sg
            groups = []
            for p_ in range(3):
                tiles = []
                for J in range(I0 - 4, I0 + 4):
                    sl = slot_of(J)
                    c0b = max(J - I0, 0)
                    c1b = min(J + 4 - I0, 3) + 1
                    ncols = (c1b - c0b) * 128
                    rlo = max(0, c0b + I0 - J)
                    rhi = min(1, c1b - 1 + I0 - J)
                    corner = (J + 4 - I0) if J + 4 <= I0 + 3 else None

                    def post(pt, pttok, p_=p_, rlo=rlo, rhi=rhi, c0b=c0b, J=J, corner=corner, I0=I0):
                        if rhi >= rlo:
                            cs = (rlo + J - I0 - c0b) * 128
                            ce = (rhi + 1 + J - I0 - c0b) * 128
                            tt(mul_engine(), pt[:, :, cs:ce], pt[:, :, cs:ce],
                               expBB_a[:, 2 * p_:2 * p_ + 2, rlo * 128:(rhi + 1) * 128], ALU.mult,
                               [pttok, "ebba"], [pttok])
                        if corner is not None:
                            off = (corner - c0b) * 128
                            mset("pool", pt[0:64, :, off + 64:off + 128], 0.0, [pttok])
                    tiles.append(dict(
                        kE=kT_a[0:64, p_, sl * 128:(sl + 1) * 128], kO=kT_a[64:128, p_, sl * 128:(sl + 1) * 128],
                        qE=qT_a[0:64, p_, c0b * 128:c1b * 128], qO=qT_a[64:128, p_, c0b * 128:c1b * 128],
                        ktoks=[("ka", sl, p_)], qtoks=[("qa", par % NQ, p_)], c0=c0b * 128, ncols=ncols, rows=128,
                        vE=vaug_a[:, sl, p_, 0:128], vO=vaug_a[:, sl, p_, 64:192], vtoks=[("va", sl)], post=post))
                def grpA(tiles=tiles, p_=p_):
                    accE, accO, tokE, tokO = yield from attention_group(tiles, p_, 512)
                    normalise(accE, accO, tokE, tokO, sg[:, p_, :], ("sg", par, p_), ogT[:, p_, :], ("sg", par, p_), 512)
                groups.append(grpA)
            for p_ in range(3):
                tiles = []
                for J in range(I0 - 1, I0 + 4):
                    sl = slot_of(J)
                    c0b = max(J - I0, 0)
                    c1b = min(J + 1 - I0, 3) + 1
                    ncols = (c1b - c0b) * 128
                    rlo = max(0, c0b + I0 - J)
                    rhi = min(1, c1b - 1 + I0 - J)

                    def post(pt, pttok, p_=p_, rlo=rlo, rhi=rhi, ncols=ncols):
                        tt(mul_engine(), pt[:, :, 0:ncols], pt[:, :, 0:ncols],
                           expBB_b[:, p_:p_ + 4:3, rlo * 128:(rhi + 1) * 128], ALU.mult, [pttok, "ebbb"], [pttok])
                    tiles.append(dict(
                        kE=kT_b[0:64, sl * 128:(sl + 1) * 128], kO=kT_b[64:128, sl * 128:(sl + 1) * 128],
                        qE=qT_b[0:64, p_, c0b * 128:c1b * 128], qO=qT_b[64:128, p_, c0b * 128:c1b * 128],
                        ktoks=[("kb", sl)], qtoks=[("qb", par % NQ, p_)], c0=c0b * 128, ncols=ncols, rows=128,
                        vE=vaug_b[:, sl, 0:128], vO=vaug_b[:, sl, 64:192], vtoks=[("vb", sl)], post=post))
                def grpB(tiles=tiles, p_=p_):
                    accE, accO, tokE, tokO = yield from attention_group(tiles, 3 + p_, 512, sink_heads=None)
                    normalise(accE, accO, tokE, tokO, sg[:, 3 + p_, :], ("sg", par, 3 + p_), ogT[:, 3 + p_, :],
                              ("sg", par, 3 + p_), 512, sinks=(p_, p_ + 3))
                groups.append(grpB)
            for c in range(2):
                tiles = []
                for mb in range(2):
                    tiles.append(dict(
                        kE=kT_m[0:64, c, mb * 128:(mb + 1) * 128], kO=kT_m[64:128, c, mb * 128:(mb + 1) * 128],
                        qE=qT_m[0:64, c, :], qO=qT_m[64:128, c, :],
                        ktoks=[("km", c)], qtoks=[("qm", par % NQ, c)], c0=0, ncols=512, rows=128,
                        vE=vaug_m[:, mb, c, 0:128], vO=vaug_m[:, mb, c, 64:192], vtoks=[("vm", mb)], post=None))
                def grpM(tiles=tiles, c=c):
                    accE, accO, tokE, tokO = yield from attention_group(tiles, 6 + c, 512)
                    normalise(accE, accO, tokE, tokO, sg[:, 6 + c, :], ("sg", par, 6 + c), ogT[:, 6 + c, :],
                              ("sg", par, 6 + c), 512)
                groups.append(grpM)
            order_ = {"seq": [0, 1, 2, 3, 4, 5, 6, 7], "mix": [0, 3, 1, 4, 2, 5, 6, 7]}["seq"]
            glist = [groups[i] for i in order_]
            if False:
                for i in range(0, 8, 2):
                    ga, gb = glist[i](), glist[i + 1]()
                    alive = [True, True]
                    gens = [ga, gb]
                    k = 0
                    while any(alive):
                        j = k % 2
                        k += 1
                        if not alive[j]:
                            continue
                        try:
                            yield next(gens[j])
                        except StopIteration:
                            alive[j] = False
            else:
                for g in glist:
                    yield from g()

        def out_block(lhs_fn, ogreads, n, x_src, y_dst):
            ob = PS_CFG["OUT"][nxt("op", len(PS_CFG["OUT"]))]
            O = PS[ob]
            otok = [("ps", 2 * ob), ("ps", 2 * ob + 1)]
            for hf in range(2):
                for c in range(8):
                    mm(O[0:n, hf * 512:(hf + 1) * 512], lhs_fn(c), w_out_bf[:, c, hf * 512:(hf + 1) * 512],
                       c == 0, c == 7, ogreads + [("wout", c)], [otok[hf]])
            ys_ = nxt("yst", 2)
            ytok = ("yst", ys_)
            dma(yst[ys_][0:n, :], x_src, "yst%d" % ys_, writes=[ytok])
            st = next_stat()
            act(junk[0:n, :], O[0:n, :], AF.Square, otok, ["junk", ("stat", st)], scale=1.0 / 32.0,
                accum=stat[0:n, st:st + 1])
            ts("dve", stat2[0:n, st:st + 1], stat[0:n, st:st + 1], EPS, None, ALU.add, None, [("stat", st)], [("stat2", st)])
            tt("pool", rstd[0:n, st:st + 1], stat2[0:n, st:st + 1], mhalf[0:n, :], ALU.pow,
               [("stat2", st), "mhalf"], [("rstd", st)])
            stt("dve", O[0:n, :], O[0:n, :], rstd[0:n, st:st + 1], gpost_b[0:n, :], ALU.mult, ALU.mult,
                otok + [("rstd", st), "gpost"], otok)
            tt("dve", yst[ys_][0:n, :], O[0:n, :], yst[ys_][0:n, :], ALU.add, otok + [ytok], [ytok])
            dma(y_dst, yst[ys_][0:n, :], "yst%d" % ys_, reads=[ytok])
            out_tokens.append(ytok)

        def out_stage(st_i):
            par = st_i % 2
            ogT = sgog[par]
            og_all = [("sg", par, c) for c in range(8)]
            for tb in range(4):
                r0 = (4 * st_i + tb) * 128
                o0 = (4 * (st_i - 1) + tb) * 128
                out_block(lambda c, tb=tb: ogT[:, c, tb * 128:(tb + 1) * 128], og_all, 128,
                          xh.ap()[r0:r0 + 128, :], y_d.ap()[o0:o0 + 128, :])
                yield 4.5

        def sample_phase():
            norm_transpose(xs_d.ap(), 16, xnTs[:, :, :], "xnTs")
            xt = ["xnTs"]
            rhs_x = lambda k: xnTs[:, k, :]
            allwin = [("win", k, cs, hf) for k in range(8) for cs in range(3) for hf in range(2)]
            fm = []
            for c in range(3):
                fm.append((lambda k, c=c: wcols(k, C_QA + 128 * c), "q", qs_a[:, c, :], "qs_a"))
            for c in range(3):
                fm.append((lambda k, c=c: wcols(k, C_KA + 128 * c), "ka", c, None))
            for p_ in range(3):
                fm.append((lambda k, p_=p_: wcols_pair(k, C_QB, p_), "q", qs_b[:, p_, :], "qs_b"))
            fm.append((lambda k: wcols(k, C_KB), "kb", None, None))
            for c in range(2):
                fm.append((lambda k, c=c: wcols(k, C_QM + 128 * c), "q", qs_m[:, c, :], "qs_m"))
            for c in range(3):
                fm.append((lambda k, c=c: wcols(k, ga_col(c)), "g", c, None))
            for p_ in range(3):
                fm.append((lambda k, p_=p_: wcols_pair(k, C_GB, p_), "g", 3 + p_, None))
            for c in range(2):
                fm.append((lambda k, c=c: wcols(k, C_GM + 128 * c), "g", 6 + c, None))
            bk = next_bank()
            pstok = ("ps", bk)
            pb = bank(bk)
            for i, (lf, kind, a1, a2) in enumerate(fm):
                for k in range(8):
                    mm(pb[:, i * 16:(i + 1) * 16], lf(k), rhs_x(k), k == 0, k == 7, allwin + xt, [pstok])
            for i, (lf, kind, a1, a2) in enumerate(fm):
                ps = pb[:, i * 16:(i + 1) * 16]
                if kind == "q":
                    cp("dve", a1, ps, [pstok], [a2])
                elif kind == "ka":
                    cp("dve", kT_a[:, a1, 512:528], ps, [pstok], [("ka", 4, a1)])
                elif kind == "kb":
                    cp("dve", kT_b[:, 128:144], ps, [pstok], [("kb", 1)])
                else:
                    act(tbs[:, a1, :], ps, AF.Tanh, [pstok], ["tbs"], scale=0.5)
                    stt("dve", sgs[:, a1, :], tbs[:, a1, :], 1.0, ps, ALU.add, ALU.mult, ["tbs", pstok], ["sgs"])
            ob = PS_CFG["OUT"][nxt("op", len(PS_CFG["OUT"]))]
            O = PS[ob]
            otok = [("ps", 2 * ob), ("ps", 2 * ob + 1)]
            for k in range(8):
                mm(O[0:16, 0:512], xnTs[:, k, :], w_in_bf[:, k, 384:896], k == 0, k == 7, allwin + xt, [otok[0]])
            for k in range(8):
                mm(O[0:16, 512:768], xnTs[:, k, :], w_in_bf[:, k, 896:1152], k == 0, k == 7, allwin + xt, [otok[1]])
            for k in range(8):
                mm(O[0:16, 768:896], xnTs[:, k, :], w_in_bf[:, k, 1920:2048], k == 0, k == 7, allwin + xt, [otok[1]])
            for k in range(8):
                mm(O[0:16, 896:1024], xnTs[:, k, :], w_in_bf[:, k, 1152:1280], k == 0, k == 7, allwin + xt, [otok[1]])
            ys_ = nxt("yst", 2)
            ytok = ("yst", ys_)
            cp("dve", yst[ys_][0:16, :], O[0:16, :], otok, [ytok])
            key = "yst%d" % ys_
            dma(aks_d.ap(), yst[ys_][0:16, 0:384], key, reads=[ytok])
            dma(avs_d.ap(), yst[ys_][0:16, 384:768], key, reads=[ytok])
            dma(bks_d.ap(), yst[ys_][0:16, 768:896], key, reads=[ytok])
            dma(bvs_d.ap(), yst[ys_][0:16, 896:1024], key, reads=[ytok])
            out_tokens.append(ytok)
            pa = O[0:16, 384:768].rearrange("p (c e d) -> p c e d", c=3, e=2, d=64)
            va = vaug_a[0:16, 4]
            cp("dve", va[:, :, 0:64], pa[:, :, 0, :], otok, [("va", 4)])
            cp("dve", va[:, :, 128:192], pa[:, :, 1, :], otok, [("va", 4)])
            cp("dve", vaug_b[0:16, 1, 0:64], O[0:16, 896:960], otok, [("vb", 1)])
            cp("dve", vaug_b[0:16, 1, 128:192], O[0:16, 960:1024], otok, [("vb", 1)])
            for blk in range(4):
                s = xin_slot()
                xk = "xin%d" % s
                dma(xin[s][:, 0:384], cak.ap()[blk * 128:(blk + 1) * 128, :], xk, writes=[("xin", s)])
                dma(xin[s][:, 384:768], cav.ap()[blk * 128:(blk + 1) * 128, :], xk, writes=[("xin", s)])
                cp("act", kcast[:, 0:384], xin[s][:, 0:384], [("xin", s)], ["kcast"])
                bk = next_bank()
                pv_ = bank_bf(bk)
                for c in range(3):
                    tr(pv_[:, c * 128:(c + 1) * 128], kcast[:, c * 128:(c + 1) * 128], ident[:], ["kcast", "ident"], [("ps", bk)])
                for c in range(3):
                    cp("dve", kT_a[:, c, blk * 128:(blk + 1) * 128], pv_[:, c * 128:(c + 1) * 128], [("ps", bk)], [("ka", blk, c)])
                xv = xin[s][:, 384:768].rearrange("p (c e d) -> p c e d", c=3, e=2, d=64)
                va = vaug_a[:, blk]
                cp("dve", va[:, :, 0:64], xv[:, :, 0, :], [("xin", s)], [("va", blk)])
                cp("dve", va[:, :, 128:192], xv[:, :, 1, :], [("xin", s)], [("va", blk)])
            s = xin_slot()
            xk = "xin%d" % s
            dma(xin[s][:, 0:128], cbk.ap(), xk, writes=[("xin", s)])
            dma(xin[s][:, 128:256], cbv.ap(), xk, writes=[("xin", s)])
            cp("act", kcast[:, 0:128], xin[s][:, 0:128], [("xin", s)], ["kcast"])
            bk = next_bank()
            pv_ = bank_bf(bk)
            tr(pv_[:, 0:128], kcast[:, 0:128], ident[:], ["kcast", "ident"], [("ps", bk)])
            cp("dve", kT_b[:, 0:128], pv_[:, 0:128], [("ps", bk)], [("kb", 0)])
            cp("dve", vaug_b[:, 0, 0:64], xin[s][:, 128:192], [("xin", s)], [("vb", 0)])
            cp("dve", vaug_b[:, 0, 128:192], xin[s][:, 192:256], [("xin", s)], [("vb", 0)])
            for mb in range(2):
                s = xin_slot()
                xk = "xin%d" % s
                dma(xin[s][:, 0:256], cmk.ap()[mb * 128:(mb + 1) * 128, :], xk, writes=[("xin", s)])
                dma(xin[s][:, 256:512], cmv.ap()[mb * 128:(mb + 1) * 128, :], xk, writes=[("xin", s)])
                cp("act", kcast[:, 0:256], xin[s][:, 0:256], [("xin", s)], ["kcast"])
                bk = next_bank()
                pv_ = bank_bf(bk)
                for c in range(2):
                    tr(pv_[:, c * 128:(c + 1) * 128], kcast[:, c * 128:(c + 1) * 128], ident[:], ["kcast", "ident"], [("ps", bk)])
                for c in range(2):
                    cp("dve", kT_m[:, c, mb * 128:(mb + 1) * 128], pv_[:, c * 128:(c + 1) * 128], [("ps", bk)], [("km", c)])
                xv = xin[s][:, 256:512].rearrange("p (c e d) -> p c e d", c=2, e=2, d=64)
                vm = vaug_m[:, mb]
                cp("dve", vm[:, :, 0:64], xv[:, :, 0, :], [("xin", s)], [("vm", mb)])
                cp("dve", vm[:, :, 128:192], xv[:, :, 1, :], [("xin", s)], [("vm", mb)])

            def sample_attn(nblk, npair, rows_of, kE_of, kO_of, q_tile, qtok, ktoks_of, v_of, vtoks_of, ptile, pttokname,
                            bias_of, sink, og_chunk0):
                ncols = npair * 16
                for blk in range(nblk):
                    r = rows_of(blk)
                    sc = PS_CFG["S"][nxt("sc", len(PS_CFG["S"]))]
                    S = PS[sc]
                    stok = [("ps", 2 * sc), ("ps", 2 * sc + 1)]
                    for c in range(npair):
                        mm(S[0:r, c * 16:(c + 1) * 16], kE_of(blk, c), q_tile[0:64, c, :], True, True,
                           ktoks_of(blk, c) + [qtok], [stok[0]])
                        mm(S[0:r, 512 + c * 16:512 + (c + 1) * 16], kO_of(blk, c), q_tile[64:128, c, :], True, True,
                           ktoks_of(blk, c) + [qtok], [stok[1]])
                    Sv = S[:].rearrange("p (b n) -> p b n", b=2)
                    pttok = (pttokname, blk)
                    act(ptile[0:r, blk, :, 0:ncols], Sv[0:r, :, 0:ncols], AF.Exp, stok, [pttok], scale=SCALE)
                    b_ = bias_of(blk)
                    if b_ is not None:
                        pf = ptile[0:r, blk].rearrange("p e n -> p (e n)")
                        tt("pool", pf, pf, b_[0:r], ALU.mult, [pttok, "ebsa", "ebsb"], [pttok])
                ab = PS_CFG["ACC"][nxt("acc", len(PS_CFG["ACC"]))]
                A = PS[ab]
                atok = [("ps", 2 * ab), ("ps", 2 * ab + 1)]
                for c in range(npair):
                    for e_ in range(2):
                        dst = A[:, e_ * 512 + c * 16:e_ * 512 + (c + 1) * 16]
                        for blk in range(nblk):
                            r = rows_of(blk)
                            mm(dst, v_of(blk, c, e_)[0:r], ptile[0:r, blk, e_, c * 16:(c + 1) * 16], blk == 0,
                               (blk == nblk - 1) and sink is None, vtoks_of(blk) + [(pttokname, blk)], [atok[e_]])
                        if sink is not None:
                            hh = sink(c, e_)
                            mm(dst, selrow[0:1, e_, :], esrow[0:1, hh, 0:16], False, True, ["selrow", "esrow"], [atok[e_]])
                Av = A[:].rearrange("p (b n) -> p b n", b=2)
                act(nrms[0:64, 0:ncols], Av[64:128, 0, 0:ncols], AF.Ln, atok, ["iota"])
                act(nrms[64:128, 0:ncols], Av[0:64, 1, 0:ncols], AF.Ln, atok, ["iota"])
                act(nrms[:, 0:ncols], nrms[:, 0:ncols], AF.Exp, ["iota"], ["iota"], scale=-1.0)
                sgv = sgs[:, og_chunk0:og_chunk0 + npair, :].rearrange("p c n -> p (c n)")
                tt("pool", nrms[:, 0:ncols], nrms[:, 0:ncols], sgv, ALU.mult, ["iota", "sgs"], ["iota"])
                ogv = ogs[:, og_chunk0:og_chunk0 + npair, :].rearrange("p c n -> p (c n)")
                stt("dve", ogv[0:64], Av[0:64, 0, 0:ncols], 0.5, nrms[0:64, 0:ncols], ALU.mult, ALU.mult, atok + ["iota"], ["ogs"])
                stt("dve", ogv[64:128], Av[64:128, 1, 0:ncols], 0.5, nrms[64:128, 0:ncols], ALU.mult, ALU.mult, atok + ["iota"], ["ogs"])

            sample_attn(
                5, 3, lambda blk: 16 if blk == 4 else 128,
                lambda blk, c: kT_a[0:64, c, blk * 128:blk * 128 + (16 if blk == 4 else 128)],
                lambda blk, c: kT_a[64:128, c, blk * 128:blk * 128 + (16 if blk == 4 else 128)],
                qs_a, "qs_a", lambda blk, c: [("ka", blk, c)],
                lambda blk, c, e_: vaug_a[:, blk, c, 64 * e_:64 * e_ + 128], lambda blk: [("va", blk)],
                PTs, "pts", lambda blk: expBs_a[:, blk - 3, :] if blk >= 3 else None, None, 0)
            sample_attn(
                2, 3, lambda blk: 16 if blk == 1 else 128,
                lambda blk, c: kT_b[0:64, blk * 128:blk * 128 + (16 if blk == 1 else 128)],
                lambda blk, c: kT_b[64:128, blk * 128:blk * 128 + (16 if blk == 1 else 128)],
                qs_b, "qs_b", lambda blk, c: [("kb", blk)],
                lambda blk, c, e_: vaug_b[:, blk, 64 * e_:64 * e_ + 128], lambda blk: [("vb", blk)],
                PTsb, "ptsb", lambda blk: expBs_b[:, blk, :], lambda c, e_: c + 3 * e_, 3)
            sample_attn(
                2, 2, lambda blk: 128,
                lambda blk, c: kT_m[0:64, c, blk * 128:(blk + 1) * 128],
                lambda blk, c: kT_m[64:128, c, blk * 128:(blk + 1) * 128],
                qs_m, "qs_m", lambda blk, c: [("km", c)],
                lambda blk, c, e_: vaug_m[:, blk, c, 64 * e_:64 * e_ + 128], lambda blk: [("vm", blk)],
                PTsm, "ptsm", lambda blk: None, None, 6)
            out_block(lambda c: ogs[:, c, :], ["ogs"], 16, xs_d.ap(), ys_d.ap())

        def run(gen, tag):
            P.tag = tag
            for _ in gen:
                pass

        def chain(*parts):
            for tag, g in parts:
                for c in g:
                    yield tag, c

        def interleave(streams):
            n = len(streams)
            done = [0.0] * n
            alive = [True] * n
            while any(alive):
                i = min((j for j in range(n) if alive[j]), key=lambda j: done[j] / streams[j][0])
                try:
                    tag, c = next(streams[i][1])
                    done[i] += c
                except StopIteration:
                    alive[i] = False
                    continue
            return

        def tagged(tag, g):
            it = iter(g)
            while True:
                P.tag = tag
                try:
                    c = next(it)
                except StopIteration:
                    return
                yield tag, c

        P.tag = "w"
        P.boost = 0.0
        run(project_stage(1, xonly=True), "w")
        P.boost = 0.0
        P.tag = "w"
        load_w_in(0)
        load_w_in(1)
        load_w_in(2)
        load_w_mem()
        run(project_stage(1, xdone=True), "proj1")
        run(project_stage(0), "proj0")
        P.tag = "mem"
        mem_kv()
        P.tag = "wout"
        load_w_out()
        ATT, PRJ, OUTC = 30.0, 62.0, 18.0
        OVL = False
        if not OVL:
            for st_i in range(1, 5):
                if st_i > 1:
                    run((c for _, c in tagged("proj%d" % st_i, project_stage(st_i))), "proj%d" % st_i)
                run((c for _, c in tagged("attn%d" % st_i, attention_stage(st_i))), "attn%d" % st_i)
                if st_i < 4:
                    run((c for _, c in tagged("out%d" % st_i, out_stage(st_i))), "out%d" % st_i)
        for st_i in (range(1, 5) if OVL else []):
            fill = []
            tot = 0.0
            if st_i >= 2:
                fill.append(tagged("out%d" % (st_i - 1), out_stage(st_i - 1)))
                tot += OUTC
            if st_i <= 3:
                fill.append(tagged("proj%d" % (st_i + 1), project_stage(st_i + 1)))
                tot += PRJ

            def fill_gen(parts=fill):
                for g in parts:
                    for x in g:
                        yield x
            interleave([(ATT, tagged("attn%d" % st_i, attention_stage(st_i))), (max(tot, 1.0), fill_gen())])
        run(out_stage(4), "out4")
        P.tag = "sample"
        mset("pool", vaug_a[:, 0:5], 1.0, [("va", s) for s in range(5)])
        mset("pool", vaug_b[:, 0:2], 1.0, [("vb", s) for s in range(2)])
        sample_phase()
        P.add("sp", lambda e: e.nop(), writes=list(set(out_tokens)), est=0.05)
        P.schedule(reorder=True)
        P.emit(sems, dsems, block)
    return nc


_CACHE = {}


def _host_tables(rel_bias_a, sink_b):
    rel = np.asarray(rel_bias_a[0], np.float32)
    p = np.arange(128)[:, None]
    t = np.arange(256)[None, :]
    idx = np.clip(t - p, -128, 128) + 128
    biasA = np.ascontiguousarray(rel[:, idx].transpose(1, 0, 2)).reshape(128, 6 * 256)
    cA = np.ascontiguousarray(np.broadcast_to(rel[:, 256][None, :], (128, 6)))
    i = np.arange(16)[None, :]
    idx0 = np.clip(128 + i - p, -128, 128) + 128
    idx1 = np.clip(i - np.minimum(p, 15), -128, 128) + 128
    bs = np.zeros((128, 2, 2, 3, 16), np.float32)
    for e in range(2):
        for c in range(3):
            h = 2 * c + e
            bs[:, 0, e, c, :] = rel[h][idx0]
            bs[:, 1, e, c, :] = rel[h][idx1]
    biasAs = bs.reshape(128, 192)
    distB = np.abs(t - p).astype(np.float32)
    dbs = np.zeros((128, 2, 16), np.float32)
    dbs[:, 0, :] = np.abs(128 + i - p)
    dbs[:, 1, :] = np.abs(i - np.minimum(p, 15))
    distBs = dbs.reshape(128, 32)
    sink_r = np.ascontiguousarray(np.broadcast_to(np.asarray(sink_b[0], np.float32)[None, :], (128, 6)))
    return biasA, cA, biasAs, distB, distBs, sink_r


def kernel(x_prompt, x_sample, cache_a_k, cache_a_v, cache_b_k, cache_b_v, cache_mem_k, cache_mem_v,
           mem_prompt, g_pre, w_in, rel_bias_a, sink_b, g_mem, w_mem_kv, w_out, g_post):
    f = lambda a: np.ascontiguousarray(np.asarray(a, dtype=np.float32))
    x_prompt = f(x_prompt); x_sample = f(x_sample)
    if "nc" not in _CACHE:
        _CACHE["nc"] = build_program()
    nc = _CACHE["nc"]
    biasA, cA, biasAs, distB, distBs, sink_r = _host_tables(f(rel_bias_a), f(sink_b))
    shared = {
        "w_in": f(w_in)[0], "w_mem": f(w_mem_kv)[0], "w_out": f(w_out)[0],
        "gpre_t": np.ascontiguousarray(f(g_pre)[0].reshape(8, 128).T),
        "gmem_t": np.ascontiguousarray(f(g_mem)[0].reshape(8, 128).T),
        "gpost_b": np.ascontiguousarray(np.broadcast_to(f(g_post)[0][None, :], (128, 1024))),
        "biasA": biasA, "cA": cA, "biasAs": biasAs, "distB": distB, "distBs": distBs, "sink_r": sink_r,
    }
    in_maps = []
    for c in range(NCORES):
        b, q = divmod(c, 4)
        t0 = q * TOK
        xhalo = np.zeros((HALO + TOK, 1024), np.float32)
        if q > 0:
            xhalo[:] = x_prompt[b, t0 - HALO:t0 + TOK]
        else:
            xhalo[HALO:] = x_prompt[b, 0:TOK]
        m = dict(shared)
        m.update({
            "xh": xhalo,
            "hm": np.full((128, 1), 1.0 if q > 0 else 0.0, np.float32),
            "xs": x_sample[c],
            "cak": f(cache_a_k)[0, c].reshape(512, 384), "cav": f(cache_a_v)[0, c].reshape(512, 384),
            "cbk": f(cache_b_k)[0, c].reshape(128, 128), "cbv": f(cache_b_v)[0, c].reshape(128, 128),
            "cmk": f(cache_mem_k)[0, c].reshape(256, 256), "cmv": f(cache_mem_v)[0, c].reshape(256, 256),
            "mem": f(mem_prompt)[b],
        })
        in_maps.append(m)
    res = run_bass_kernel_spmd(nc, in_maps, core_ids=list(range(NCORES)))
    R = res.results
    yp = np.stack([np.concatenate([R[b * 4 + q]["y"] for q in range(4)], axis=0) for b in range(2)])
    ys = np.stack([R[c]["ys"] for c in range(8)])
    sak = np.stack([R[b * 4 + 3]["sak"].reshape(512, 6, 64) for b in range(2)])[None]
    sav = np.stack([R[b * 4 + 3]["sav"].reshape(512, 6, 64) for b in range(2)])[None]
    sbk = np.stack([R[b * 4 + 3]["sbk"].reshape(128, 2, 64) for b in range(2)])[None]
    sbv = np.stack([R[b * 4 + 3]["sbv"].reshape(128, 2, 64) for b in range(2)])[None]
    smk = np.stack([R[b * 4]["smk"].reshape(256, 4, 64) for b in range(2)])[None]
    smv = np.stack([R[b * 4]["smv"].reshape(256, 4, 64) for b in range(2)])[None]
    aks = np.stack([R[c]["aks"].reshape(16, 6, 64) for c in range(8)])[None]
    avs = np.stack([R[c]["avs"].reshape(16, 6, 64) for c in range(8)])[None]
    bks = np.stack([R[c]["bks"].reshape(16, 2, 64) for c in range(8)])[None]
    bvs = np.stack([R[c]["bvs"].reshape(16, 2, 64) for c in range(8)])[None]
    return tuple(np.ascontiguousarray(a.astype(np.float32)) for a in
                 (yp, ys, sak, sav, sbk, sbv, smk, smv, aks, avs, bks, bvs))
```

```python
import os
import numpy as np
import concourse.bass as bass
import concourse.mybir as mybir
from concourse.bass_utils import run_bass_kernel_spmd
from contextlib import ExitStack

F32 = mybir.dt.float32
BF16 = mybir.dt.bfloat16
ALU = mybir.AluOpType
AF = mybir.ActivationFunctionType

ENGS = ("pe", "act", "dve", "pool", "sp")
NCORES = 8
TOK = 2048
HALO = 512
SCALE = 0.125
EPS = 1e-6
SLOPES = [2.0 ** (-8.0 * (h + 1) / 6.0) for h in range(6)]
SCHED_WINDOW = 0.5
XLAT = 0.3
PECAL = False
SCHED_MODE = 'nodelay'
ND_MARGIN = 1.0
_CFG = '0'
PS_CFG = {
    '0': dict(S=[0, 1], ACC=[2, 3], PROJ=[0, 1, 2, 3], OUT=[2, 3]),
    'I': dict(S=[0, 1], ACC=[2], PROJ=[6, 7], OUT=[3]),
    'J': dict(S=[0], ACC=[1, 2], PROJ=[6, 7], OUT=[3]),
    'Z': dict(S=[0], ACC=[2, 3], PROJ=[2, 3], OUT=[1]),
    'W': dict(S=[0, 1], ACC=[2], PROJ=[6, 7], OUT=[3]),
    'V': dict(S=[0], ACC=[1, 2], PROJ=[6, 7], OUT=[3]),
}[_CFG]


class Ins:
    __slots__ = ("eng", "fn", "dma_key", "deps", "needs_inc", "milestone", "dma_target", "est", "cls", "pos",
                 "tag", "t0", "t1", "boost")

    def __init__(self, eng, fn, dma_key, est, cls):
        self.eng = eng
        self.fn = fn
        self.dma_key = dma_key
        self.deps = {}
        self.needs_inc = False
        self.milestone = None
        self.dma_target = None
        self.est = est
        self.cls = cls
        self.pos = None


class Prog:
    def __init__(self):
        self.order = []
        self.tok_w = {}
        self.tok_r = {}
        self.lists = None

    def add(self, eng, fn, reads=(), writes=(), dma_key=None, est=0.1, cls=None):
        ins = Ins(eng, fn, dma_key, est, cls)
        ins.tag = getattr(self, "tag", None)
        ins.boost = getattr(self, "boost", 0.0)
        for t in reads:
            w = self.tok_w.get(t)
            if w is not None:
                ins.deps[w] = "raw"
            if isinstance(t, tuple) and t[0] == "ps":
                for r in self.tok_r.get(t, ()):
                    if r.eng != eng and r not in ins.deps:
                        ins.deps[r] = "rar"
        for t in writes:
            w = self.tok_w.get(t)
            if w is not None and w not in ins.deps:
                ins.deps[w] = "waw"
            for r in self.tok_r.get(t, ()):
                if r is not ins and r not in ins.deps:
                    ins.deps[r] = "war"
        for t in writes:
            self.tok_w[t] = ins
            self.tok_r[t] = []
        for t in reads:
            if t in writes:
                continue
            self.tok_r.setdefault(t, []).append(ins)
        self.order.append(ins)
        return ins

    @staticmethod
    def _need_wait(cons, prod, kind):
        if prod.dma_key is not None:
            return True
        if prod.eng != cons.eng:
            return True
        if cons.dma_key is not None:
            return True
        if prod.eng == "pe":
            return False
        return kind in ("raw", "waw", "war")

    def schedule(self, reorder=True):
        order = self.order
        if not reorder:
            self.lists = {e: [i for i in order if i.eng == e] for e in ENGS}
            return
        succs = {i: [] for i in order}
        for i in order:
            for d in i.deps:
                succs[d].append(i)
        prio = {}
        for i in reversed(order):
            m = 0.0
            for sc in succs[i]:
                if prio[sc] > m:
                    m = prio[sc]
            prio[i] = i.est + m
        for i in order:
            prio[i] += i.boost
        noise = getattr(self, "noise", 0.0)
        if noise > 0:
            import random
            rnd = random.Random(getattr(self, "seed", 0))
            for i in order:
                prio[i] *= 1.0 + noise * (2 * rnd.random() - 1)
        window = getattr(self, "window", SCHED_WINDOW)
        ndep = {i: len(i.deps) for i in order}
        ready = {e: [] for e in ENGS}
        for i in order:
            if ndep[i] == 0:
                ready[i.eng].append(i)
        free = {e: 0.0 for e in ENGS}
        done = {}
        lists = {e: [] for e in ENGS}
        act_cls = None
        dma_free = 0.0
        n = len(order)
        k = 0
        while k < n:
            cands = []
            for e in ENGS:
                fe = free[e]
                for i in ready[e]:
                    st = fe
                    for d in i.deps:
                        t = done[d] + (XLAT if (d.eng != e or d.dma_key is not None) else 0.0)
                        if t > st:
                            st = t
                    if e == "act" and i.cls is not None and i.cls != act_cls:
                        st += 1.3
                    cands.append((st, -prio[i], len(cands), i))
            if SCHED_MODE == "nodelay":
                chosen = []
                per = {}
                for c in cands:
                    per.setdefault(c[3].eng, []).append(c)
                for e, cl in per.items():
                    cl.sort(key=lambda c: (c[0], c[1]))
                    x = cl[0]
                    changed = True
                    while changed:
                        changed = False
                        dur = 0.07 if x[3].dma_key is not None else x[3].est
                        for y in cl:
                            if y is x:
                                continue
                            if y[1] < x[1] * ND_MARGIN and y[0] < x[0] + dur - 1e-9 and y[0] >= x[0]:
                                x = y
                                changed = True
                                break
                    chosen.append(x)
                mn = min(c[0] for c in chosen)
                win = [c for c in chosen if c[0] <= mn + 1e-9]
                st, _, _, i = min(win, key=lambda c: (c[1], c[0]))
            else:
                mn = min(c[0] for c in cands)
                win = [c for c in cands if c[0] <= mn + window]
                _, _, _, i = min(win, key=lambda c: (c[1], c[0]))
                st = [c[0] for c in win if c[3] is i][0]
            e = i.eng
            ready[e].remove(i)
            lists[e].append(i)
            if i.dma_key is not None:
                free[e] = st + 0.07
                x0 = max(st + 1.0, dma_free)
                dma_free = x0 + i.est
                done[i] = dma_free + 1.0
            else:
                if e == "act" and i.cls is not None:
                    act_cls = i.cls
                done[i] = st + i.est
                free[e] = done[i]
            i.t0 = st
            i.t1 = done[i]
            for sc in succs[i]:
                ndep[sc] -= 1
                if ndep[sc] == 0:
                    ready[sc.eng].append(sc)
            k += 1
        self.lists = lists
        self.makespan = max(done.values())

    def emit(self, sems, dma_sems, block):
        lists = self.lists
        for e in ENGS:
            cnt = {}
            for p_, ins in enumerate(lists[e]):
                ins.pos = p_
                if ins.dma_key is not None:
                    c = cnt.get(ins.dma_key, 0) + 1
                    cnt[ins.dma_key] = c
                    ins.dma_target = 16 * c
        waits = {}
        for e in ENGS:
            for ins in lists[e]:
                latest = {}
                dmaw = {}
                for p, kind in ins.deps.items():
                    if not self._need_wait(ins, p, kind):
                        continue
                    if p.dma_key is not None:
                        if p.dma_target > dmaw.get(p.dma_key, 0):
                            dmaw[p.dma_key] = p.dma_target
                    else:
                        q = latest.get(p.eng)
                        if q is None or p.pos > q.pos:
                            latest[p.eng] = p
                for p in latest.values():
                    p.needs_inc = True
                waits[ins] = (latest, dmaw)
        for e in ENGS:
            c = 0
            for ins in lists[e]:
                if ins.dma_key is None and ins.needs_inc:
                    c += 1
                    ins.milestone = c
        if False:
            for e in ENGS:
                ms = [i.milestone for i in lists[e] if i.milestone]
                print("ENG", e, "n_ins", len(lists[e]), "max_milestone", max(ms) if ms else 0)
            print("est makespan us", getattr(self, "makespan", None))

        def run(engname, eng):
            waited = {}
            for ins in lists[engname]:
                latest, dmaw = waits[ins]
                for pe_, p in latest.items():
                    s_ = ("e", pe_)
                    if waited.get(s_, 0) >= p.milestone:
                        continue
                    waited[s_] = p.milestone
                    eng.wait_ge(sems[pe_], p.milestone)
                for key, v in dmaw.items():
                    s_ = ("d", key)
                    if waited.get(s_, 0) >= v:
                        continue
                    waited[s_] = v
                    eng.wait_ge(dma_sems[key], v)
                h = ins.fn(eng)
                if ins.dma_key is not None:
                    h.then_inc(dma_sems[ins.dma_key], 16)
                elif ins.needs_inc:
                    h.then_inc(sems[engname], 1)

        @block.tensor
        def _(eng):
            run("pe", eng)

        @block.scalar
        def _(eng):
            run("act", eng)

        @block.vector
        def _(eng):
            run("dve", eng)

        @block.gpsimd
        def _(eng):
            run("pool", eng)

        @block.sync
        def _(eng):
            run("sp", eng)


def build_program():
    nc = bass.Bass("TRN2", target_bir_lowering=False)

    def din(name, shape):
        return nc.dram_tensor(name, list(shape), F32, kind="ExternalInput")

    def dout(name, shape):
        return nc.dram_tensor(name, list(shape), F32, kind="ExternalOutput")

    xh = din("xh", [HALO + TOK, 1024])
    hm_d = din("hm", [128, 1])
    xs_d = din("xs", [16, 1024])
    cak = din("cak", [512, 384]); cav = din("cav", [512, 384])
    cbk = din("cbk", [128, 128]); cbv = din("cbv", [128, 128])
    cmk = din("cmk", [256, 256]); cmv = din("cmv", [256, 256])
    mem_d = din("mem", [256, 1024])
    w_in = din("w_in", [1024, 3072])
    w_mem = din("w_mem", [1024, 512])
    w_out = din("w_out", [1024, 1024])
    gpre_d = din("gpre_t", [128, 8])
    gmem_d = din("gmem_t", [128, 8])
    gpost_d = din("gpost_b", [128, 1024])
    biasA_d = din("biasA", [128, 6 * 256])
    cA_d = din("cA", [128, 6])
    biasAs_d = din("biasAs", [128, 2 * 96])
    distB_d = din("distB", [128, 256])
    distBs_d = din("distBs", [128, 2 * 16])
    sink_d = din("sink_r", [128, 6])

    y_d = dout("y", [TOK, 1024])
    ys_d = dout("ys", [16, 1024])
    sak_d = dout("sak", [512, 384]); sav_d = dout("sav", [512, 384])
    sbk_d = dout("sbk", [128, 128]); sbv_d = dout("sbv", [128, 128])
    smk_d = dout("smk", [256, 256]); smv_d = dout("smv", [256, 256])
    aks_d = dout("aks", [16, 384]); avs_d = dout("avs", [16, 384])
    bks_d = dout("bks", [16, 128]); bvs_d = dout("bvs", [16, 128])

    es = ExitStack()
    with es:
        def sb(name, shape, dt):
            return es.enter_context(nc.sbuf_tensor("sb_" + name, list(shape), dt))

        w_in_bf = sb("w_in_bf", [128, 8, 3072], BF16)
        w_out_bf = sb("w_out_bf", [128, 8, 1024], BF16)
        xin = [sb("xin%d" % i, [128, 1024], F32) for i in range(3)]
        yst = [sb("yst%d" % i, [128, 1024], F32) for i in range(2)]
        xn_bf = [sb("xn_bf%d" % i, [128, 1024], BF16) for i in range(2)]
        junk = sb("junk", [128, 1024], BF16)
        NSLOT = 16
        xnT2 = [sb("xnT%d" % i, [128, 8, 512], BF16) for i in range(2)]
        kT_a = sb("kT_a", [128, 3, NSLOT * 128], BF16)
        vaug_a = sb("vaug_a", [128, NSLOT, 3, 192], BF16)
        kT_b = sb("kT_b", [128, NSLOT * 128], BF16)
        vaug_b = sb("vaug_b", [128, NSLOT, 192], BF16)
        NQ = 1
        qT_a2 = [sb("qT_a%d" % i, [128, 3, 512], BF16) for i in range(NQ)] * (2 // NQ)
        qT_b2 = [sb("qT_b%d" % i, [128, 3, 512], BF16) for i in range(NQ)] * (2 // NQ)
        qT_m2 = [sb("qT_m%d" % i, [128, 2, 512], BF16) for i in range(NQ)] * (2 // NQ)
        sgog = [sb("sgog%d" % i, [128, 8, 512], BF16) for i in range(2)]
        tbuf = [sb("tbuf%d" % i, [128, 512], BF16) for i in range(2)]
        NPT = 6
        PT = [sb("PT%d" % i, [128, 2, 512], BF16) for i in range(NPT)]
        nrm = [sb("nrm%d" % i, [128, 512], F32) for i in range(1)]
        nrmT = [sb("nrmT%d" % i, [128, 512], F32) for i in range(1)]
        expBB_a = sb("expBB_a", [128, 6, 256], BF16)
        expBB_b = sb("expBB_b", [128, 6, 256], BF16)
        gpost_b = sb("gpost_b", [128, 1024], F32)
        kT_m = sb("kT_m", [128, 2, 256], BF16)
        vaug_m = sb("vaug_m", [128, 2, 2, 192], BF16)
        ident = sb("ident", [128, 128], BF16)
        iota_t = sb("iota_t", [128, 128], F32)
        pid = sb("pid", [128, 1], F32)
        stat = sb("stat", [128, 48], F32)
        stat2 = sb("stat2", [128, 48], F32)
        rstd = sb("rstd", [128, 48], F32)
        mhalf = sb("mhalf", [128, 1], F32)
        onec = sb("onec", [128, 1], F32)
        gpre_t = sb("gpre_t", [128, 8], F32)
        gmem_t = sb("gmem_t", [128, 8], F32)
        hm = sb("hm", [128, 1], F32)
        esink = sb("esink", [128, 6], F32)
        esink2 = sb("esink2", [128, 3], F32)
        sink_t = sb("sink_t", [128, 6], F32)
        cA = sb("cA", [128, 6], F32)
        esrow = sb("esrow", [1, 6, 16], BF16)
        selrow = sb("selrow", [1, 2, 128], BF16)
        xnTs = sb("xnTs", [128, 8, 16], BF16)
        qs_a = sb("qs_a", [128, 3, 16], BF16)
        qs_b = sb("qs_b", [128, 3, 16], BF16)
        qs_m = sb("qs_m", [128, 2, 16], BF16)
        sgs = sb("sgs", [128, 8, 16], BF16)
        tbs = sb("tbs", [128, 8, 16], BF16)
        ogs = sb("ogs", [128, 8, 16], BF16)
        PTs = sb("PTs", [128, 5, 2, 48], BF16)
        PTsb = sb("PTsb", [128, 2, 2, 48], BF16)
        PTsm = sb("PTsm", [128, 2, 2, 32], BF16)
        expBs_a = sb("expBs_a", [128, 2, 96], BF16)
        expBs_b = sb("expBs_b", [128, 2, 96], BF16)
        distBs = sb("distBs", [128, 32], F32)
        nrms = iota_t
        kcast = sb("kcast", [128, 384], BF16)

        PS = [es.enter_context(nc.psum_tensor("PS%d" % i, [128, 1024], F32)) for i in range(4)]

        def bank(i):
            return PS[i // 2][:, (i % 2) * 512:(i % 2) * 512 + 512]

        def bank_bf(i):
            return PS[i // 2][:].bitcast(BF16)[:, (i % 2) * 1024:(i % 2) * 1024 + 1024]

        sems = {e: es.enter_context(nc.semaphore("s_" + e)) for e in ENGS}
        dkeys = ["xin0", "xin1", "xin2", "yst0", "yst1", "c0", "c1", "c2", "c3", "c4", "c5",
                 "c6", "c7", "c8", "c9", "o0", "o1", "o2", "o3", "o4", "o5", "o6", "o7"]
        dsems = {k: es.enter_context(nc.semaphore("d_" + k)) for k in dkeys}
        block = es.enter_context(nc.Block())
        P = Prog()
        out_tokens = []

        def fsz(ap):
            n = 1
            for d in ap.shape[1:]:
                n *= d
            return n

        def dma(out, in_, key, reads=(), writes=(), eng="sp"):
            nbytes = out.shape[0] * fsz(out) * 4
            return P.add(eng, lambda e, o=out, i=in_: e.dma_start(out=o, in_=i),
                         reads=reads, writes=writes, dma_key=key, est=nbytes / 170e3)

        def mm(out, lhsT, rhs, start, stop, reads, writes, skip=False):
            return P.add("pe", lambda e, o=out, l=lhsT, r=rhs, s=start, t=stop, k=skip:
                         e.matmul(o, lhsT=l, rhs=r, start=s, stop=t, skip_group_check=k), reads=reads, writes=writes,
                         est=(max(64, fsz(rhs)) / 2100.0 + 0.06) if PECAL else (max(64, fsz(rhs)) / 1960.0 + 0.02))

        def tr(out, in_, idn, reads, writes):
            return P.add("pe", lambda e, o=out, i=in_, d=idn: e.transpose(o, i, d),
                         reads=reads, writes=writes, est=0.12)

        def act(out, in_, func, reads, writes, bias=None, scale=None, accum=None):
            kw = {}
            if bias is not None:
                kw["bias"] = bias
            if scale is not None:
                kw["scale"] = scale
            if accum is not None:
                kw["accum_out"] = accum
            cls = "tanh" if func == AF.Tanh else ("ln" if func == AF.Ln else None)
            est = (fsz(in_) + (230 if PECAL else 300)) / 1400.0 + (0.1 if accum is not None else 0.0) + \
                (0.09 if not isinstance(scale, (int, float, type(None))) else 0.0)
            return P.add("act", lambda e, o=out, i=in_, f=func, k=kw: e.activation(o, i, f, **k),
                         reads=reads, writes=writes, est=est, cls=cls)

        def vest(eng, n, two_src=False):
            if eng == "pool":
                return 0.12 + n * 0.002
            return (n + 150) / 960.0

        def ts(eng, out, in0, s1, s2, op0, op1, reads, writes):
            est = vest(eng, fsz(in0))
            if op1 is None:
                return P.add(eng, lambda e, o=out, i=in0, a=s1, p0=op0: e.tensor_scalar(o, i, a, None, p0),
                             reads=reads, writes=writes, est=est)
            return P.add(eng, lambda e, o=out, i=in0, a=s1, b=s2, p0=op0, p1=op1:
                         e.tensor_scalar(o, i, a, b, p0, p1), reads=reads, writes=writes, est=est)

        def tt(eng, out, in0, in1, op, reads, writes):
            est = 0.5 if op == ALU.pow else vest(eng, fsz(in0))
            return P.add(eng, lambda e, o=out, a=in0, b=in1, p=op: e.tensor_tensor(o, a, b, p),
                         reads=reads, writes=writes, est=est)

        def stt(eng, out, in0, scalar, in1, op0, op1, reads, writes):
            return P.add(eng, lambda e, o=out, a=in0, s=scalar, b=in1, p0=op0, p1=op1:
                         e.scalar_tensor_tensor(o, a, s, b, p0, p1), reads=reads, writes=writes,
                         est=vest(eng, fsz(in0)))

        def cp(eng, out, in_, reads, writes):
            if eng == "act":
                return act(out, in_, AF.Copy, reads, writes)
            p0 = in_.base_partition()
            sc = onec[p0:p0 + in_.shape[0], 0:1]
            return P.add(eng, lambda e, o=out, i=in_, c=sc: e.tensor_scalar(o, i, c, None, ALU.mult),
                         reads=list(reads) + ["onec"], writes=writes, est=vest(eng, fsz(in_)) + 0.06)

        def mset(eng, ap, val, writes):
            return P.add(eng, lambda e, a=ap, v=val: e.memset(a, v), writes=writes, est=0.15 + fsz(ap) * 0.0008)

        rr = {"xin": 0, "bank": 0, "yst": 0, "xn": 0, "stat": 0, "tb": 0, "pt": 0, "nrm": 0,
              "sc": 0, "acc": 0, "nrmT": 0, "op": 0, "evac": 0, "cast": 0, "ck": 0, "ok": 0}

        def nxt(name, n):
            v = rr[name]
            rr[name] = (v + 1) % n
            return v

        def next_bank():
            return PS_CFG["PROJ"][nxt("bank", len(PS_CFG["PROJ"]))]

        def next_stat():
            v = rr["stat"]
            rr["stat"] += 1
            assert v < 48
            return v

        P.add("pool", lambda e: e.iota(iota_t[:], [[1, 128]], base=0, channel_multiplier=0,
                                       allow_small_or_imprecise_dtypes=True), writes=["iota"], est=0.3)
        P.add("pool", lambda e: e.iota(pid[:], [[0, 1]], base=0, channel_multiplier=1,
                                       allow_small_or_imprecise_dtypes=True), writes=["pid"], est=0.3)
        ts("dve", ident[:], iota_t[:], pid[:, 0:1], None, ALU.is_equal, None, ["iota", "pid"], ["ident"])
        mset("pool", mhalf[:], -0.5, ["mhalf"])
        mset("pool", onec[:], 1.0, ["onec"])
        mset("pool", selrow[0:1, 0, 0:64], 0.0, ["selrow"])
        mset("pool", selrow[0:1, 0, 64:128], 1.0, ["selrow"])
        mset("pool", selrow[0:1, 1, 0:64], 1.0, ["selrow"])
        mset("pool", selrow[0:1, 1, 64:128], 0.0, ["selrow"])
        mset("pool", vaug_a[:], 1.0, [("va", s) for s in range(NSLOT)])
        mset("pool", vaug_b[:], 1.0, [("vb", s) for s in range(NSLOT)])
        mset("pool", vaug_m[:], 1.0, [("vm", 0), ("vm", 1)])

        dma(gpre_t[:], gpre_d.ap(), "c0", writes=["gpre"])
        dma(gmem_t[:], gmem_d.ap(), "c1", writes=["gmem"])
        dma(hm[:], hm_d.ap(), "c2", writes=["hm"])
        dma(sink_t[:], sink_d.ap(), "c3", writes=["sink_t"])
        dma(cA[:], cA_d.ap(), "c4", writes=["cA"])
        dma(gpost_b[:], gpost_d.ap(), "c5", writes=["gpost"])
        dma(distBs[:], distBs_d.ap(), "c6", writes=["distBs"])
        for s in range(12, 16):
            ts("dve", vaug_a[:, s, :, 64:128], vaug_a[:, s, :, 64:128], hm[:, 0:1], None, ALU.mult, None,
               ["hm", ("va", s)], [("va", s)])
            ts("dve", vaug_b[:, s, 64:128], vaug_b[:, s, 64:128], hm[:, 0:1], None, ALU.mult, None,
               ["hm", ("vb", s)], [("vb", s)])
        act(esink[:], sink_t[:], AF.Exp, ["sink_t"], ["esink"])
        cp("dve", esink2[0:64, 0:3], esink[0:64, 0:3], ["esink"], ["esink2"])
        cp("dve", esink2[64:128, 0:3], esink[64:128, 3:6], ["esink"], ["esink2"])
        for h in range(6):
            cp("dve", esrow[0:1, h, :], esink[0:1, h:h + 1].to_broadcast([1, 16]), ["esink"], ["esrow"])
        ts("dve", cA[:], cA[:], -1.0, None, ALU.mult, None, ["cA"], ["ncA"])

        def xin_slot():
            return nxt("xin", 3)

        for half in range(2):
            s = xin_slot()
            dma(xin[s][:, 0:768], biasA_d.ap()[:, half * 768:(half + 1) * 768], "xin%d" % s,
                writes=[("xin", s)])
            for hh in range(3):
                h = half * 3 + hh
                act(expBB_a[:, h, :], xin[s][:, hh * 256:(hh + 1) * 256], AF.Exp, [("xin", s), "ncA"],
                    ["ebba"], bias=cA[:, h:h + 1])
        s = xin_slot()
        dma(xin[s][:, 0:256], distB_d.ap(), "xin%d" % s, writes=[("xin", s)])
        dma(xin[s][:, 256:448], biasAs_d.ap(), "xin%d" % s, writes=[("xin", s)])
        for h in range(6):
            act(expBB_b[:, h, :], xin[s][:, 0:256], AF.Exp, [("xin", s)], ["ebbb"], scale=-SLOPES[h])
        for blk in range(2):
            for e_ in range(2):
                for c_ in range(3):
                    ha = 2 * c_ + e_
                    hb = c_ + 3 * e_
                    col = blk * 96 + e_ * 48 + c_ * 16
                    act(expBs_a[:, blk, e_ * 48 + c_ * 16:e_ * 48 + c_ * 16 + 16],
                        xin[s][:, 256 + col:256 + col + 16], AF.Exp, [("xin", s), "ncA"], ["ebsa"],
                        bias=cA[:, ha:ha + 1])
                    act(expBs_b[:, blk, e_ * 48 + c_ * 16:e_ * 48 + c_ * 16 + 16],
                        distBs[:, blk * 16:blk * 16 + 16], AF.Exp, ["distBs"], ["ebsb"], scale=-SLOPES[hb])
        mset("pool", expBB_a[64:128, :, 0:64], 0.0, ["ebba"])
        mset("pool", expBB_b[64:128, :, 0:64], 0.0, ["ebbb"])
        mset("pool", expBB_b[0:64, :, 192:256], 0.0, ["ebbb"])

        wpool = [(xin[0], "xin0", ("xin", 0)), (yst[0], "yst0", ("yst", 0)), (xin[1], "xin1", ("xin", 1)),
                 (yst[1], "yst1", ("yst", 1)), (xin[2], "xin2", ("xin", 2))]
        rr["wp"] = 0

        def wslot():
            return wpool[nxt("wp", 5)]

        def load_w_in(cs):
            for k in range(8):
                wt, wkey, wtok = wslot()
                dma(wt[:], w_in.ap()[k * 128:(k + 1) * 128, cs * 1024:(cs + 1) * 1024], wkey, writes=[wtok])
                pieces = {
                    0: [(0, 1024, 0, False)],
                    1: [(0, 128, 1024, False), (128, 256, 2048, False), (256, 512, 1280, False),
                        (512, 896, 1536, True), (896, 1024, 1920, False)],
                    2: [(0, 128, 1152, False), (128, 512, 2176, True), (512, 1024, 2560, False)],
                }[cs]
                use_dve = nxt("cast", 2) == 0
                for (a0, a1, d0, pm) in pieces:
                    o_ap = w_in_bf[:, k, d0:d0 + (a1 - a0)]
                    i_ap = wt[:, a0:a1]
                    if pm:
                        o_ap = o_ap.rearrange("p (h a d) -> p h a d", h=3, a=2, d=64)
                        i_ap = i_ap.rearrange("p (a h d) -> p h a d", a=2, h=3, d=64)
                    if use_dve:
                        ts("dve", o_ap, i_ap, gpre_t[:, k:k + 1], None, ALU.mult, None, [wtok, "gpre"], [("win", k, cs)])
                    else:
                        act(o_ap, i_ap, AF.Copy, [wtok, "gpre"], [("win", k, cs)], scale=gpre_t[:, k:k + 1])

        def load_w_mem():
            for k in range(8):
                wt, wkey, wtok = wslot()
                dma(wt[:, 0:512], w_mem.ap()[k * 128:(k + 1) * 128, :], wkey, writes=[wtok])
                ts("dve", sgog[0][:, k, :], wt[:, 0:512], gmem_t[:, k:k + 1], None, ALU.mult, None,
                   [wtok, "gmem"], [("sg", 0, k)])

        def load_w_out():
            for c in range(8):
                s = xin_slot()
                if 3 <= c < 6:
                    p_ = c - 3
                    r0 = 384 + 64 * p_
                    r1 = 384 + 64 * (p_ + 3)
                    dma(xin[s][0:64, :], w_out.ap()[r0:r0 + 64, :], "xin%d" % s, writes=[("xin", s)])
                    dma(xin[s][64:128, :], w_out.ap()[r1:r1 + 64, :], "xin%d" % s, writes=[("xin", s)])
                else:
                    dma(xin[s][:], w_out.ap()[c * 128:(c + 1) * 128, :], "xin%d" % s, writes=[("xin", s)])
                cp("act" if c % 2 else "dve", w_out_bf[:, c, :], xin[s][:], [("xin", s)], [("wout", c)])

        def norm_transpose(src_ap, n, dst, dst_tok, gain_unused=None):
            import os
            NT = 99
            s = xin_slot()
            dma(xin[s][0:n, :], src_ap, "xin%d" % s, writes=[("xin", s)])
            b = nxt("xn", 2)
            st = next_stat()
            if NT < 1: return
            act(xn_bf[b][0:n, :], xin[s][0:n, :], AF.Square, [("xin", s)], [("xn", b), ("stat", st)],
                scale=1.0 / 32.0, accum=stat[0:n, st:st + 1])
            if NT < 2: return
            ts("dve", stat2[0:n, st:st + 1], stat[0:n, st:st + 1], EPS, None, ALU.add, None,
               [("stat", st)], [("stat2", st)])
            tt("pool", rstd[0:n, st:st + 1], stat2[0:n, st:st + 1], mhalf[0:n, :], ALU.pow,
               [("stat2", st), "mhalf"], [("rstd", st)])
            if NT < 3: return
            ts("dve", xn_bf[b][0:n, :], xin[s][0:n, :], rstd[0:n, st:st + 1], None, ALU.mult, None,
               [("xin", s), ("rstd", st)], [("xn", b)])
            if NT < 4: return
            bk = next_bank()
            pv = bank_bf(bk)
            for k in range(8):
                tr(pv[:, k * n:(k + 1) * n], xn_bf[b][0:n, k * 128:(k + 1) * 128], ident[0:n, 0:n],
                   [("xn", b), "ident"], [("ps", bk)])
            if NT < 5: return
            cp("dve", dst, pv[:, 0:8 * n].rearrange("p (k n) -> p k n", k=8), [("ps", bk)], [dst_tok])

        C_QA, C_KA, C_VA, C_GA, C_QB, C_KB, C_VB, C_GB, C_QM, C_GM = 0, 384, 768, 1152, 1536, 1920, 2048, 2176, 2560, 2816

        def win_reads(c0, c1):
            css = sorted(set([c0 // 1024, (c1 - 1) // 1024]))
            return [("win", k, cs) for k in range(8) for cs in css]

        def wcols(k, c0, n=128):
            return w_in_bf[:, k, c0:c0 + n]

        def ga_col(c):
            return 2048 if c == 0 else C_GA + 128 * c

        def wcols_pair(k, base, p_):
            return w_in_bf[:, k, base + 128 * p_:base + 128 * p_ + 128]

        def evac_engine():
            return "dve"

        rr["mul"] = 0

        def mul_engine():
            return "pool" if nxt("mul", 3) == 0 else "dve"

        def proj_fm(lhs_fn, wreads, rhs_fn, xtoks, ncols, evac):
            bk = next_bank()
            for k in range(8):
                mm(bank(bk)[:, 0:ncols], lhs_fn(k), rhs_fn(k), k == 0, k == 7, wreads + xtoks, [("ps", bk)])
            evac(bank(bk)[:, 0:ncols], ("ps", bk))

        def gate_evac(dst, dst_tok, n):
            def f(ps, pstok):
                import os
                KG = 99
                if KG < 1: return
                b = nxt("tb", 2)
                act(tbuf[b][:, 0:n], ps, AF.Tanh, [pstok], [("tb", b)], scale=0.5)
                if KG < 2: return
                stt("dve", dst, tbuf[b][:, 0:n], 1.0, ps, ALU.add, ALU.mult, [("tb", b), pstok], [dst_tok])
            return f

        def copy_evac(dst, dst_tok):
            def f(ps, pstok):
                cp(evac_engine(), dst, ps, [pstok], [dst_tok])
            return f

        def slot_of(blk):
            return 12 + blk if blk < 4 else (blk - 4) % 12

        xbuf_of = {}
        rr["xb"] = 0

        def project_stage(st_i, xonly=False, xdone=False):
            halo = st_i == 0
            blk0 = 4 * st_i
            sl0 = slot_of(blk0)
            if not xdone:
                xb = nxt("xb", 2)
                xbuf_of[st_i] = xb
                for tb in range(4):
                    r0 = (blk0 + tb) * 128
                    norm_transpose(xh.ap()[r0:r0 + 128, :], 128, xnT2[xb][:, :, tb * 128:(tb + 1) * 128], ("xnT", xb, tb))
                    yield 1.0
            xb = xbuf_of[st_i]
            xnT = xnT2[xb]
            xt = [("xnT", xb, tb) for tb in range(4)]
            par = st_i % 2
            qT_a, qT_b, qT_m, sg = qT_a2[par], qT_b2[par], qT_m2[par], sgog[par]
            if xonly:
                return
            rhs_x = lambda k: xnT[:, k, :]
            allwin_ = [("win", k, cs) for k in range(8) for cs in range(3)]
            CH = 2.3
            if not halo:
                for c in range(3):
                    proj_fm(lambda k, c=c: wcols(k, C_QA + 128 * c), win_reads(C_QA, C_QA + 384), rhs_x, xt, 512,
                            copy_evac(qT_a[:, c, :], ("qa", par % NQ, c)))
                    yield CH
            for c in range(3):
                def kevac(ps, pstok, c=c):
                    cp(evac_engine(), kT_a[:, c, sl0 * 128:sl0 * 128 + 512], ps, [pstok],
                       [("ka", sl0 + i, c) for i in range(4)])
                proj_fm(lambda k, c=c: wcols(k, C_KA + 128 * c), win_reads(C_KA, C_KA + 384), rhs_x, xt, 512, kevac)
                yield CH
            if not halo:
                for p_ in range(3):
                    proj_fm(lambda k, p_=p_: wcols_pair(k, C_QB, p_), win_reads(C_QB, C_QB + 384), rhs_x, xt, 512,
                            copy_evac(qT_b[:, p_, :], ("qb", par % NQ, p_)))
                    yield CH

            def kbevac(ps, pstok):
                cp(evac_engine(), kT_b[:, sl0 * 128:sl0 * 128 + 512], ps, [pstok],
                   [("kb", sl0 + i) for i in range(4)])
            proj_fm(lambda k: wcols(k, C_KB), win_reads(C_KB, C_KB + 128), rhs_x, xt, 512, kbevac)
            yield CH
            if not halo:
                for c in range(2):
                    proj_fm(lambda k, c=c: wcols(k, C_QM + 128 * c), win_reads(C_QM, C_QM + 256), rhs_x, xt, 512,
                            copy_evac(qT_m[:, c, :], ("qm", par % NQ, c)))
                    yield CH
            last = st_i == 4
            for tb in range(4):
                sl = sl0 + tb
                bk = next_bank()
                pstok = ("ps", bk)
                pb = bank(bk)
                for k in range(8):
                    mm(pb[:, 0:512], xnT[:, k, tb * 128:(tb + 1) * 128], w_in_bf[:, k, C_VA:C_VA + 512],
                       k == 0, k == 7, allwin_ + [("xnT", xb, tb)], [pstok])
                pa = pb[:, 0:384].rearrange("p (c e d) -> p c e d", c=3, e=2, d=64)
                va = vaug_a[:, sl]
                if halo:
                    ts("dve", va[:, :, 0:64], pa[:, :, 0, :], hm[:, 0:1], None, ALU.mult, None, [pstok, "hm"], [("va", sl)])
                    ts("dve", va[:, :, 128:192], pa[:, :, 1, :], hm[:, 0:1], None, ALU.mult, None, [pstok, "hm"], [("va", sl)])
                    ts("dve", vaug_b[:, sl, 0:64], pb[:, 384:448], hm[:, 0:1], None, ALU.mult, None, [pstok, "hm"], [("vb", sl)])
                    ts("dve", vaug_b[:, sl, 128:192], pb[:, 448:512], hm[:, 0:1], None, ALU.mult, None, [pstok, "hm"], [("vb", sl)])
                else:
                    cp("dve", va[:, :, 0:64], pa[:, :, 0, :], [pstok], [("va", sl)])
                    cp("dve", va[:, :, 128:192], pa[:, :, 1, :], [pstok], [("va", sl)])
                    cp("dve", vaug_b[:, sl, 0:64], pb[:, 384:448], [pstok], [("vb", sl)])
                    cp("dve", vaug_b[:, sl, 128:192], pb[:, 448:512], [pstok], [("vb", sl)])
                if last:
                    ys_ = nxt("yst", 2)
                    ytok = ("yst", ys_)
                    cp("act", yst[ys_][:, 512:1024], pb[:, 0:512], [pstok], [ytok])
                    bk2 = next_bank()
                    pstok2 = ("ps", bk2)
                    pb2 = bank(bk2)
                    for k in range(8):
                        mm(pb2[:, 0:384], xnT[:, k, tb * 128:(tb + 1) * 128], w_in_bf[:, k, C_KA:C_KA + 384],
                           k == 0, k == 7, win_reads(C_KA, C_KA + 384) + [("xnT", xb, tb)], [pstok2])
                    for k in range(8):
                        mm(pb2[:, 384:512], xnT[:, k, tb * 128:(tb + 1) * 128], w_in_bf[:, k, C_KB:C_KB + 128],
                           k == 0, k == 7, win_reads(C_KB, C_KB + 128) + [("xnT", xb, tb)], [pstok2])
                    cp("dve", yst[ys_][:, 0:512], pb2[:, 0:512], [pstok2], [ytok])
                    key = "yst%d" % ys_
                    dma(sak_d.ap()[tb * 128:(tb + 1) * 128, :], yst[ys_][:, 0:384], key, reads=[ytok])
                    dma(sav_d.ap()[tb * 128:(tb + 1) * 128, :], yst[ys_][:, 512:896], key, reads=[ytok])
                    if tb == 3:
                        dma(sbk_d.ap(), yst[ys_][:, 384:512], key, reads=[ytok])
                        dma(sbv_d.ap(), yst[ys_][:, 896:1024], key, reads=[ytok])
                    out_tokens.append(ytok)
                yield 3.0 if not last else 6.0
            if not halo:
                for c in range(3):
                    proj_fm(lambda k, c=c: wcols(k, ga_col(c)), win_reads(C_GA, C_GA + 384), rhs_x, xt, 512,
                            gate_evac(sg[:, c, :], ("sg", par, c), 512))
                    yield CH
                for p_ in range(3):
                    proj_fm(lambda k, p_=p_: wcols_pair(k, C_GB, p_), win_reads(C_GB, C_GB + 384), rhs_x, xt, 512,
                            gate_evac(sg[:, 3 + p_, :], ("sg", par, 3 + p_), 512))
                    yield CH
                for c in range(2):
                    proj_fm(lambda k, c=c: wcols(k, C_GM + 128 * c), win_reads(C_GM, C_GM + 256), rhs_x, xt, 512,
                            gate_evac(sg[:, 6 + c, :], ("sg", par, 6 + c), 512))
                    yield CH

        def mem_kv():
            xb = nxt("xb", 2)
            memxT = xnT2[xb]
            ogT = sgog[0]
            for mb in range(2):
                norm_transpose(mem_d.ap()[mb * 128:(mb + 1) * 128, :], 128, memxT[:, :, mb * 128:(mb + 1) * 128],
                               ("xnT", xb, mb))
            mt = [("xnT", xb, 0), ("xnT", xb, 1)]
            ogr = [("sg", 0, k) for k in range(8)]
            for c in range(2):
                bk = next_bank()
                for k in range(8):
                    mm(bank(bk)[:, 0:256], ogT[:, k, c * 128:(c + 1) * 128], memxT[:, k, 0:256], k == 0, k == 7,
                       ogr + mt, [("ps", bk)])
                cp("dve", kT_m[:, c, :], bank(bk)[:, 0:256], [("ps", bk)], [("km", c)])
            for mb in range(2):
                bk = next_bank()
                pstok = ("ps", bk)
                for k in range(8):
                    mm(bank(bk)[:, 0:512], memxT[:, k, mb * 128:(mb + 1) * 128], ogT[:, k, :], k == 0, k == 7,
                       ogr + mt, [pstok])
                ys_ = nxt("yst", 2)
                ytok = ("yst", ys_)
                cp("act", yst[ys_][:, 0:512], bank(bk)[:, 0:512], [pstok], [ytok])
                pm = bank(bk)[:, 256:512].rearrange("p (c e d) -> p c e d", c=2, e=2, d=64)
                vm = vaug_m[:, mb]
                cp("act", vm[:, :, 0:64], pm[:, :, 0, :], [pstok], [("vm", mb)])
                cp("act", vm[:, :, 128:192], pm[:, :, 1, :], [pstok], [("vm", mb)])
                key = "yst%d" % ys_
                dma(smk_d.ap()[mb * 128:(mb + 1) * 128, :], yst[ys_][:, 0:256], key, reads=[ytok])
                dma(smv_d.ap()[mb * 128:(mb + 1) * 128, :], yst[ys_][:, 256:512], key, reads=[ytok])
                out_tokens.append(ytok)

        def attention_group(tiles, chunk, n_q, sink_heads=None):
            ab = PS_CFG["ACC"][nxt("acc", len(PS_CFG["ACC"]))]
            accE = bank(2 * ab)
            accO = bank(2 * ab + 1)
            tokE = ("ps", 2 * ab)
            tokO = ("ps", 2 * ab + 1)
            n = len(tiles)
            state = {}

            def qk(i):
                t = tiles[i]
                sc = PS_CFG["S"][nxt("sc", len(PS_CFG["S"]))]
                S = PS[sc]
                stok = [("ps", 2 * sc), ("ps", 2 * sc + 1)]
                nc_ = t["ncols"]
                r = t["rows"]
                mm(S[0:r, 0:nc_], t["kE"], t["qE"], True, True, t["ktoks"] + t["qtoks"], [stok[0]])
                mm(S[0:r, 512:512 + nc_], t["kO"], t["qO"], True, True, t["ktoks"] + t["qtoks"], [stok[1]])
                pi = nxt("pt", NPT)
                pttok = ("pt", pi)
                Sv = S[:].rearrange("p (b n) -> p b n", b=2)
                act(PT[pi][0:r, :, 0:nc_], Sv[0:r, :, 0:nc_], AF.Exp, stok, [pttok], scale=SCALE)
                if t["post"] is not None:
                    t["post"](PT[pi], pttok)
                state[i] = (pi, pttok)

            def pv(i):
                t = tiles[i]
                pi, pttok = state[i]
                nc_ = t["ncols"]
                r = t["rows"]
                c0 = t["c0"]
                first = i == 0
                lastmm = (i == n - 1) and sink_heads is None
                vE_, one_, vO_ = t["vE"][:, 0:64], t["vE"][:, 64:128], t["vO"][:, 64:128]
                pe_, po_ = PT[pi][0:r, 0, 0:nc_], PT[pi][0:r, 1, 0:nc_]
                rd = t["vtoks"] + [pttok]
                mm(accE[0:64, c0:c0 + nc_], vE_, pe_, first, lastmm, rd, [tokE], skip=True)
                mm(accE[64:128, c0:c0 + nc_], vO_, po_, first, lastmm, rd, [tokE], skip=True)
                mm(accO[0:64, c0:c0 + nc_], one_, pe_, first, lastmm, rd, [tokO], skip=True)
                mm(accO[64:128, c0:c0 + nc_], one_, po_, first, lastmm, rd, [tokO], skip=True)

            for i in range(n + 1):
                if i < n:
                    qk(i)
                if i >= 1:
                    pv(i - 1)
                yield 0.7
            if sink_heads is not None:
                hE, hO, srow = sink_heads
                mm(accE[:, 0:n_q], selrow[0:1, 0, :], srow(hE), False, True, ["selrow", "esrow"], [tokE])
                mm(accO[:, 0:n_q], selrow[0:1, 1, :], srow(hO), False, True, ["selrow", "esrow"], [tokO])
            return accE, accO, tokE, tokO

        def normalise(accE, accO, tokE, tokO, sg_ap, sgtok, og_ap, ogtok, n_q, sinks=None):
            ni = nxt("nrm", 1)
            ntok = ("nrm", ni)
            N = nrm[ni]
            if sinks is None:
                act(N[:, 0:n_q], accO[:, 0:n_q], AF.Ln, [tokO], [ntok])
            else:
                act(N[:, 0:n_q], accO[:, 0:n_q], AF.Ln, [tokO, "esink2"], [ntok], bias=esink2[:, sinks[0]:sinks[0] + 1])
            act(N[:, 0:n_q], N[:, 0:n_q], AF.Exp, [ntok], [ntok], scale=-1.0)
            ti = nxt("nrmT", 1)
            ttok = ("nrmT", ti)
            stt("dve", nrmT[ti][:, 0:n_q], accE[:, 0:n_q], 0.5, sg_ap, ALU.mult, ALU.mult, [tokE, sgtok], [ttok])
            tt("dve", og_ap, nrmT[ti][:, 0:n_q], N[:, 0:n_q], ALU.mult, [ttok, ntok], [ogtok])

        def attention_stage(st_i):
            I0 = 4 * st_i
            par = st_i % 2
            qT_a, qT_b, qT_m, sg = qT_a2[par], qT_b2[par], qT_m2[par], sgog[par]
            ogT = sg
            groups = []
            for p_ in range(3):
                tiles = []
                for J in range(I0 - 4, I0 + 4):
                    sl = slot_of(J)
                    c0b = max(J - I0, 0)
                    c1b = min(J + 4 - I0, 3) + 1
                    ncols = (c1b - c0b) * 128
                    rlo = max(0, c0b + I0 - J)
                    rhi = min(1, c1b - 1 + I0 - J)
                    corner = (J + 4 - I0) if J + 4 <= I0 + 3 else None

                    def post(pt, pttok, p_=p_, rlo=rlo, rhi=rhi, c0b=c0b, J=J, corner=corner, I0=I0):
                        if rhi >= rlo:
                            cs = (rlo + J - I0 - c0b) * 128
                            ce = (rhi + 1 + J - I0 - c0b) * 128
                            tt(mul_engine(), pt[:, :, cs:ce], pt[:, :, cs:ce],
                               expBB_a[:, 2 * p_:2 * p_ + 2, rlo * 128:(rhi + 1) * 128], ALU.mult,
                               [pttok, "ebba"], [pttok])
                        if corner is not None:
                            off = (corner - c0b) * 128
                            mset("pool", pt[0:64, :, off + 64:off + 128], 0.0, [pttok])
                    tiles.append(dict(
                        kE=kT_a[0:64, p_, sl * 128:(sl + 1) * 128], kO=kT_a[64:128, p_, sl * 128:(sl + 1) * 128],
                        qE=qT_a[0:64, p_, c0b * 128:c1b * 128], qO=qT_a[64:128, p_, c0b * 128:c1b * 128],
                        ktoks=[("ka", sl, p_)], qtoks=[("qa", par % NQ, p_)], c0=c0b * 128, ncols=ncols, rows=128,
                        vE=vaug_a[:, sl, p_, 0:128], vO=vaug_a[:, sl, p_, 64:192], vtoks=[("va", sl)], post=post))
                def grpA(tiles=tiles, p_=p_):
                    accE, accO, tokE, tokO = yield from attention_group(tiles, p_, 512)
                    normalise(accE, accO, tokE, tokO, sg[:, p_, :], ("sg", par, p_), ogT[:, p_, :], ("sg", par, p_), 512)
                groups.append(grpA)
            for p_ in range(3):
                tiles = []
                for J in range(I0 - 1, I0 + 4):
                    sl = slot_of(J)
                    c0b = max(J - I0, 0)
                    c1b = min(J + 1 - I0, 3) + 1
                    ncols = (c1b - c0b) * 128
                    rlo = max(0, c0b + I0 - J)
                    rhi = min(1, c1b - 1 + I0 - J)

                    def post(pt, pttok, p_=p_, rlo=rlo, rhi=rhi, ncols=ncols):
                        tt(mul_engine(), pt[:, :, 0:ncols], pt[:, :, 0:ncols],
                           expBB_b[:, p_:p_ + 4:3, rlo * 128:(rhi + 1) * 128], ALU.mult, [pttok, "ebbb"], [pttok])
                    tiles.append(dict(
                        kE=kT_b[0:64, sl * 128:(sl + 1) * 128], kO=kT_b[64:128, sl * 128:(sl + 1) * 128],
                        qE=qT_b[0:64, p_, c0b * 128:c1b * 128], qO=qT_b[64:128, p_, c0b * 128:c1b * 128],
                        ktoks=[("kb", sl)], qtoks=[("qb", par % NQ, p_)], c0=c0b * 128, ncols=ncols, rows=128,
                        vE=vaug_b[:, sl, 0:128], vO=vaug_b[:, sl, 64:192], vtoks=[("vb", sl)], post=post))
                def grpB(tiles=tiles, p_=p_):
                    accE, accO, tokE, tokO = yield from attention_group(tiles, 3 + p_, 512, sink_heads=None)
                    normalise(accE, accO, tokE, tokO, sg[:, 3 + p_, :], ("sg", par, 3 + p_), ogT[:, 3 + p_, :],
                              ("sg", par, 3 + p_), 512, sinks=(p_, p_ + 3))
                groups.append(grpB)
            for c in range(2):
                tiles = []
                for mb in range(2):
                    tiles.append(dict(
                        kE=kT_m[0:64, c, mb * 128:(mb + 1) * 128], kO=kT_m[64:128, c, mb * 128:(mb + 1) * 128],
                        qE=qT_m[0:64, c, :], qO=qT_m[64:128, c, :],
                        ktoks=[("km", c)], qtoks=[("qm", par % NQ, c)], c0=0, ncols=512, rows=128,
                        vE=vaug_m[:, mb, c, 0:128], vO=vaug_m[:, mb, c, 64:192], vtoks=[("vm", mb)], post=None))
                def grpM(tiles=tiles, c=c):
                    accE, accO, tokE, tokO = yield from attention_group(tiles, 6 + c, 512)
                    normalise(accE, accO, tokE, tokO, sg[:, 6 + c, :], ("sg", par, 6 + c), ogT[:, 6 + c, :],
                              ("sg", par, 6 + c), 512)
                groups.append(grpM)
            order_ = {"seq": [0, 1, 2, 3, 4, 5, 6, 7], "mix": [0, 3, 1, 4, 2, 5, 6, 7]}["seq"]
            glist = [groups[i] for i in order_]
            if False:
                for i in range(0, 8, 2):
                    ga, gb = glist[i](), glist[i + 1]()
                    alive = [True, True]
                    gens = [ga, gb]
                    k = 0
                    while any(alive):
                        j = k % 2
                        k += 1
                        if not alive[j]:
                            continue
                        try:
                            yield next(gens[j])
                        except StopIteration:
                            alive[j] = False
            else:
                for g in glist:
                    yield from g()

        def out_block(lhs_fn, ogreads, n, x_src, y_dst):
            ob = PS_CFG["OUT"][nxt("op", len(PS_CFG["OUT"]))]
            O = PS[ob]
            otok = [("ps", 2 * ob), ("ps", 2 * ob + 1)]
            for hf in range(2):
                for c in range(8):
                    mm(O[0:n, hf * 512:(hf + 1) * 512], lhs_fn(c), w_out_bf[:, c, hf * 512:(hf + 1) * 512],
                       c == 0, c == 7, ogreads + [("wout", c)], [otok[hf]])
            ys_ = nxt("yst", 2)
            ytok = ("yst", ys_)
            dma(yst[ys_][0:n, :], x_src, "yst%d" % ys_, writes=[ytok])
            st = next_stat()
            act(junk[0:n, :], O[0:n, :], AF.Square, otok, ["junk", ("stat", st)], scale=1.0 / 32.0,
                accum=stat[0:n, st:st + 1])
            ts("dve", stat2[0:n, st:st + 1], stat[0:n, st:st + 1], EPS, None, ALU.add, None, [("stat", st)], [("stat2", st)])
            tt("pool", rstd[0:n, st:st + 1], stat2[0:n, st:st + 1], mhalf[0:n, :], ALU.pow,
               [("stat2", st), "mhalf"], [("rstd", st)])
            stt("dve", O[0:n, :], O[0:n, :], rstd[0:n, st:st + 1], gpost_b[0:n, :], ALU.mult, ALU.mult,
                otok + [("rstd", st), "gpost"], otok)
            tt("dve", yst[ys_][0:n, :], O[0:n, :], yst[ys_][0:n, :], ALU.add, otok + [ytok], [ytok])
            dma(y_dst, yst[ys_][0:n, :], "yst%d" % ys_, reads=[ytok])
            out_tokens.append(ytok)

        def out_stage(st_i):
            par = st_i % 2
            ogT = sgog[par]
            og_all = [("sg", par, c) for c in range(8)]
            for tb in range(4):
                r0 = (4 * st_i + tb) * 128
                o0 = (4 * (st_i - 1) + tb) * 128
                out_block(lambda c, tb=tb: ogT[:, c, tb * 128:(tb + 1) * 128], og_all, 128,
                          xh.ap()[r0:r0 + 128, :], y_d.ap()[o0:o0 + 128, :])
                yield 4.5

        def sample_phase():
            norm_transpose(xs_d.ap(), 16, xnTs[:, :, :], "xnTs")
            xt = ["xnTs"]
            rhs_x = lambda k: xnTs[:, k, :]
            allwin = [("win", k, cs) for k in range(8) for cs in range(3)]
            fm = []
            for c in range(3):
                fm.append((lambda k, c=c: wcols(k, C_QA + 128 * c), "q", qs_a[:, c, :], "qs_a"))
            for c in range(3):
                fm.append((lambda k, c=c: wcols(k, C_KA + 128 * c), "ka", c, None))
            for p_ in range(3):
                fm.append((lambda k, p_=p_: wcols_pair(k, C_QB, p_), "q", qs_b[:, p_, :], "qs_b"))
            fm.append((lambda k: wcols(k, C_KB), "kb", None, None))
            for c in range(2):
                fm.append((lambda k, c=c: wcols(k, C_QM + 128 * c), "q", qs_m[:, c, :], "qs_m"))
            for c in range(3):
                fm.append((lambda k, c=c: wcols(k, ga_col(c)), "g", c, None))
            for p_ in range(3):
                fm.append((lambda k, p_=p_: wcols_pair(k, C_GB, p_), "g", 3 + p_, None))
            for c in range(2):
                fm.append((lambda k, c=c: wcols(k, C_GM + 128 * c), "g", 6 + c, None))
            bk = next_bank()
            pstok = ("ps", bk)
            pb = bank(bk)
            for i, (lf, kind, a1, a2) in enumerate(fm):
                for k in range(8):
                    mm(pb[:, i * 16:(i + 1) * 16], lf(k), rhs_x(k), k == 0, k == 7, allwin + xt, [pstok])
            for i, (lf, kind, a1, a2) in enumerate(fm):
                ps = pb[:, i * 16:(i + 1) * 16]
                if kind == "q":
                    cp("dve", a1, ps, [pstok], [a2])
                elif kind == "ka":
                    cp("dve", kT_a[:, a1, 512:528], ps, [pstok], [("ka", 4, a1)])
                elif kind == "kb":
                    cp("dve", kT_b[:, 128:144], ps, [pstok], [("kb", 1)])
                else:
                    act(tbs[:, a1, :], ps, AF.Tanh, [pstok], ["tbs"], scale=0.5)
                    stt("dve", sgs[:, a1, :], tbs[:, a1, :], 1.0, ps, ALU.add, ALU.mult, ["tbs", pstok], ["sgs"])
            ob = PS_CFG["OUT"][nxt("op", len(PS_CFG["OUT"]))]
            O = PS[ob]
            otok = [("ps", 2 * ob), ("ps", 2 * ob + 1)]
            for k in range(8):
                mm(O[0:16, 0:512], xnTs[:, k, :], w_in_bf[:, k, 384:896], k == 0, k == 7, allwin + xt, [otok[0]])
            for k in range(8):
                mm(O[0:16, 512:768], xnTs[:, k, :], w_in_bf[:, k, 896:1152], k == 0, k == 7, allwin + xt, [otok[1]])
            for k in range(8):
                mm(O[0:16, 768:896], xnTs[:, k, :], w_in_bf[:, k, 1920:2048], k == 0, k == 7, allwin + xt, [otok[1]])
            for k in range(8):
                mm(O[0:16, 896:1024], xnTs[:, k, :], w_in_bf[:, k, 1152:1280], k == 0, k == 7, allwin + xt, [otok[1]])
            ys_ = nxt("yst", 2)
            ytok = ("yst", ys_)
            cp("dve", yst[ys_][0:16, :], O[0:16, :], otok, [ytok])
            key = "yst%d" % ys_
            dma(aks_d.ap(), yst[ys_][0:16, 0:384], key, reads=[ytok])
            dma(avs_d.ap(), yst[ys_][0:16, 384:768], key, reads=[ytok])
            dma(bks_d.ap(), yst[ys_][0:16, 768:896], key, reads=[ytok])
            dma(bvs_d.ap(), yst[ys_][0:16, 896:1024], key, reads=[ytok])
            out_tokens.append(ytok)
            pa = O[0:16, 384:768].rearrange("p (c e d) -> p c e d", c=3, e=2, d=64)
            va = vaug_a[0:16, 4]
            cp("dve", va[:, :, 0:64], pa[:, :, 0, :], otok, [("va", 4)])
            cp("dve", va[:, :, 128:192], pa[:, :, 1, :], otok, [("va", 4)])
            cp("dve", vaug_b[0:16, 1, 0:64], O[0:16, 896:960], otok, [("vb", 1)])
            cp("dve", vaug_b[0:16, 1, 128:192], O[0:16, 960:1024], otok, [("vb", 1)])
            for blk in range(4):
                s = xin_slot()
                xk = "xin%d" % s
                dma(xin[s][:, 0:384], cak.ap()[blk * 128:(blk + 1) * 128, :], xk, writes=[("xin", s)])
                dma(xin[s][:, 384:768], cav.ap()[blk * 128:(blk + 1) * 128, :], xk, writes=[("xin", s)])
                cp("act", kcast[:, 0:384], xin[s][:, 0:384], [("xin", s)], ["kcast"])
                bk = next_bank()
                pv_ = bank_bf(bk)
                for c in range(3):
                    tr(pv_[:, c * 128:(c + 1) * 128], kcast[:, c * 128:(c + 1) * 128], ident[:], ["kcast", "ident"], [("ps", bk)])
                for c in range(3):
                    cp("dve", kT_a[:, c, blk * 128:(blk + 1) * 128], pv_[:, c * 128:(c + 1) * 128], [("ps", bk)], [("ka", blk, c)])
                xv = xin[s][:, 384:768].rearrange("p (c e d) -> p c e d", c=3, e=2, d=64)
                va = vaug_a[:, blk]
                cp("dve", va[:, :, 0:64], xv[:, :, 0, :], [("xin", s)], [("va", blk)])
                cp("dve", va[:, :, 128:192], xv[:, :, 1, :], [("xin", s)], [("va", blk)])
            s = xin_slot()
            xk = "xin%d" % s
            dma(xin[s][:, 0:128], cbk.ap(), xk, writes=[("xin", s)])
            dma(xin[s][:, 128:256], cbv.ap(), xk, writes=[("xin", s)])
            cp("act", kcast[:, 0:128], xin[s][:, 0:128], [("xin", s)], ["kcast"])
            bk = next_bank()
            pv_ = bank_bf(bk)
            tr(pv_[:, 0:128], kcast[:, 0:128], ident[:], ["kcast", "ident"], [("ps", bk)])
            cp("dve", kT_b[:, 0:128], pv_[:, 0:128], [("ps", bk)], [("kb", 0)])
            cp("dve", vaug_b[:, 0, 0:64], xin[s][:, 128:192], [("xin", s)], [("vb", 0)])
            cp("dve", vaug_b[:, 0, 128:192], xin[s][:, 192:256], [("xin", s)], [("vb", 0)])
            for mb in range(2):
                s = xin_slot()
                xk = "xin%d" % s
                dma(xin[s][:, 0:256], cmk.ap()[mb * 128:(mb + 1) * 128, :], xk, writes=[("xin", s)])
                dma(xin[s][:, 256:512], cmv.ap()[mb * 128:(mb + 1) * 128, :], xk, writes=[("xin", s)])
                cp("act", kcast[:, 0:256], xin[s][:, 0:256], [("xin", s)], ["kcast"])
                bk = next_bank()
                pv_ = bank_bf(bk)
                for c in range(2):
                    tr(pv_[:, c * 128:(c + 1) * 128], kcast[:, c * 128:(c + 1) * 128], ident[:], ["kcast", "ident"], [("ps", bk)])
                for c in range(2):
                    cp("dve", kT_m[:, c, mb * 128:(mb + 1) * 128], pv_[:, c * 128:(c + 1) * 128], [("ps", bk)], [("km", c)])
                xv = xin[s][:, 256:512].rearrange("p (c e d) -> p c e d", c=2, e=2, d=64)
                vm = vaug_m[:, mb]
                cp("dve", vm[:, :, 0:64], xv[:, :, 0, :], [("xin", s)], [("vm", mb)])
                cp("dve", vm[:, :, 128:192], xv[:, :, 1, :], [("xin", s)], [("vm", mb)])

            def sample_attn(nblk, npair, rows_of, kE_of, kO_of, q_tile, qtok, ktoks_of, v_of, vtoks_of, ptile, pttokname,
                            bias_of, sink, og_chunk0):
                ncols = npair * 16
                for blk in range(nblk):
                    r = rows_of(blk)
                    sc = PS_CFG["S"][nxt("sc", len(PS_CFG["S"]))]
                    S = PS[sc]
                    stok = [("ps", 2 * sc), ("ps", 2 * sc + 1)]
                    for c in range(npair):
                        mm(S[0:r, c * 16:(c + 1) * 16], kE_of(blk, c), q_tile[0:64, c, :], True, True,
                           ktoks_of(blk, c) + [qtok], [stok[0]])
                        mm(S[0:r, 512 + c * 16:512 + (c + 1) * 16], kO_of(blk, c), q_tile[64:128, c, :], True, True,
                           ktoks_of(blk, c) + [qtok], [stok[1]])
                    Sv = S[:].rearrange("p (b n) -> p b n", b=2)
                    pttok = (pttokname, blk)
                    act(ptile[0:r, blk, :, 0:ncols], Sv[0:r, :, 0:ncols], AF.Exp, stok, [pttok], scale=SCALE)
                    b_ = bias_of(blk)
                    if b_ is not None:
                        pf = ptile[0:r, blk].rearrange("p e n -> p (e n)")
                        tt("dve", pf, pf, b_[0:r], ALU.mult, [pttok, "ebsa", "ebsb"], [pttok])
                ab = PS_CFG["ACC"][nxt("acc", len(PS_CFG["ACC"]))]
                A = PS[ab]
                atok = [("ps", 2 * ab), ("ps", 2 * ab + 1)]
                for c in range(npair):
                    for e_ in range(2):
                        dst = A[:, e_ * 512 + c * 16:e_ * 512 + (c + 1) * 16]
                        for blk in range(nblk):
                            r = rows_of(blk)
                            mm(dst, v_of(blk, c, e_)[0:r], ptile[0:r, blk, e_, c * 16:(c + 1) * 16], blk == 0,
                               (blk == nblk - 1) and sink is None, vtoks_of(blk) + [(pttokname, blk)], [atok[e_]])
                        if sink is not None:
                            hh = sink(c, e_)
                            mm(dst, selrow[0:1, e_, :], esrow[0:1, hh, 0:16], False, True, ["selrow", "esrow"], [atok[e_]])
                Av = A[:].rearrange("p (b n) -> p b n", b=2)
                act(nrms[0:64, 0:ncols], Av[64:128, 0, 0:ncols], AF.Ln, atok, ["iota"])
                act(nrms[64:128, 0:ncols], Av[0:64, 1, 0:ncols], AF.Ln, atok, ["iota"])
                act(nrms[:, 0:ncols], nrms[:, 0:ncols], AF.Exp, ["iota"], ["iota"], scale=-1.0)
                sgv = sgs[:, og_chunk0:og_chunk0 + npair, :].rearrange("p c n -> p (c n)")
                tt("dve", nrms[:, 0:ncols], nrms[:, 0:ncols], sgv, ALU.mult, ["iota", "sgs"], ["iota"])
                ogv = ogs[:, og_chunk0:og_chunk0 + npair, :].rearrange("p c n -> p (c n)")
                stt("dve", ogv[0:64], Av[0:64, 0, 0:ncols], 0.5, nrms[0:64, 0:ncols], ALU.mult, ALU.mult, atok + ["iota"], ["ogs"])
                stt("dve", ogv[64:128], Av[64:128, 1, 0:ncols], 0.5, nrms[64:128, 0:ncols], ALU.mult, ALU.mult, atok + ["iota"], ["ogs"])

            sample_attn(
                5, 3, lambda blk: 16 if blk == 4 else 128,
                lambda blk, c: kT_a[0:64, c, blk * 128:blk * 128 + (16 if blk == 4 else 128)],
                lambda blk, c: kT_a[64:128, c, blk * 128:blk * 128 + (16 if blk == 4 else 128)],
                qs_a, "qs_a", lambda blk, c: [("ka", blk, c)],
                lambda blk, c, e_: vaug_a[:, blk, c, 64 * e_:64 * e_ + 128], lambda blk: [("va", blk)],
                PTs, "pts", lambda blk: expBs_a[:, blk - 3, :] if blk >= 3 else None, None, 0)
            sample_attn(
                2, 3, lambda blk: 16 if blk == 1 else 128,
                lambda blk, c: kT_b[0:64, blk * 128:blk * 128 + (16 if blk == 1 else 128)],
                lambda blk, c: kT_b[64:128, blk * 128:blk * 128 + (16 if blk == 1 else 128)],
                qs_b, "qs_b", lambda blk, c: [("kb", blk)],
                lambda blk, c, e_: vaug_b[:, blk, 64 * e_:64 * e_ + 128], lambda blk: [("vb", blk)],
                PTsb, "ptsb", lambda blk: expBs_b[:, blk, :], lambda c, e_: c + 3 * e_, 3)
            sample_attn(
                2, 2, lambda blk: 128,
                lambda blk, c: kT_m[0:64, c, blk * 128:(blk + 1) * 128],
                lambda blk, c: kT_m[64:128, c, blk * 128:(blk + 1) * 128],
                qs_m, "qs_m", lambda blk, c: [("km", c)],
                lambda blk, c, e_: vaug_m[:, blk, c, 64 * e_:64 * e_ + 128], lambda blk: [("vm", blk)],
                PTsm, "ptsm", lambda blk: None, None, 6)
            out_block(lambda c: ogs[:, c, :], ["ogs"], 16, xs_d.ap(), ys_d.ap())

        def run(gen, tag):
            P.tag = tag
            for _ in gen:
                pass

        def chain(*parts):
            for tag, g in parts:
                for c in g:
                    yield tag, c

        def interleave(streams):
            n = len(streams)
            done = [0.0] * n
            alive = [True] * n
            while any(alive):
                i = min((j for j in range(n) if alive[j]), key=lambda j: done[j] / streams[j][0])
                try:
                    tag, c = next(streams[i][1])
                    done[i] += c
                except StopIteration:
                    alive[i] = False
                    continue
            return

        def tagged(tag, g):
            it = iter(g)
            while True:
                P.tag = tag
                try:
                    c = next(it)
                except StopIteration:
                    return
                yield tag, c

        P.tag = "w"
        P.boost = 0.0
        run(project_stage(1, xonly=True), "w")
        P.boost = 0.0
        P.tag = "w"
        load_w_in(0)
        load_w_in(1)
        load_w_in(2)
        load_w_mem()
        run(project_stage(1, xdone=True), "proj1")
        run(project_stage(0), "proj0")
        P.tag = "mem"
        mem_kv()
        P.tag = "wout"
        load_w_out()
        ATT, PRJ, OUTC = 30.0, 62.0, 18.0
        OVL = False
        if not OVL:
            for st_i in range(1, 5):
                if st_i > 1:
                    run((c for _, c in tagged("proj%d" % st_i, project_stage(st_i))), "proj%d" % st_i)
                run((c for _, c in tagged("attn%d" % st_i, attention_stage(st_i))), "attn%d" % st_i)
                if st_i < 4:
                    run((c for _, c in tagged("out%d" % st_i, out_stage(st_i))), "out%d" % st_i)
        for st_i in (range(1, 5) if OVL else []):
            fill = []
            tot = 0.0
            if st_i >= 2:
                fill.append(tagged("out%d" % (st_i - 1), out_stage(st_i - 1)))
                tot += OUTC
            if st_i <= 3:
                fill.append(tagged("proj%d" % (st_i + 1), project_stage(st_i + 1)))
                tot += PRJ

            def fill_gen(parts=fill):
                for g in parts:
                    for x in g:
                        yield x
            interleave([(ATT, tagged("attn%d" % st_i, attention_stage(st_i))), (max(tot, 1.0), fill_gen())])
        run(out_stage(4), "out4")
        P.tag = "sample"
        mset("pool", vaug_a[:, 0:5], 1.0, [("va", s) for s in range(5)])
        mset("pool", vaug_b[:, 0:2], 1.0, [("vb", s) for s in range(2)])
        sample_phase()
        P.add("sp", lambda e: e.nop(), writes=list(set(out_tokens)), est=0.05)
        P.schedule(reorder=True)
        P.emit(sems, dsems, block)
    return nc


_CACHE = {}


def _host_tables(rel_bias_a, sink_b):
    rel = np.asarray(rel_bias_a[0], np.float32)
    p = np.arange(128)[:, None]
    t = np.arange(256)[None, :]
    idx = np.clip(t - p, -128, 128) + 128
    biasA = np.ascontiguousarray(rel[:, idx].transpose(1, 0, 2)).reshape(128, 6 * 256)
    cA = np.ascontiguousarray(np.broadcast_to(rel[:, 256][None, :], (128, 6)))
    i = np.arange(16)[None, :]
    idx0 = np.clip(128 + i - p, -128, 128) + 128
    idx1 = np.clip(i - np.minimum(p, 15), -128, 128) + 128
    bs = np.zeros((128, 2, 2, 3, 16), np.float32)
    for e in range(2):
        for c in range(3):
            h = 2 * c + e
            bs[:, 0, e, c, :] = rel[h][idx0]
            bs[:, 1, e, c, :] = rel[h][idx1]
    biasAs = bs.reshape(128, 192)
    distB = np.abs(t - p).astype(np.float32)
    dbs = np.zeros((128, 2, 16), np.float32)
    dbs[:, 0, :] = np.abs(128 + i - p)
    dbs[:, 1, :] = np.abs(i - np.minimum(p, 15))
    distBs = dbs.reshape(128, 32)
    sink_r = np.ascontiguousarray(np.broadcast_to(np.asarray(sink_b[0], np.float32)[None, :], (128, 6)))
    return biasA, cA, biasAs, distB, distBs, sink_r


def kernel(x_prompt, x_sample, cache_a_k, cache_a_v, cache_b_k, cache_b_v, cache_mem_k, cache_mem_v,
           mem_prompt, g_pre, w_in, rel_bias_a, sink_b, g_mem, w_mem_kv, w_out, g_post):
    f = lambda a: np.ascontiguousarray(np.asarray(a, dtype=np.float32))
    x_prompt = f(x_prompt); x_sample = f(x_sample)
    if "nc" not in _CACHE:
        _CACHE["nc"] = build_program()
    nc = _CACHE["nc"]
    biasA, cA, biasAs, distB, distBs, sink_r = _host_tables(f(rel_bias_a), f(sink_b))
    shared = {
        "w_in": f(w_in)[0], "w_mem": f(w_mem_kv)[0], "w_out": f(w_out)[0],
        "gpre_t": np.ascontiguousarray(f(g_pre)[0].reshape(8, 128).T),
        "gmem_t": np.ascontiguousarray(f(g_mem)[0].reshape(8, 128).T),
        "gpost_b": np.ascontiguousarray(np.broadcast_to(f(g_post)[0][None, :], (128, 1024))),
        "biasA": biasA, "cA": cA, "biasAs": biasAs, "distB": distB, "distBs": distBs, "sink_r": sink_r,
    }
    in_maps = []
    for c in range(NCORES):
        b, q = divmod(c, 4)
        t0 = q * TOK
        xhalo = np.zeros((HALO + TOK, 1024), np.float32)
        if q > 0:
            xhalo[:] = x_prompt[b, t0 - HALO:t0 + TOK]
        else:
            xhalo[HALO:] = x_prompt[b, 0:TOK]
        m = dict(shared)
        m.update({
            "xh": xhalo,
            "hm": np.full((128, 1), 1.0 if q > 0 else 0.0, np.float32),
            "xs": x_sample[c],
            "cak": f(cache_a_k)[0, c].reshape(512, 384), "cav": f(cache_a_v)[0, c].reshape(512, 384),
            "cbk": f(cache_b_k)[0, c].reshape(128, 128), "cbv": f(cache_b_v)[0, c].reshape(128, 128),
            "cmk": f(cache_mem_k)[0, c].reshape(256, 256), "cmv": f(cache_mem_v)[0, c].reshape(256, 256),
            "mem": f(mem_prompt)[b],
        })
        in_maps.append(m)
    res = run_bass_kernel_spmd(nc, in_maps, core_ids=list(range(NCORES)))
    R = res.results
    yp = np.stack([np.concatenate([R[b * 4 + q]["y"] for q in range(4)], axis=0) for b in range(2)])
    ys = np.stack([R[c]["ys"] for c in range(8)])
    sak = np.stack([R[b * 4 + 3]["sak"].reshape(512, 6, 64) for b in range(2)])[None]
    sav = np.stack([R[b * 4 + 3]["sav"].reshape(512, 6, 64) for b in range(2)])[None]
    sbk = np.stack([R[b * 4 + 3]["sbk"].reshape(128, 2, 64) for b in range(2)])[None]
    sbv = np.stack([R[b * 4 + 3]["sbv"].reshape(128, 2, 64) for b in range(2)])[None]
    smk = np.stack([R[b * 4]["smk"].reshape(256, 4, 64) for b in range(2)])[None]
    smv = np.stack([R[b * 4]["smv"].reshape(256, 4, 64) for b in range(2)])[None]
    aks = np.stack([R[c]["aks"].reshape(16, 6, 64) for c in range(8)])[None]
    avs = np.stack([R[c]["avs"].reshape(16, 6, 64) for c in range(8)])[None]
    bks = np.stack([R[c]["bks"].reshape(16, 2, 64) for c in range(8)])[None]
    bvs = np.stack([R[c]["bvs"].reshape(16, 2, 64) for c in range(8)])[None]
    return tuple(np.ascontiguousarray(a.astype(np.float32)) for a in
                 (yp, ys, sak, sav, sbk, sbv, smk, smv, aks, avs, bks, bvs))
```

```python
import os
import numpy as np
import concourse.bass as bass
import concourse.mybir as mybir
from concourse.bass_utils import run_bass_kernel_spmd
from contextlib import ExitStack

F32 = mybir.dt.float32
BF16 = mybir.dt.bfloat16
ALU = mybir.AluOpType
AF = mybir.ActivationFunctionType

ENGS = ("pe", "act", "dve", "pool", "sp")
NCORES = 8
TOK = 2048
HALO = 512
SCALE = 0.125
EPS = 1e-6
SLOPES = [2.0 ** (-8.0 * (h + 1) / 6.0) for h in range(6)]
SCHED_WINDOW = 0.5
XLAT = 0.3
PECAL = False
SCHED_MODE = 'nodelay'
ND_MARGIN = 1.0
_CFG = '0'
PS_CFG = {
    '0': dict(S=[0, 1], ACC=[2, 3], PROJ=[0, 1, 2, 3], OUT=[2, 3]),
    'I': dict(S=[0, 1], ACC=[2], PROJ=[6, 7], OUT=[3]),
    'J': dict(S=[0], ACC=[1, 2], PROJ=[6, 7], OUT=[3]),
    'Z': dict(S=[0], ACC=[2, 3], PROJ=[2, 3], OUT=[1]),
    'W': dict(S=[0, 1], ACC=[2], PROJ=[6, 7], OUT=[3]),
    'V': dict(S=[0], ACC=[1, 2], PROJ=[6, 7], OUT=[3]),
}[_CFG]


class Ins:
    __slots__ = ("eng", "fn", "dma_key", "deps", "needs_inc", "milestone", "dma_target", "est", "cls", "pos",
                 "tag", "t0", "t1", "boost")

    def __init__(self, eng, fn, dma_key, est, cls):
        self.eng = eng
        self.fn = fn
        self.dma_key = dma_key
        self.deps = {}
        self.needs_inc = False
        self.milestone = None
        self.dma_target = None
        self.est = est
        self.cls = cls
        self.pos = None


class Prog:
    def __init__(self):
        self.order = []
        self.tok_w = {}
        self.tok_r = {}
        self.lists = None

    def add(self, eng, fn, reads=(), writes=(), dma_key=None, est=0.1, cls=None):
        ins = Ins(eng, fn, dma_key, est, cls)
        ins.tag = getattr(self, "tag", None)
        ins.boost = getattr(self, "boost", 0.0)
        for t in reads:
            w = self.tok_w.get(t)
            if w is not None:
                ins.deps[w] = "raw"
            if isinstance(t, tuple) and t[0] == "ps":
                for r in self.tok_r.get(t, ()):
                    if r.eng != eng and r not in ins.deps:
                        ins.deps[r] = "rar"
        for t in writes:
            w = self.tok_w.get(t)
            if w is not None and w not in ins.deps:
                ins.deps[w] = "waw"
            for r in self.tok_r.get(t, ()):
                if r is not ins and r not in ins.deps:
                    ins.deps[r] = "war"
        for t in writes:
            self.tok_w[t] = ins
            self.tok_r[t] = []
        for t in reads:
            if t in writes:
                continue
            self.tok_r.setdefault(t, []).append(ins)
        self.order.append(ins)
        return ins

    @staticmethod
    def _need_wait(cons, prod, kind):
        if prod.dma_key is not None:
            return True
        if prod.eng != cons.eng:
            return True
        if cons.dma_key is not None:
            return True
        if prod.eng == "pe":
            return False
        return kind in ("raw", "waw", "war")

    def schedule(self, reorder=True):
        order = self.order
        if not reorder:
            self.lists = {e: [i for i in order if i.eng == e] for e in ENGS}
            return
        succs = {i: [] for i in order}
        for i in order:
            for d in i.deps:
                succs[d].append(i)
        prio = {}
        for i in reversed(order):
            m = 0.0
            for sc in succs[i]:
                if prio[sc] > m:
                    m = prio[sc]
            prio[i] = i.est + m
        for i in order:
            prio[i] += i.boost
        noise = getattr(self, "noise", 0.0)
        if noise > 0:
            import random
            rnd = random.Random(getattr(self, "seed", 0))
            for i in order:
                prio[i] *= 1.0 + noise * (2 * rnd.random() - 1)
        window = getattr(self, "window", SCHED_WINDOW)
        ndep = {i: len(i.deps) for i in order}
        ready = {e: [] for e in ENGS}
        for i in order:
            if ndep[i] == 0:
                ready[i.eng].append(i)
        free = {e: 0.0 for e in ENGS}
        done = {}
        lists = {e: [] for e in ENGS}
        act_cls = None
        dma_free = 0.0
        n = len(order)
        k = 0
        while k < n:
            cands = []
            for e in ENGS:
                fe = free[e]
                for i in ready[e]:
                    st = fe
                    for d in i.deps:
                        t = done[d] + (XLAT if (d.eng != e or d.dma_key is not None) else 0.0)
                        if t > st:
                            st = t
                    if e == "act" and i.cls is not None and i.cls != act_cls:
                        st += 1.3
                    cands.append((st, -prio[i], len(cands), i))
            if SCHED_MODE == "nodelay":
                chosen = []
                per = {}
                for c in cands:
                    per.setdefault(c[3].eng, []).append(c)
                for e, cl in per.items():
                    cl.sort(key=lambda c: (c[0], c[1]))
                    x = cl[0]
                    changed = True
                    while changed:
                        changed = False
                        dur = 0.07 if x[3].dma_key is not None else x[3].est
                        for y in cl:
                            if y is x:
                                continue
                            if y[1] < x[1] * ND_MARGIN and y[0] < x[0] + dur - 1e-9 and y[0] >= x[0]:
                                x = y
                                changed = True
                                break
                    chosen.append(x)
                mn = min(c[0] for c in chosen)
                win = [c for c in chosen if c[0] <= mn + 1e-9]
                st, _, _, i = min(win, key=lambda c: (c[1], c[0]))
            else:
                mn = min(c[0] for c in cands)
                win = [c for c in cands if c[0] <= mn + window]
                _, _, _, i = min(win, key=lambda c: (c[1], c[0]))
                st = [c[0] for c in win if c[3] is i][0]
            e = i.eng
            ready[e].remove(i)
            lists[e].append(i)
            if i.dma_key is not None:
                free[e] = st + 0.07
                x0 = max(st + 1.0, dma_free)
                dma_free = x0 + i.est
                done[i] = dma_free + 1.0
            else:
                if e == "act" and i.cls is not None:
                    act_cls = i.cls
                done[i] = st + i.est
                free[e] = done[i]
            i.t0 = st
            i.t1 = done[i]
            for sc in succs[i]:
                ndep[sc] -= 1
                if ndep[sc] == 0:
                    ready[sc.eng].append(sc)
            k += 1
        self.lists = lists
        self.makespan = max(done.values())

    def emit(self, sems, dma_sems, block):
        lists = self.lists
        for e in ENGS:
            cnt = {}
            for p_, ins in enumerate(lists[e]):
                ins.pos = p_
                if ins.dma_key is not None:
                    c = cnt.get(ins.dma_key, 0) + 1
                    cnt[ins.dma_key] = c
                    ins.dma_target = 16 * c
        waits = {}
        for e in ENGS:
            for ins in lists[e]:
                latest = {}
                dmaw = {}
                for p, kind in ins.deps.items():
                    if not self._need_wait(ins, p, kind):
                        continue
                    if p.dma_key is not None:
                        if p.dma_target > dmaw.get(p.dma_key, 0):
                            dmaw[p.dma_key] = p.dma_target
                    else:
                        q = latest.get(p.eng)
                        if q is None or p.pos > q.pos:
                            latest[p.eng] = p
                for p in latest.values():
                    p.needs_inc = True
                waits[ins] = (latest, dmaw)
        for e in ENGS:
            c = 0
            for ins in lists[e]:
                if ins.dma_key is None and ins.needs_inc:
                    c += 1
                    ins.milestone = c
        if False:
            for e in ENGS:
                ms = [i.milestone for i in lists[e] if i.milestone]
                print("ENG", e, "n_ins", len(lists[e]), "max_milestone", max(ms) if ms else 0)
            print("est makespan us", getattr(self, "makespan", None))

        def run(engname, eng):
            waited = {}
            for ins in lists[engname]:
                latest, dmaw = waits[ins]
                for pe_, p in latest.items():
                    s_ = ("e", pe_)
                    if waited.get(s_, 0) >= p.milestone:
                        continue
                    waited[s_] = p.milestone
                    eng.wait_ge(sems[pe_], p.milestone)
                for key, v in dmaw.items():
                    s_ = ("d", key)
                    if waited.get(s_, 0) >= v:
                        continue
                    waited[s_] = v
                    eng.wait_ge(dma_sems[key], v)
                h = ins.fn(eng)
                if ins.dma_key is not None:
                    h.then_inc(dma_sems[ins.dma_key], 16)
                elif ins.needs_inc:
                    h.then_inc(sems[engname], 1)

        @block.tensor
        def _(eng):
            run("pe", eng)

        @block.scalar
        def _(eng):
            run("act", eng)

        @block.vector
        def _(eng):
            run("dve", eng)

        @block.gpsimd
        def _(eng):
            run("pool", eng)

        @block.sync
        def _(eng):
            run("sp", eng)


def build_program():
    nc = bass.Bass("TRN2", target_bir_lowering=False)

    def din(name, shape):
        return nc.dram_tensor(name, list(shape), F32, kind="ExternalInput")

    def dout(name, shape):
        return nc.dram_tensor(name, list(shape), F32, kind="ExternalOutput")

    xh = din("xh", [HALO + TOK, 1024])
    hm_d = din("hm", [128, 1])
    xs_d = din("xs", [16, 1024])
    cak = din("cak", [512, 384]); cav = din("cav", [512, 384])
    cbk = din("cbk", [128, 128]); cbv = din("cbv", [128, 128])
    cmk = din("cmk", [256, 256]); cmv = din("cmv", [256, 256])
    mem_d = din("mem", [256, 1024])
    w_in = din("w_in", [1024, 3072])
    w_mem = din("w_mem", [1024, 512])
    w_out = din("w_out", [1024, 1024])
    gpre_d = din("gpre_t", [128, 8])
    gmem_d = din("gmem_t", [128, 8])
    gpost_d = din("gpost_b", [128, 1024])
    biasA_d = din("biasA", [128, 6 * 256])
    cA_d = din("cA", [128, 6])
    biasAs_d = din("biasAs", [128, 2 * 96])
    distB_d = din("distB", [128, 256])
    distBs_d = din("distBs", [128, 2 * 16])
    sink_d = din("sink_r", [128, 6])

    y_d = dout("y", [TOK, 1024])
    ys_d = dout("ys", [16, 1024])
    sak_d = dout("sak", [512, 384]); sav_d = dout("sav", [512, 384])
    sbk_d = dout("sbk", [128, 128]); sbv_d = dout("sbv", [128, 128])
    smk_d = dout("smk", [256, 256]); smv_d = dout("smv", [256, 256])
    aks_d = dout("aks", [16, 384]); avs_d = dout("avs", [16, 384])
    bks_d = dout("bks", [16, 128]); bvs_d = dout("bvs", [16, 128])

    es = ExitStack()
    with es:
        def sb(name, shape, dt):
            return es.enter_context(nc.sbuf_tensor("sb_" + name, list(shape), dt))

        w_in_bf = sb("w_in_bf", [128, 8, 3072], BF16)
        w_out_bf = sb("w_out_bf", [128, 8, 1024], BF16)
        xin = [sb("xin%d" % i, [128, 1024], F32) for i in range(3)]
        yst = [sb("yst%d" % i, [128, 1024], F32) for i in range(2)]
        xn_bf = [sb("xn_bf%d" % i, [128, 1024], BF16) for i in range(2)]
        junk = sb("junk", [128, 1024], BF16)
        NSLOT = 16
        xnT2 = [sb("xnT%d" % i, [128, 8, 512], BF16) for i in range(2)]
        kT_a = sb("kT_a", [128, 3, NSLOT * 128], BF16)
        vaug_a = sb("vaug_a", [128, NSLOT, 3, 192], BF16)
        kT_b = sb("kT_b", [128, NSLOT * 128], BF16)
        vaug_b = sb("vaug_b", [128, NSLOT, 192], BF16)
        NQ = 1
        qT_a2 = [sb("qT_a%d" % i, [128, 3, 512], BF16) for i in range(NQ)] * (2 // NQ)
        qT_b2 = [sb("qT_b%d" % i, [128, 3, 512], BF16) for i in range(NQ)] * (2 // NQ)
        qT_m2 = [sb("qT_m%d" % i, [128, 2, 512], BF16) for i in range(NQ)] * (2 // NQ)
        sgog = [sb("sgog%d" % i, [128, 8, 512], BF16) for i in range(2)]
        tbuf = [sb("tbuf%d" % i, [128, 512], BF16) for i in range(2)]
        NPT = 6
        PT = [sb("PT%d" % i, [128, 2, 512], BF16) for i in range(NPT)]
        nrm = [sb("nrm%d" % i, [128, 512], F32) for i in range(1)]
        nrmT = [sb("nrmT%d" % i, [128, 512], F32) for i in range(1)]
        expBB_a = sb("expBB_a", [128, 6, 256], BF16)
        expBB_b = sb("expBB_b", [128, 6, 256], BF16)
        gpost_b = sb("gpost_b", [128, 1024], F32)
        kT_m = sb("kT_m", [128, 2, 256], BF16)
        vaug_m = sb("vaug_m", [128, 2, 2, 192], BF16)
        ident = sb("ident", [128, 128], BF16)
        iota_t = sb("iota_t", [128, 128], F32)
        pid = sb("pid", [128, 1], F32)
        stat = sb("stat", [128, 48], F32)
        stat2 = sb("stat2", [128, 48], F32)
        rstd = sb("rstd", [128, 48], F32)
        mhalf = sb("mhalf", [128, 1], F32)
        onec = sb("onec", [128, 1], F32)
        gpre_t = sb("gpre_t", [128, 8], F32)
        gmem_t = sb("gmem_t", [128, 8], F32)
        hm = sb("hm", [128, 1], F32)
        esink = sb("esink", [128, 6], F32)
        esink2 = sb("esink2", [128, 3], F32)
        sink_t = sb("sink_t", [128, 6], F32)
        cA = sb("cA", [128, 6], F32)
        esrow = sb("esrow", [1, 6, 16], BF16)
        selrow = sb("selrow", [1, 2, 128], BF16)
        xnTs = sb("xnTs", [128, 8, 16], BF16)
        qs_a = sb("qs_a", [128, 3, 16], BF16)
        qs_b = sb("qs_b", [128, 3, 16], BF16)
        qs_m = sb("qs_m", [128, 2, 16], BF16)
        sgs = sb("sgs", [128, 8, 16], BF16)
        tbs = sb("tbs", [128, 8, 16], BF16)
        ogs = sb("ogs", [128, 8, 16], BF16)
        PTs = sb("PTs", [128, 5, 2, 48], BF16)
        PTsb = sb("PTsb", [128, 2, 2, 48], BF16)
        PTsm = sb("PTsm", [128, 2, 2, 32], BF16)
        expBs_a = sb("expBs_a", [128, 2, 96], BF16)
        expBs_b = sb("expBs_b", [128, 2, 96], BF16)
        distBs = sb("distBs", [128, 32], F32)
        nrms = iota_t
        kcast = sb("kcast", [128, 384], BF16)

        PS = [es.enter_context(nc.psum_tensor("PS%d" % i, [128, 1024], F32)) for i in range(4)]

        def bank(i):
            return PS[i // 2][:, (i % 2) * 512:(i % 2) * 512 + 512]

        def bank_bf(i):
            return PS[i // 2][:].bitcast(BF16)[:, (i % 2) * 1024:(i % 2) * 1024 + 1024]

        sems = {e: es.enter_context(nc.semaphore("s_" + e)) for e in ENGS}
        dkeys = ["xin0", "xin1", "xin2", "yst0", "yst1", "c0", "c1", "c2", "c3", "c4", "c5",
                 "c6", "c7", "c8", "c9", "o0", "o1", "o2", "o3", "o4", "o5", "o6", "o7"]
        dsems = {k: es.enter_context(nc.semaphore("d_" + k)) for k in dkeys}
        block = es.enter_context(nc.Block())
        P = Prog()
        out_tokens = []

        def fsz(ap):
            n = 1
            for d in ap.shape[1:]:
                n *= d
            return n

        def dma(out, in_, key, reads=(), writes=(), eng="sp"):
            nbytes = out.shape[0] * fsz(out) * 4
            return P.add(eng, lambda e, o=out, i=in_: e.dma_start(out=o, in_=i),
                         reads=reads, writes=writes, dma_key=key, est=nbytes / 170e3)

        def mm(out, lhsT, rhs, start, stop, reads, writes, skip=False):
            return P.add("pe", lambda e, o=out, l=lhsT, r=rhs, s=start, t=stop, k=skip:
                         e.matmul(o, lhsT=l, rhs=r, start=s, stop=t, skip_group_check=k), reads=reads, writes=writes,
                         est=(max(64, fsz(rhs)) / 2100.0 + 0.06) if PECAL else (max(64, fsz(rhs)) / 1960.0 + 0.02))

        def tr(out, in_, idn, reads, writes):
            return P.add("pe", lambda e, o=out, i=in_, d=idn: e.transpose(o, i, d),
                         reads=reads, writes=writes, est=0.12)

        def act(out, in_, func, reads, writes, bias=None, scale=None, accum=None):
            kw = {}
            if bias is not None:
                kw["bias"] = bias
            if scale is not None:
                kw["scale"] = scale
            if accum is not None:
                kw["accum_out"] = accum
            cls = "tanh" if func == AF.Tanh else ("ln" if func == AF.Ln else None)
            est = (fsz(in_) + (230 if PECAL else 300)) / 1400.0 + (0.1 if accum is not None else 0.0) + \
                (0.09 if not isinstance(scale, (int, float, type(None))) else 0.0)
            return P.add("act", lambda e, o=out, i=in_, f=func, k=kw: e.activation(o, i, f, **k),
                         reads=reads, writes=writes, est=est, cls=cls)

        def vest(eng, n, two_src=False):
            if eng == "pool":
                return 0.12 + n * 0.002
            return (n + 150) / 960.0

        def ts(eng, out, in0, s1, s2, op0, op1, reads, writes):
            est = vest(eng, fsz(in0))
            if op1 is None:
                return P.add(eng, lambda e, o=out, i=in0, a=s1, p0=op0: e.tensor_scalar(o, i, a, None, p0),
                             reads=reads, writes=writes, est=est)
            return P.add(eng, lambda e, o=out, i=in0, a=s1, b=s2, p0=op0, p1=op1:
                         e.tensor_scalar(o, i, a, b, p0, p1), reads=reads, writes=writes, est=est)

        def tt(eng, out, in0, in1, op, reads, writes):
            est = 0.5 if op == ALU.pow else vest(eng, fsz(in0))
            return P.add(eng, lambda e, o=out, a=in0, b=in1, p=op: e.tensor_tensor(o, a, b, p),
                         reads=reads, writes=writes, est=est)

        def stt(eng, out, in0, scalar, in1, op0, op1, reads, writes):
            return P.add(eng, lambda e, o=out, a=in0, s=scalar, b=in1, p0=op0, p1=op1:
                         e.scalar_tensor_tensor(o, a, s, b, p0, p1), reads=reads, writes=writes,
                         est=vest(eng, fsz(in0)))

        def cp(eng, out, in_, reads, writes):
            if eng == "act":
                return act(out, in_, AF.Copy, reads, writes)
            p0 = in_.base_partition()
            sc = onec[p0:p0 + in_.shape[0], 0:1]
            return P.add(eng, lambda e, o=out, i=in_, c=sc: e.tensor_scalar(o, i, c, None, ALU.mult),
                         reads=list(reads) + ["onec"], writes=writes, est=vest(eng, fsz(in_)) + 0.06)

        def mset(eng, ap, val, writes):
            return P.add(eng, lambda e, a=ap, v=val: e.memset(a, v), writes=writes, est=0.15 + fsz(ap) * 0.0008)

        rr = {"xin": 0, "bank": 0, "yst": 0, "xn": 0, "stat": 0, "tb": 0, "pt": 0, "nrm": 0,
              "sc": 0, "acc": 0, "nrmT": 0, "op": 0, "evac": 0, "cast": 0, "ck": 0, "ok": 0}

        def nxt(name, n):
            v = rr[name]
            rr[name] = (v + 1) % n
            return v

        def next_bank():
            return PS_CFG["PROJ"][nxt("bank", len(PS_CFG["PROJ"]))]

        def next_stat():
            v = rr["stat"]
            rr["stat"] += 1
            assert v < 48
            return v

        P.add("pool", lambda e: e.iota(iota_t[:], [[1, 128]], base=0, channel_multiplier=0,
                                       allow_small_or_imprecise_dtypes=True), writes=["iota"], est=0.3)
        P.add("pool", lambda e: e.iota(pid[:], [[0, 1]], base=0, channel_multiplier=1,
                                       allow_small_or_imprecise_dtypes=True), writes=["pid"], est=0.3)
        ts("dve", ident[:], iota_t[:], pid[:, 0:1], None, ALU.is_equal, None, ["iota", "pid"], ["ident"])
        mset("pool", mhalf[:], -0.5, ["mhalf"])
        mset("pool", onec[:], 1.0, ["onec"])
        mset("pool", selrow[0:1, 0, 0:64], 0.0, ["selrow"])
        mset("pool", selrow[0:1, 0, 64:128], 1.0, ["selrow"])
        mset("pool", selrow[0:1, 1, 0:64], 1.0, ["selrow"])
        mset("pool", selrow[0:1, 1, 64:128], 0.0, ["selrow"])
        mset("pool", vaug_a[:], 1.0, [("va", s) for s in range(NSLOT)])
        mset("pool", vaug_b[:], 1.0, [("vb", s) for s in range(NSLOT)])
        mset("pool", vaug_m[:], 1.0, [("vm", 0), ("vm", 1)])

        dma(gpre_t[:], gpre_d.ap(), "c0", writes=["gpre"])
        dma(gmem_t[:], gmem_d.ap(), "c1", writes=["gmem"])
        dma(hm[:], hm_d.ap(), "c2", writes=["hm"])
        dma(sink_t[:], sink_d.ap(), "c3", writes=["sink_t"])
        dma(cA[:], cA_d.ap(), "c4", writes=["cA"])
        dma(gpost_b[:], gpost_d.ap(), "c5", writes=["gpost"])
        dma(distBs[:], distBs_d.ap(), "c6", writes=["distBs"])
        for s in range(12, 16):
            ts("dve", vaug_a[:, s, :, 64:128], vaug_a[:, s, :, 64:128], hm[:, 0:1], None, ALU.mult, None,
               ["hm", ("va", s)], [("va", s)])
            ts("dve", vaug_b[:, s, 64:128], vaug_b[:, s, 64:128], hm[:, 0:1], None, ALU.mult, None,
               ["hm", ("vb", s)], [("vb", s)])
        act(esink[:], sink_t[:], AF.Exp, ["sink_t"], ["esink"])
        cp("dve", esink2[0:64, 0:3], esink[0:64, 0:3], ["esink"], ["esink2"])
        cp("dve", esink2[64:128, 0:3], esink[64:128, 3:6], ["esink"], ["esink2"])
        for h in range(6):
            cp("dve", esrow[0:1, h, :], esink[0:1, h:h + 1].to_broadcast([1, 16]), ["esink"], ["esrow"])
        ts("dve", cA[:], cA[:], -1.0, None, ALU.mult, None, ["cA"], ["ncA"])

        def xin_slot():
            return nxt("xin", 3)

        for half in range(2):
            s = xin_slot()
            dma(xin[s][:, 0:768], biasA_d.ap()[:, half * 768:(half + 1) * 768], "xin%d" % s,
                writes=[("xin", s)])
            for hh in range(3):
                h = half * 3 + hh
                act(expBB_a[:, h, :], xin[s][:, hh * 256:(hh + 1) * 256], AF.Exp, [("xin", s), "ncA"],
                    ["ebba"], bias=cA[:, h:h + 1])
        s = xin_slot()
        dma(xin[s][:, 0:256], distB_d.ap(), "xin%d" % s, writes=[("xin", s)])
        dma(xin[s][:, 256:448], biasAs_d.ap(), "xin%d" % s, writes=[("xin", s)])
        for h in range(6):
            act(expBB_b[:, h, :], xin[s][:, 0:256], AF.Exp, [("xin", s)], ["ebbb"], scale=-SLOPES[h])
        for blk in range(2):
            for e_ in range(2):
                for c_ in range(3):
                    ha = 2 * c_ + e_
                    hb = c_ + 3 * e_
                    col = blk * 96 + e_ * 48 + c_ * 16
                    act(expBs_a[:, blk, e_ * 48 + c_ * 16:e_ * 48 + c_ * 16 + 16],
                        xin[s][:, 256 + col:256 + col + 16], AF.Exp, [("xin", s), "ncA"], ["ebsa"],
                        bias=cA[:, ha:ha + 1])
                    act(expBs_b[:, blk, e_ * 48 + c_ * 16:e_ * 48 + c_ * 16 + 16],
                        distBs[:, blk * 16:blk * 16 + 16], AF.Exp, ["distBs"], ["ebsb"], scale=-SLOPES[hb])
        mset("pool", expBB_a[64:128, :, 0:64], 0.0, ["ebba"])
        mset("pool", expBB_b[64:128, :, 0:64], 0.0, ["ebbb"])
        mset("pool", expBB_b[0:64, :, 192:256], 0.0, ["ebbb"])

        wpool = [(xin[0], "xin0", ("xin", 0)), (yst[0], "yst0", ("yst", 0)), (xin[1], "xin1", ("xin", 1)),
                 (yst[1], "yst1", ("yst", 1)), (xin[2], "xin2", ("xin", 2))]
        rr["wp"] = 0

        def wslot():
            return wpool[nxt("wp", 5)]

        def load_w_in(cs):
            for k in range(8):
                wt, wkey, wtok = wslot()
                dma(wt[:], w_in.ap()[k * 128:(k + 1) * 128, cs * 1024:(cs + 1) * 1024], wkey, writes=[wtok])
                pieces = {
                    0: [(0, 1024, 0, False)],
                    1: [(0, 128, 1024, False), (128, 256, 2048, False), (256, 512, 1280, False),
                        (512, 896, 1536, True), (896, 1024, 1920, False)],
                    2: [(0, 128, 1152, False), (128, 512, 2176, True), (512, 1024, 2560, False)],
                }[cs]
                use_dve = nxt("cast", 2) == 0
                for (a0, a1, d0, pm) in pieces:
                    o_ap = w_in_bf[:, k, d0:d0 + (a1 - a0)]
                    i_ap = wt[:, a0:a1]
                    if pm:
                        o_ap = o_ap.rearrange("p (h a d) -> p h a d", h=3, a=2, d=64)
                        i_ap = i_ap.rearrange("p (a h d) -> p h a d", a=2, h=3, d=64)
                    if use_dve:
                        ts("dve", o_ap, i_ap, gpre_t[:, k:k + 1], None, ALU.mult, None, [wtok, "gpre"], [("win", k, cs)])
                    else:
                        act(o_ap, i_ap, AF.Copy, [wtok, "gpre"], [("win", k, cs)], scale=gpre_t[:, k:k + 1])

        def load_w_mem():
            for k in range(8):
                wt, wkey, wtok = wslot()
                dma(wt[:, 0:512], w_mem.ap()[k * 128:(k + 1) * 128, :], wkey, writes=[wtok])
                ts("dve", sgog[0][:, k, :], wt[:, 0:512], gmem_t[:, k:k + 1], None, ALU.mult, None,
                   [wtok, "gmem"], [("sg", 0, k)])

        def load_w_out():
            for c in range(8):
                s = xin_slot()
                if 3 <= c < 6:
                    p_ = c - 3
                    r0 = 384 + 64 * p_
                    r1 = 384 + 64 * (p_ + 3)
                    dma(xin[s][0:64, :], w_out.ap()[r0:r0 + 64, :], "xin%d" % s, writes=[("xin", s)])
                    dma(xin[s][64:128, :], w_out.ap()[r1:r1 + 64, :], "xin%d" % s, writes=[("xin", s)])
                else:
                    dma(xin[s][:], w_out.ap()[c * 128:(c + 1) * 128, :], "xin%d" % s, writes=[("xin", s)])
                cp("act" if c % 2 else "dve", w_out_bf[:, c, :], xin[s][:], [("xin", s)], [("wout", c)])

        def norm_transpose(src_ap, n, dst, dst_tok, gain_unused=None):
            import os
            NT = 99
            s = xin_slot()
            dma(xin[s][0:n, :], src_ap, "xin%d" % s, writes=[("xin", s)])
            b = nxt("xn", 2)
            st = next_stat()
            if NT < 1: return
            act(xn_bf[b][0:n, :], xin[s][0:n, :], AF.Square, [("xin", s)], [("xn", b), ("stat", st)],
                scale=1.0 / 32.0, accum=stat[0:n, st:st + 1])
            if NT < 2: return
            ts("dve", stat2[0:n, st:st + 1], stat[0:n, st:st + 1], EPS, None, ALU.add, None,
               [("stat", st)], [("stat2", st)])
            tt("pool", rstd[0:n, st:st + 1], stat2[0:n, st:st + 1], mhalf[0:n, :], ALU.pow,
               [("stat2", st), "mhalf"], [("rstd", st)])
            if NT < 3: return
            ts("dve", xn_bf[b][0:n, :], xin[s][0:n, :], rstd[0:n, st:st + 1], None, ALU.mult, None,
               [("xin", s), ("rstd", st)], [("xn", b)])
            if NT < 4: return
            bk = next_bank()
            pv = bank_bf(bk)
            for k in range(8):
                tr(pv[:, k * n:(k + 1) * n], xn_bf[b][0:n, k * 128:(k + 1) * 128], ident[0:n, 0:n],
                   [("xn", b), "ident"], [("ps", bk)])
            if NT < 5: return
            cp("dve", dst, pv[:, 0:8 * n].rearrange("p (k n) -> p k n", k=8), [("ps", bk)], [dst_tok])

        C_QA, C_KA, C_VA, C_GA, C_QB, C_KB, C_VB, C_GB, C_QM, C_GM = 0, 384, 768, 1152, 1536, 1920, 2048, 2176, 2560, 2816

        def win_reads(c0, c1):
            css = sorted(set([c0 // 1024, (c1 - 1) // 1024]))
            return [("win", k, cs) for k in range(8) for cs in css]

        def wcols(k, c0, n=128):
            return w_in_bf[:, k, c0:c0 + n]

        def ga_col(c):
            return 2048 if c == 0 else C_GA + 128 * c

        def wcols_pair(k, base, p_):
            return w_in_bf[:, k, base + 128 * p_:base + 128 * p_ + 128]

        def evac_engine():
            return "dve"

        rr["mul"] = 0

        def mul_engine():
            return "pool" if nxt("mul", 3) == 0 else "dve"

        def proj_fm(lhs_fn, wreads, rhs_fn, xtoks, ncols, evac):
            bk = next_bank()
            for k in range(8):
                mm(bank(bk)[:, 0:ncols], lhs_fn(k), rhs_fn(k), k == 0, k == 7, wreads + xtoks, [("ps", bk)])
            evac(bank(bk)[:, 0:ncols], ("ps", bk))

        def gate_evac(dst, dst_tok, n):
            def f(ps, pstok):
                import os
                KG = 99
                if KG < 1: return
                b = nxt("tb", 2)
                act(tbuf[b][:, 0:n], ps, AF.Tanh, [pstok], [("tb", b)], scale=0.5)
                if KG < 2: return
                stt("dve", dst, tbuf[b][:, 0:n], 1.0, ps, ALU.add, ALU.mult, [("tb", b), pstok], [dst_tok])
            return f

        def copy_evac(dst, dst_tok):
            def f(ps, pstok):
                cp(evac_engine(), dst, ps, [pstok], [dst_tok])
            return f

        def slot_of(blk):
            return 12 + blk if blk < 4 else (blk - 4) % 12

        xbuf_of = {}
        rr["xb"] = 0

        def project_stage(st_i, xonly=False, xdone=False):
            halo = st_i == 0
            blk0 = 4 * st_i
            sl0 = slot_of(blk0)
            if not xdone:
                xb = nxt("xb", 2)
                xbuf_of[st_i] = xb
                for tb in range(4):
                    r0 = (blk0 + tb) * 128
                    norm_transpose(xh.ap()[r0:r0 + 128, :], 128, xnT2[xb][:, :, tb * 128:(tb + 1) * 128], ("xnT", xb, tb))
                    yield 1.0
            xb = xbuf_of[st_i]
            xnT = xnT2[xb]
            xt = [("xnT", xb, tb) for tb in range(4)]
            par = st_i % 2
            qT_a, qT_b, qT_m, sg = qT_a2[par], qT_b2[par], qT_m2[par], sgog[par]
            if xonly:
                return
            rhs_x = lambda k: xnT[:, k, :]
            allwin_ = [("win", k, cs) for k in range(8) for cs in range(3)]
            CH = 2.3
            if not halo:
                for c in range(3):
                    proj_fm(lambda k, c=c: wcols(k, C_QA + 128 * c), win_reads(C_QA, C_QA + 384), rhs_x, xt, 512,
                            copy_evac(qT_a[:, c, :], ("qa", par % NQ, c)))
                    yield CH
            for c in range(3):
                def kevac(ps, pstok, c=c):
                    cp(evac_engine(), kT_a[:, c, sl0 * 128:sl0 * 128 + 512], ps, [pstok],
                       [("ka", sl0 + i, c) for i in range(4)])
                proj_fm(lambda k, c=c: wcols(k, C_KA + 128 * c), win_reads(C_KA, C_KA + 384), rhs_x, xt, 512, kevac)
                yield CH
            if not halo:
                for p_ in range(3):
                    proj_fm(lambda k, p_=p_: wcols_pair(k, C_QB, p_), win_reads(C_QB, C_QB + 384), rhs_x, xt, 512,
                            copy_evac(qT_b[:, p_, :], ("qb", par % NQ, p_)))
                    yield CH

            def kbevac(ps, pstok):
                cp(evac_engine(), kT_b[:, sl0 * 128:sl0 * 128 + 512], ps, [pstok],
                   [("kb", sl0 + i) for i in range(4)])
            proj_fm(lambda k: wcols(k, C_KB), win_reads(C_KB, C_KB + 128), rhs_x, xt, 512, kbevac)
            yield CH
            if not halo:
                for c in range(2):
                    proj_fm(lambda k, c=c: wcols(k, C_QM + 128 * c), win_reads(C_QM, C_QM + 256), rhs_x, xt, 512,
                            copy_evac(qT_m[:, c, :], ("qm", par % NQ, c)))
                    yield CH
            last = st_i == 4
            for tb in range(4):
                sl = sl0 + tb
                bk = next_bank()
                pstok = ("ps", bk)
                pb = bank(bk)
                for k in range(8):
                    mm(pb[:, 0:512], xnT[:, k, tb * 128:(tb + 1) * 128], w_in_bf[:, k, C_VA:C_VA + 512],
                       k == 0, k == 7, allwin_ + [("xnT", xb, tb)], [pstok])
                pa = pb[:, 0:384].rearrange("p (c e d) -> p c e d", c=3, e=2, d=64)
                va = vaug_a[:, sl]
                if halo:
                    ts("dve", va[:, :, 0:64], pa[:, :, 0, :], hm[:, 0:1], None, ALU.mult, None, [pstok, "hm"], [("va", sl)])
                    ts("dve", va[:, :, 128:192], pa[:, :, 1, :], hm[:, 0:1], None, ALU.mult, None, [pstok, "hm"], [("va", sl)])
                    ts("dve", vaug_b[:, sl, 0:64], pb[:, 384:448], hm[:, 0:1], None, ALU.mult, None, [pstok, "hm"], [("vb", sl)])
                    ts("dve", vaug_b[:, sl, 128:192], pb[:, 448:512], hm[:, 0:1], None, ALU.mult, None, [pstok, "hm"], [("vb", sl)])
                else:
                    cp("dve", va[:, :, 0:64], pa[:, :, 0, :], [pstok], [("va", sl)])
                    cp("dve", va[:, :, 128:192], pa[:, :, 1, :], [pstok], [("va", sl)])
                    cp("dve", vaug_b[:, sl, 0:64], pb[:, 384:448], [pstok], [("vb", sl)])
                    cp("dve", vaug_b[:, sl, 128:192], pb[:, 448:512], [pstok], [("vb", sl)])
                if last:
                    ys_ = nxt("yst", 2)
                    ytok = ("yst", ys_)
                    cp("act", yst[ys_][:, 512:1024], pb[:, 0:512], [pstok], [ytok])
                    bk2 = next_bank()
                    pstok2 = ("ps", bk2)
                    pb2 = bank(bk2)
                    for k in range(8):
                        mm(pb2[:, 0:384], xnT[:, k, tb * 128:(tb + 1) * 128], w_in_bf[:, k, C_KA:C_KA + 384],
                           k == 0, k == 7, win_reads(C_KA, C_KA + 384) + [("xnT", xb, tb)], [pstok2])
                    for k in range(8):
                        mm(pb2[:, 384:512], xnT[:, k, tb * 128:(tb + 1) * 128], w_in_bf[:, k, C_KB:C_KB + 128],
                           k == 0, k == 7, win_reads(C_KB, C_KB + 128) + [("xnT", xb, tb)], [pstok2])
                    cp("dve", yst[ys_][:, 0:512], pb2[:, 0:512], [pstok2], [ytok])
                    key = "yst%d" % ys_
                    dma(sak_d.ap()[tb * 128:(tb + 1) * 128, :], yst[ys_][:, 0:384], key, reads=[ytok])
                    dma(sav_d.ap()[tb * 128:(tb + 1) * 128, :], yst[ys_][:, 512:896], key, reads=[ytok])
                    if tb == 3:
                        dma(sbk_d.ap(), yst[ys_][:, 384:512], key, reads=[ytok])
                        dma(sbv_d.ap(), yst[ys_][:, 896:1024], key, reads=[ytok])
                    out_tokens.append(ytok)
                yield 3.0 if not last else 6.0
            if not halo:
                for c in range(3):
                    proj_fm(lambda k, c=c: wcols(k, ga_col(c)), win_reads(C_GA, C_GA + 384), rhs_x, xt, 512,
                            gate_evac(sg[:, c, :], ("sg", par, c), 512))
                    yield CH
                for p_ in range(3):
                    proj_fm(lambda k, p_=p_: wcols_pair(k, C_GB, p_), win_reads(C_GB, C_GB + 384), rhs_x, xt, 512,
                            gate_evac(sg[:, 3 + p_, :], ("sg", par, 3 + p_), 512))
                    yield CH
                for c in range(2):
                    proj_fm(lambda k, c=c: wcols(k, C_GM + 128 * c), win_reads(C_GM, C_GM + 256), rhs_x, xt, 512,
                            gate_evac(sg[:, 6 + c, :], ("sg", par, 6 + c), 512))
                    yield CH

        def mem_kv():
            xb = nxt("xb", 2)
            memxT = xnT2[xb]
            ogT = sgog[0]
            for mb in range(2):
                norm_transpose(mem_d.ap()[mb * 128:(mb + 1) * 128, :], 128, memxT[:, :, mb * 128:(mb + 1) * 128],
                               ("xnT", xb, mb))
            mt = [("xnT", xb, 0), ("xnT", xb, 1)]
            ogr = [("sg", 0, k) for k in range(8)]
            for c in range(2):
                bk = next_bank()
                for k in range(8):
                    mm(bank(bk)[:, 0:256], ogT[:, k, c * 128:(c + 1) * 128], memxT[:, k, 0:256], k == 0, k == 7,
                       ogr + mt, [("ps", bk)])
                cp("dve", kT_m[:, c, :], bank(bk)[:, 0:256], [("ps", bk)], [("km", c)])
            for mb in range(2):
                bk = next_bank()
                pstok = ("ps", bk)
                for k in range(8):
                    mm(bank(bk)[:, 0:512], memxT[:, k, mb * 128:(mb + 1) * 128], ogT[:, k, :], k == 0, k == 7,
                       ogr + mt, [pstok])
                ys_ = nxt("yst", 2)
                ytok = ("yst", ys_)
                cp("act", yst[ys_][:, 0:512], bank(bk)[:, 0:512], [pstok], [ytok])
                pm = bank(bk)[:, 256:512].rearrange("p (c e d) -> p c e d", c=2, e=2, d=64)
                vm = vaug_m[:, mb]
                cp("act", vm[:, :, 0:64], pm[:, :, 0, :], [pstok], [("vm", mb)])
                cp("act", vm[:, :, 128:192], pm[:, :, 1, :], [pstok], [("vm", mb)])
                key = "yst%d" % ys_
                dma(smk_d.ap()[mb * 128:(mb + 1) * 128, :], yst[ys_][:, 0:256], key, reads=[ytok])
                dma(smv_d.ap()[mb * 128:(mb + 1) * 128, :], yst[ys_][:, 256:512], key, reads=[ytok])
                out_tokens.append(ytok)

        def attention_group(tiles, chunk, n_q, sink_heads=None):
            ab = PS_CFG["ACC"][nxt("acc", len(PS_CFG["ACC"]))]
            accE = bank(2 * ab)
            accO = bank(2 * ab + 1)
            tokE = ("ps", 2 * ab)
            tokO = ("ps", 2 * ab + 1)
            n = len(tiles)
            state = {}

            def qk(i):
                t = tiles[i]
                sc = PS_CFG["S"][nxt("sc", len(PS_CFG["S"]))]
                S = PS[sc]
                stok = [("ps", 2 * sc), ("ps", 2 * sc + 1)]
                nc_ = t["ncols"]
                r = t["rows"]
                mm(S[0:r, 0:nc_], t["kE"], t["qE"], True, True, t["ktoks"] + t["qtoks"], [stok[0]])
                mm(S[0:r, 512:512 + nc_], t["kO"], t["qO"], True, True, t["ktoks"] + t["qtoks"], [stok[1]])
                pi = nxt("pt", NPT)
                pttok = ("pt", pi)
                Sv = S[:].rearrange("p (b n) -> p b n", b=2)
                act(PT[pi][0:r, :, 0:nc_], Sv[0:r, :, 0:nc_], AF.Exp, stok, [pttok], scale=SCALE)
                if t["post"] is not None:
                    t["post"](PT[pi], pttok)
                state[i] = (pi, pttok)

            def pv(i):
                t = tiles[i]
                pi, pttok = state[i]
                nc_ = t["ncols"]
                r = t["rows"]
                c0 = t["c0"]
                first = i == 0
                lastmm = (i == n - 1) and sink_heads is None
                vE_, one_, vO_ = t["vE"][:, 0:64], t["vE"][:, 64:128], t["vO"][:, 64:128]
                pe_, po_ = PT[pi][0:r, 0, 0:nc_], PT[pi][0:r, 1, 0:nc_]
                rd = t["vtoks"] + [pttok]
                mm(accE[0:64, c0:c0 + nc_], vE_, pe_, first, lastmm, rd, [tokE], skip=True)
                mm(accE[64:128, c0:c0 + nc_], vO_, po_, first, lastmm, rd, [tokE], skip=True)
                mm(accO[0:64, c0:c0 + nc_], one_, pe_, first, lastmm, rd, [tokO], skip=True)
                mm(accO[64:128, c0:c0 + nc_], one_, po_, first, lastmm, rd, [tokO], skip=True)

            for i in range(n + 1):
                if i < n:
                    qk(i)
                if i >= 1:
                    pv(i - 1)
                yield 0.7
            if sink_heads is not None:
                hE, hO, srow = sink_heads
                mm(accE[:, 0:n_q], selrow[0:1, 0, :], srow(hE), False, True, ["selrow", "esrow"], [tokE])
                mm(accO[:, 0:n_q], selrow[0:1, 1, :], srow(hO), False, True, ["selrow", "esrow"], [tokO])
            return accE, accO, tokE, tokO

        def normalise(accE, accO, tokE, tokO, sg_ap, sgtok, og_ap, ogtok, n_q, sinks=None):
            ni = nxt("nrm", 1)
            ntok = ("nrm", ni)
            N = nrm[ni]
            if sinks is None:
                act(N[:, 0:n_q], accO[:, 0:n_q], AF.Ln, [tokO], [ntok])
            else:
                act(N[:, 0:n_q], accO[:, 0:n_q], AF.Ln, [tokO, "esink2"], [ntok], bias=esink2[:, sinks[0]:sinks[0] + 1])
            act(N[:, 0:n_q], N[:, 0:n_q], AF.Exp, [ntok], [ntok], scale=-1.0)
            ti = nxt("nrmT", 1)
            ttok = ("nrmT", ti)
            stt("dve", nrmT[ti][:, 0:n_q], accE[:, 0:n_q], 0.5, sg_ap, ALU.mult, ALU.mult, [tokE, sgtok], [ttok])
            tt("dve", og_ap, nrmT[ti][:, 0:n_q], N[:, 0:n_q], ALU.mult, [ttok, ntok], [ogtok])

        def attention_stage(st_i):
            I0 = 4 * st_i
            par = st_i % 2
            qT_a, qT_b, qT_m, sg = qT_a2[par], qT_b2[par], qT_m2[par], sgog[par]
            ogT = sg
            groups = []
            for p_ in range(3):
                tiles = []
                for J in range(I0 - 4, I0 + 4):
                    sl = slot_of(J)
                    c0b = max(J - I0, 0)
                    c1b = min(J + 4 - I0, 3) + 1
                    ncols = (c1b - c0b) * 128
                    rlo = max(0, c0b + I0 - J)
                    rhi = min(1, c1b - 1 + I0 - J)
                    corner = (J + 4 - I0) if J + 4 <= I0 + 3 else None

                    def post(pt, pttok, p_=p_, rlo=rlo, rhi=rhi, c0b=c0b, J=J, corner=corner, I0=I0):
                        if rhi >= rlo:
                            cs = (rlo + J - I0 - c0b) * 128
                            ce = (rhi + 1 + J - I0 - c0b) * 128
                            tt(mul_engine(), pt[:, :, cs:ce], pt[:, :, cs:ce],
                               expBB_a[:, 2 * p_:2 * p_ + 2, rlo * 128:(rhi + 1) * 128], ALU.mult,
                               [pttok, "ebba"], [pttok])
                        if corner is not None:
                            off = (corner - c0b) * 128
                            mset("pool", pt[0:64, :, off + 64:off + 128], 0.0, [pttok])
                    tiles.append(dict(
                        kE=kT_a[0:64, p_, sl * 128:(sl + 1) * 128], kO=kT_a[64:128, p_, sl * 128:(sl + 1) * 128],
                        qE=qT_a[0:64, p_, c0b * 128:c1b * 128], qO=qT_a[64:128, p_, c0b * 128:c1b * 128],
                        ktoks=[("ka", sl, p_)], qtoks=[("qa", par % NQ, p_)], c0=c0b * 128, ncols=ncols, rows=128,
                        vE=vaug_a[:, sl, p_, 0:128], vO=vaug_a[:, sl, p_, 64:192], vtoks=[("va", sl)], post=post))
                def grpA(tiles=tiles, p_=p_):
                    accE, accO, tokE, tokO = yield from attention_group(tiles, p_, 512)
                    normalise(accE, accO, tokE, tokO, sg[:, p_, :], ("sg", par, p_), ogT[:, p_, :], ("sg", par, p_), 512)
                groups.append(grpA)
            for p_ in range(3):
                tiles = []
                for J in range(I0 - 1, I0 + 4):
                    sl = slot_of(J)
                    c0b = max(J - I0, 0)
                    c1b = min(J + 1 - I0, 3) + 1
                    ncols = (c1b - c0b) * 128
                    rlo = max(0, c0b + I0 - J)
                    rhi = min(1, c1b - 1 + I0 - J)

                    def post(pt, pttok, p_=p_, rlo=rlo, rhi=rhi, ncols=ncols):
                        tt(mul_engine(), pt[:, :, 0:ncols], pt[:, :, 0:ncols],
                           expBB_b[:, p_:p_ + 4:3, rlo * 128:(rhi + 1) * 128], ALU.mult, [pttok, "ebbb"], [pttok])
                    tiles.append(dict(
                        kE=kT_b[0:64, sl * 128:(sl + 1) * 128], kO=kT_b[64:128, sl * 128:(sl + 1) * 128],
                        qE=qT_b[0:64, p_, c0b * 128:c1b * 128], qO=qT_b[64:128, p_, c0b * 128:c1b * 128],
                        ktoks=[("kb", sl)], qtoks=[("qb", par % NQ, p_)], c0=c0b * 128, ncols=ncols, rows=128,
                        vE=vaug_b[:, sl, 0:128], vO=vaug_b[:, sl, 64:192], vtoks=[("vb", sl)], post=post))
                def grpB(tiles=tiles, p_=p_):
                    accE, accO, tokE, tokO = yield from attention_group(tiles, 3 + p_, 512, sink_heads=None)
                    normalise(accE, accO, tokE, tokO, sg[:, 3 + p_, :], ("sg", par, 3 + p_), ogT[:, 3 + p_, :],
                              ("sg", par, 3 + p_), 512, sinks=(p_, p_ + 3))
                groups.append(grpB)
            for c in range(2):
                tiles = []
                for mb in range(2):
                    tiles.append(dict(
                        kE=kT_m[0:64, c, mb * 128:(mb + 1) * 128], kO=kT_m[64:128, c, mb * 128:(mb + 1) * 128],
                        qE=qT_m[0:64, c, :], qO=qT_m[64:128, c, :],
                        ktoks=[("km", c)], qtoks=[("qm", par % NQ, c)], c0=0, ncols=512, rows=128,
                        vE=vaug_m[:, mb, c, 0:128], vO=vaug_m[:, mb, c, 64:192], vtoks=[("vm", mb)], post=None))
                def grpM(tiles=tiles, c=c):
                    accE, accO, tokE, tokO = yield from attention_group(tiles, 6 + c, 512)
                    normalise(accE, accO, tokE, tokO, sg[:, 6 + c, :], ("sg", par, 6 + c), ogT[:, 6 + c, :],
                              ("sg", par, 6 + c), 512)
                groups.append(grpM)
            order_ = {"seq": [0, 1, 2, 3, 4, 5, 6, 7], "mix": [0, 3, 1, 4, 2, 5, 6, 7]}["seq"]
            glist = [groups[i] for i in order_]
            if False:
                for i in range(0, 8, 2):
                    ga, gb = glist[i](), glist[i + 1]()
                    alive = [True, True]
                    gens = [ga, gb]
                    k = 0
                    while any(alive):
                        j = k % 2
                        k += 1
                        if not alive[j]:
                            continue
                        try:
                            yield next(gens[j])
                        except StopIteration:
                            alive[j] = False
            else:
                for g in glist:
                    yield from g()

        def out_block(lhs_fn, ogreads, n, x_src, y_dst):
            ob = PS_CFG["OUT"][nxt("op", len(PS_CFG["OUT"]))]
            O = PS[ob]
            otok = [("ps", 2 * ob), ("ps", 2 * ob + 1)]
            for hf in range(2):
                for c in range(8):
                    mm(O[0:n, hf * 512:(hf + 1) * 512], lhs_fn(c), w_out_bf[:, c, hf * 512:(hf + 1) * 512],
                       c == 0, c == 7, ogreads + [("wout", c)], [otok[hf]])
            ys_ = nxt("yst", 2)
            ytok = ("yst", ys_)
            dma(yst[ys_][0:n, :], x_src, "yst%d" % ys_, writes=[ytok])
            st = next_stat()
            act(junk[0:n, :], O[0:n, :], AF.Square, otok, ["junk", ("stat", st)], scale=1.0 / 32.0,
                accum=stat[0:n, st:st + 1])
            ts("dve", stat2[0:n, st:st + 1], stat[0:n, st:st + 1], EPS, None, ALU.add, None, [("stat", st)], [("stat2", st)])
            tt("pool", rstd[0:n, st:st + 1], stat2[0:n, st:st + 1], mhalf[0:n, :], ALU.pow,
               [("stat2", st), "mhalf"], [("rstd", st)])
            tt("dve", O[0:n, :], O[0:n, :], gpost_b[0:n, :], ALU.mult, otok + ["gpost"], otok)
            stt("dve", yst[ys_][0:n, :], O[0:n, :], rstd[0:n, st:st + 1], yst[ys_][0:n, :], ALU.mult, ALU.add,
                otok + [("rstd", st), ytok], [ytok])
            dma(y_dst, yst[ys_][0:n, :], "yst%d" % ys_, reads=[ytok])
            out_tokens.append(ytok)

        def out_stage(st_i):
            par = st_i % 2
            ogT = sgog[par]
            og_all = [("sg", par, c) for c in range(8)]
            for tb in range(4):
                r0 = (4 * st_i + tb) * 128
                o0 = (4 * (st_i - 1) + tb) * 128
                out_block(lambda c, tb=tb: ogT[:, c, tb * 128:(tb + 1) * 128], og_all, 128,
                          xh.ap()[r0:r0 + 128, :], y_d.ap()[o0:o0 + 128, :])
                yield 4.5

        def sample_phase():
            norm_transpose(xs_d.ap(), 16, xnTs[:, :, :], "xnTs")
            xt = ["xnTs"]
            rhs_x = lambda k: xnTs[:, k, :]
            allwin = [("win", k, cs) for k in range(8) for cs in range(3)]
            fm = []
            for c in range(3):
                fm.append((lambda k, c=c: wcols(k, C_QA + 128 * c), "q", qs_a[:, c, :], "qs_a"))
            for c in range(3):
                fm.append((lambda k, c=c: wcols(k, C_KA + 128 * c), "ka", c, None))
            for p_ in range(3):
                fm.append((lambda k, p_=p_: wcols_pair(k, C_QB, p_), "q", qs_b[:, p_, :], "qs_b"))
            fm.append((lambda k: wcols(k, C_KB), "kb", None, None))
            for c in range(2):
                fm.append((lambda k, c=c: wcols(k, C_QM + 128 * c), "q", qs_m[:, c, :], "qs_m"))
            for c in range(3):
                fm.append((lambda k, c=c: wcols(k, ga_col(c)), "g", c, None))
            for p_ in range(3):
                fm.append((lambda k, p_=p_: wcols_pair(k, C_GB, p_), "g", 3 + p_, None))
            for c in range(2):
                fm.append((lambda k, c=c: wcols(k, C_GM + 128 * c), "g", 6 + c, None))
            bk = next_bank()
            pstok = ("ps", bk)
            pb = bank(bk)
            for i, (lf, kind, a1, a2) in enumerate(fm):
                for k in range(8):
                    mm(pb[:, i * 16:(i + 1) * 16], lf(k), rhs_x(k), k == 0, k == 7, allwin + xt, [pstok])
            for i, (lf, kind, a1, a2) in enumerate(fm):
                ps = pb[:, i * 16:(i + 1) * 16]
                if kind == "q":
                    cp("dve", a1, ps, [pstok], [a2])
                elif kind == "ka":
                    cp("dve", kT_a[:, a1, 512:528], ps, [pstok], [("ka", 4, a1)])
                elif kind == "kb":
                    cp("dve", kT_b[:, 128:144], ps, [pstok], [("kb", 1)])
                else:
                    act(tbs[:, a1, :], ps, AF.Tanh, [pstok], ["tbs"], scale=0.5)
                    stt("dve", sgs[:, a1, :], tbs[:, a1, :], 1.0, ps, ALU.add, ALU.mult, ["tbs", pstok], ["sgs"])
            ob = PS_CFG["OUT"][nxt("op", len(PS_CFG["OUT"]))]
            O = PS[ob]
            otok = [("ps", 2 * ob), ("ps", 2 * ob + 1)]
            for k in range(8):
                mm(O[0:16, 0:512], xnTs[:, k, :], w_in_bf[:, k, 384:896], k == 0, k == 7, allwin + xt, [otok[0]])
            for k in range(8):
                mm(O[0:16, 512:768], xnTs[:, k, :], w_in_bf[:, k, 896:1152], k == 0, k == 7, allwin + xt, [otok[1]])
            for k in range(8):
                mm(O[0:16, 768:896], xnTs[:, k, :], w_in_bf[:, k, 1920:2048], k == 0, k == 7, allwin + xt, [otok[1]])
            for k in range(8):
                mm(O[0:16, 896:1024], xnTs[:, k, :], w_in_bf[:, k, 1152:1280], k == 0, k == 7, allwin + xt, [otok[1]])
            ys_ = nxt("yst", 2)
            ytok = ("yst", ys_)
            cp("dve", yst[ys_][0:16, :], O[0:16, :], otok, [ytok])
            key = "yst%d" % ys_
            dma(aks_d.ap(), yst[ys_][0:16, 0:384], key, reads=[ytok])
            dma(avs_d.ap(), yst[ys_][0:16, 384:768], key, reads=[ytok])
            dma(bks_d.ap(), yst[ys_][0:16, 768:896], key, reads=[ytok])
            dma(bvs_d.ap(), yst[ys_][0:16, 896:1024], key, reads=[ytok])
            out_tokens.append(ytok)
            pa = O[0:16, 384:768].rearrange("p (c e d) -> p c e d", c=3, e=2, d=64)
            va = vaug_a[0:16, 4]
            cp("dve", va[:, :, 0:64], pa[:, :, 0, :], otok, [("va", 4)])
            cp("dve", va[:, :, 128:192], pa[:, :, 1, :], otok, [("va", 4)])
            cp("dve", vaug_b[0:16, 1, 0:64], O[0:16, 896:960], otok, [("vb", 1)])
            cp("dve", vaug_b[0:16, 1, 128:192], O[0:16, 960:1024], otok, [("vb", 1)])
            for blk in range(4):
                s = xin_slot()
                xk = "xin%d" % s
                dma(xin[s][:, 0:384], cak.ap()[blk * 128:(blk + 1) * 128, :], xk, writes=[("xin", s)])
                dma(xin[s][:, 384:768], cav.ap()[blk * 128:(blk + 1) * 128, :], xk, writes=[("xin", s)])
                cp("act", kcast[:, 0:384], xin[s][:, 0:384], [("xin", s)], ["kcast"])
                bk = next_bank()
                pv_ = bank_bf(bk)
                for c in range(3):
                    tr(pv_[:, c * 128:(c + 1) * 128], kcast[:, c * 128:(c + 1) * 128], ident[:], ["kcast", "ident"], [("ps", bk)])
                for c in range(3):
                    cp("dve", kT_a[:, c, blk * 128:(blk + 1) * 128], pv_[:, c * 128:(c + 1) * 128], [("ps", bk)], [("ka", blk, c)])
                xv = xin[s][:, 384:768].rearrange("p (c e d) -> p c e d", c=3, e=2, d=64)
                va = vaug_a[:, blk]
                cp("dve", va[:, :, 0:64], xv[:, :, 0, :], [("xin", s)], [("va", blk)])
                cp("dve", va[:, :, 128:192], xv[:, :, 1, :], [("xin", s)], [("va", blk)])
            s = xin_slot()
            xk = "xin%d" % s
            dma(xin[s][:, 0:128], cbk.ap(), xk, writes=[("xin", s)])
            dma(xin[s][:, 128:256], cbv.ap(), xk, writes=[("xin", s)])
            cp("act", kcast[:, 0:128], xin[s][:, 0:128], [("xin", s)], ["kcast"])
            bk = next_bank()
            pv_ = bank_bf(bk)
            tr(pv_[:, 0:128], kcast[:, 0:128], ident[:], ["kcast", "ident"], [("ps", bk)])
            cp("dve", kT_b[:, 0:128], pv_[:, 0:128], [("ps", bk)], [("kb", 0)])
            cp("dve", vaug_b[:, 0, 0:64], xin[s][:, 128:192], [("xin", s)], [("vb", 0)])
            cp("dve", vaug_b[:, 0, 128:192], xin[s][:, 192:256], [("xin", s)], [("vb", 0)])
            for mb in range(2):
                s = xin_slot()
                xk = "xin%d" % s
                dma(xin[s][:, 0:256], cmk.ap()[mb * 128:(mb + 1) * 128, :], xk, writes=[("xin", s)])
                dma(xin[s][:, 256:512], cmv.ap()[mb * 128:(mb + 1) * 128, :], xk, writes=[("xin", s)])
                cp("act", kcast[:, 0:256], xin[s][:, 0:256], [("xin", s)], ["kcast"])
                bk = next_bank()
                pv_ = bank_bf(bk)
                for c in range(2):
                    tr(pv_[:, c * 128:(c + 1) * 128], kcast[:, c * 128:(c + 1) * 128], ident[:], ["kcast", "ident"], [("ps", bk)])
                for c in range(2):
                    cp("dve", kT_m[:, c, mb * 128:(mb + 1) * 128], pv_[:, c * 128:(c + 1) * 128], [("ps", bk)], [("km", c)])
                xv = xin[s][:, 256:512].rearrange("p (c e d) -> p c e d", c=2, e=2, d=64)
                vm = vaug_m[:, mb]
                cp("dve", vm[:, :, 0:64], xv[:, :, 0, :], [("xin", s)], [("vm", mb)])
                cp("dve", vm[:, :, 128:192], xv[:, :, 1, :], [("xin", s)], [("vm", mb)])

            def sample_attn(nblk, npair, rows_of, kE_of, kO_of, q_tile, qtok, ktoks_of, v_of, vtoks_of, ptile, pttokname,
                            bias_of, sink, og_chunk0):
                ncols = npair * 16
                for blk in range(nblk):
                    r = rows_of(blk)
                    sc = PS_CFG["S"][nxt("sc", len(PS_CFG["S"]))]
                    S = PS[sc]
                    stok = [("ps", 2 * sc), ("ps", 2 * sc + 1)]
                    for c in range(npair):
                        mm(S[0:r, c * 16:(c + 1) * 16], kE_of(blk, c), q_tile[0:64, c, :], True, True,
                           ktoks_of(blk, c) + [qtok], [stok[0]])
                        mm(S[0:r, 512 + c * 16:512 + (c + 1) * 16], kO_of(blk, c), q_tile[64:128, c, :], True, True,
                           ktoks_of(blk, c) + [qtok], [stok[1]])
                    Sv = S[:].rearrange("p (b n) -> p b n", b=2)
                    pttok = (pttokname, blk)
                    act(ptile[0:r, blk, :, 0:ncols], Sv[0:r, :, 0:ncols], AF.Exp, stok, [pttok], scale=SCALE)
                    b_ = bias_of(blk)
                    if b_ is not None:
                        pf = ptile[0:r, blk].rearrange("p e n -> p (e n)")
                        tt("pool", pf, pf, b_[0:r], ALU.mult, [pttok, "ebsa", "ebsb"], [pttok])
                ab = PS_CFG["ACC"][nxt("acc", len(PS_CFG["ACC"]))]
                A = PS[ab]
                atok = [("ps", 2 * ab), ("ps", 2 * ab + 1)]
                for c in range(npair):
                    for e_ in range(2):
                        dst = A[:, e_ * 512 + c * 16:e_ * 512 + (c + 1) * 16]
                        for blk in range(nblk):
                            r = rows_of(blk)
                            mm(dst, v_of(blk, c, e_)[0:r], ptile[0:r, blk, e_, c * 16:(c + 1) * 16], blk == 0,
                               (blk == nblk - 1) and sink is None, vtoks_of(blk) + [(pttokname, blk)], [atok[e_]])
                        if sink is not None:
                            hh = sink(c, e_)
                            mm(dst, selrow[0:1, e_, :], esrow[0:1, hh, 0:16], False, True, ["selrow", "esrow"], [atok[e_]])
                Av = A[:].rearrange("p (b n) -> p b n", b=2)
                act(nrms[0:64, 0:ncols], Av[64:128, 0, 0:ncols], AF.Ln, atok, ["iota"])
                act(nrms[64:128, 0:ncols], Av[0:64, 1, 0:ncols], AF.Ln, atok, ["iota"])
                act(nrms[:, 0:ncols], nrms[:, 0:ncols], AF.Exp, ["iota"], ["iota"], scale=-1.0)
                sgv = sgs[:, og_chunk0:og_chunk0 + npair, :].rearrange("p c n -> p (c n)")
                tt("pool", nrms[:, 0:ncols], nrms[:, 0:ncols], sgv, ALU.mult, ["iota", "sgs"], ["iota"])
                ogv = ogs[:, og_chunk0:og_chunk0 + npair, :].rearrange("p c n -> p (c n)")
                stt("dve", ogv[0:64], Av[0:64, 0, 0:ncols], 0.5, nrms[0:64, 0:ncols], ALU.mult, ALU.mult, atok + ["iota"], ["ogs"])
                stt("dve", ogv[64:128], Av[64:128, 1, 0:ncols], 0.5, nrms[64:128, 0:ncols], ALU.mult, ALU.mult, atok + ["iota"], ["ogs"])

            sample_attn(
                5, 3, lambda blk: 16 if blk == 4 else 128,
                lambda blk, c: kT_a[0:64, c, blk * 128:blk * 128 + (16 if blk == 4 else 128)],
                lambda blk, c: kT_a[64:128, c, blk * 128:blk * 128 + (16 if blk == 4 else 128)],
                qs_a, "qs_a", lambda blk, c: [("ka", blk, c)],
                lambda blk, c, e_: vaug_a[:, blk, c, 64 * e_:64 * e_ + 128], lambda blk: [("va", blk)],
                PTs, "pts", lambda blk: expBs_a[:, blk - 3, :] if blk >= 3 else None, None, 0)
            sample_attn(
                2, 3, lambda blk: 16 if blk == 1 else 128,
                lambda blk, c: kT_b[0:64, blk * 128:blk * 128 + (16 if blk == 1 else 128)],
                lambda blk, c: kT_b[64:128, blk * 128:blk * 128 + (16 if blk == 1 else 128)],
                qs_b, "qs_b", lambda blk, c: [("kb", blk)],
                lambda blk, c, e_: vaug_b[:, blk, 64 * e_:64 * e_ + 128], lambda blk: [("vb", blk)],
                PTsb, "ptsb", lambda blk: expBs_b[:, blk, :], lambda c, e_: c + 3 * e_, 3)
            sample_attn(
                2, 2, lambda blk: 128,
                lambda blk, c: kT_m[0:64, c, blk * 128:(blk + 1) * 128],
                lambda blk, c: kT_m[64:128, c, blk * 128:(blk + 1) * 128],
                qs_m, "qs_m", lambda blk, c: [("km", c)],
                lambda blk, c, e_: vaug_m[:, blk, c, 64 * e_:64 * e_ + 128], lambda blk: [("vm", blk)],
                PTsm, "ptsm", lambda blk: None, None, 6)
            out_block(lambda c: ogs[:, c, :], ["ogs"], 16, xs_d.ap(), ys_d.ap())

        def run(gen, tag):
            P.tag = tag
            for _ in gen:
                pass

        def chain(*parts):
            for tag, g in parts:
                for c in g:
                    yield tag, c

        def interleave(streams):
            n = len(streams)
            done = [0.0] * n
            alive = [True] * n
            while any(alive):
                i = min((j for j in range(n) if alive[j]), key=lambda j: done[j] / streams[j][0])
                try:
                    tag, c = next(streams[i][1])
                    done[i] += c
                except StopIteration:
                    alive[i] = False
                    continue
            return

        def tagged(tag, g):
            it = iter(g)
            while True:
                P.tag = tag
                try:
                    c = next(it)
                except StopIteration:
                    return
                yield tag, c

        P.tag = "w"
        P.boost = 0.0
        run(project_stage(1, xonly=True), "w")
        P.boost = 0.0
        P.tag = "w"
        load_w_in(0)
        load_w_in(1)
        load_w_in(2)
        load_w_mem()
        run(project_stage(1, xdone=True), "proj1")
        run(project_stage(0), "proj0")
        P.tag = "mem"
        mem_kv()
        P.tag = "wout"
        load_w_out()
        ATT, PRJ, OUTC = 30.0, 62.0, 18.0
        OVL = False
        if not OVL:
            for st_i in range(1, 5):
                if st_i > 1:
                    run((c for _, c in tagged("proj%d" % st_i, project_stage(st_i))), "proj%d" % st_i)
                run((c for _, c in tagged("attn%d" % st_i, attention_stage(st_i))), "attn%d" % st_i)
                if st_i < 4:
                    run((c for _, c in tagged("out%d" % st_i, out_stage(st_i))), "out%d" % st_i)
        for st_i in (range(1, 5) if OVL else []):
            fill = []
            tot = 0.0
            if st_i >= 2:
                fill.append(tagged("out%d" % (st_i - 1), out_stage(st_i - 1)))
                tot += OUTC
            if st_i <= 3:
                fill.append(tagged("proj%d" % (st_i + 1), project_stage(st_i + 1)))
                tot += PRJ

            def fill_gen(parts=fill):
                for g in parts:
                    for x in g:
                        yield x
            interleave([(ATT, tagged("attn%d" % st_i, attention_stage(st_i))), (max(tot, 1.0), fill_gen())])
        run(out_stage(4), "out4")
        P.tag = "sample"
        mset("pool", vaug_a[:, 0:5], 1.0, [("va", s) for s in range(5)])
        mset("pool", vaug_b[:, 0:2], 1.0, [("vb", s) for s in range(2)])
        sample_phase()
        P.add("sp", lambda e: e.nop(), writes=list(set(out_tokens)), est=0.05)
        P.schedule(reorder=True)
        P.emit(sems, dsems, block)
    return nc


_CACHE = {}


def _host_tables(rel_bias_a, sink_b):
    rel = np.asarray(rel_bias_a[0], np.float32)
    p = np.arange(128)[:, None]
    t = np.arange(256)[None, :]
    idx = np.clip(t - p, -128, 128) + 128
    biasA = np.ascontiguousarray(rel[:, idx].transpose(1, 0, 2)).reshape(128, 6 * 256)
    cA = np.ascontiguousarray(np.broadcast_to(rel[:, 256][None, :], (128, 6)))
    i = np.arange(16)[None, :]
    idx0 = np.clip(128 + i - p, -128, 128) + 128
    idx1 = np.clip(i - np.minimum(p, 15), -128, 128) + 128
    bs = np.zeros((128, 2, 2, 3, 16), np.float32)
    for e in range(2):
        for c in range(3):
            h = 2 * c + e
            bs[:, 0, e, c, :] = rel[h][idx0]
            bs[:, 1, e, c, :] = rel[h][idx1]
    biasAs = bs.reshape(128, 192)
    distB = np.abs(t - p).astype(np.float32)
    dbs = np.zeros((128, 2, 16), np.float32)
    dbs[:, 0, :] = np.abs(128 + i - p)
    dbs[:, 1, :] = np.abs(i - np.minimum(p, 15))
    distBs = dbs.reshape(128, 32)
    sink_r = np.ascontiguousarray(np.broadcast_to(np.asarray(sink_b[0], np.float32)[None, :], (128, 6)))
    return biasA, cA, biasAs, distB, distBs, sink_r


def kernel(x_prompt, x_sample, cache_a_k, cache_a_v, cache_b_k, cache_b_v, cache_mem_k, cache_mem_v,
           mem_prompt, g_pre, w_in, rel_bias_a, sink_b, g_mem, w_mem_kv, w_out, g_post):
    f = lambda a: np.ascontiguousarray(np.asarray(a, dtype=np.float32))
    x_prompt = f(x_prompt); x_sample = f(x_sample)
    if "nc" not in _CACHE:
        _CACHE["nc"] = build_program()
    nc = _CACHE["nc"]
    biasA, cA, biasAs, distB, distBs, sink_r = _host_tables(f(rel_bias_a), f(sink_b))
    shared = {
        "w_in": f(w_in)[0], "w_mem": f(w_mem_kv)[0], "w_out": f(w_out)[0],
        "gpre_t": np.ascontiguousarray(f(g_pre)[0].reshape(8, 128).T),
        "gmem_t": np.ascontiguousarray(f(g_mem)[0].reshape(8, 128).T),
        "gpost_b": np.ascontiguousarray(np.broadcast_to(f(g_post)[0][None, :], (128, 1024))),
        "biasA": biasA, "cA": cA, "biasAs": biasAs, "distB": distB, "distBs": distBs, "sink_r": sink_r,
    }
    in_maps = []
    for c in range(NCORES):
        b, q = divmod(c, 4)
        t0 = q * TOK
        xhalo = np.zeros((HALO + TOK, 1024), np.float32)
        if q > 0:
            xhalo[:] = x_prompt[b, t0 - HALO:t0 + TOK]
        else:
            xhalo[HALO:] = x_prompt[b, 0:TOK]
        m = dict(shared)
        m.update({
            "xh": xhalo,
            "hm": np.full((128, 1), 1.0 if q > 0 else 0.0, np.float32),
            "xs": x_sample[c],
            "cak": f(cache_a_k)[0, c].reshape(512, 384), "cav": f(cache_a_v)[0, c].reshape(512, 384),
            "cbk": f(cache_b_k)[0, c].reshape(128, 128), "cbv": f(cache_b_v)[0, c].reshape(128, 128),
            "cmk": f(cache_mem_k)[0, c].reshape(256, 256), "cmv": f(cache_mem_v)[0, c].reshape(256, 256),
            "mem": f(mem_prompt)[b],
        })
        in_maps.append(m)
    res = run_bass_kernel_spmd(nc, in_maps, core_ids=list(range(NCORES)))
    R = res.results
    yp = np.stack([np.concatenate([R[b * 4 + q]["y"] for q in range(4)], axis=0) for b in range(2)])
    ys = np.stack([R[c]["ys"] for c in range(8)])
    sak = np.stack([R[b * 4 + 3]["sak"].reshape(512, 6, 64) for b in range(2)])[None]
    sav = np.stack([R[b * 4 + 3]["sav"].reshape(512, 6, 64) for b in range(2)])[None]
    sbk = np.stack([R[b * 4 + 3]["sbk"].reshape(128, 2, 64) for b in range(2)])[None]
    sbv = np.stack([R[b * 4 + 3]["sbv"].reshape(128, 2, 64) for b in range(2)])[None]
    smk = np.stack([R[b * 4]["smk"].reshape(256, 4, 64) for b in range(2)])[None]
    smv = np.stack([R[b * 4]["smv"].reshape(256, 4, 64) for b in range(2)])[None]
    aks = np.stack([R[c]["aks"].reshape(16, 6, 64) for c in range(8)])[None]
    avs = np.stack([R[c]["avs"].reshape(16, 6, 64) for c in range(8)])[None]
    bks = np.stack([R[c]["bks"].reshape(16, 2, 64) for c in range(8)])[None]
    bvs = np.stack([R[c]["bvs"].reshape(16, 2, 64) for c in range(8)])[None]
    return tuple(np.ascontiguousarray(a.astype(np.float32)) for a in
                 (yp, ys, sak, sav, sbk, sbv, smk, smv, aks, avs, bks, bvs))
```
